# Optimizing a Trainium2 kernel written in Bass

```python
import jax, jax.numpy as jnp
from jax import lax
import numpy as np

D_MODEL = 1024
BATCH = 8
SEQ = 4096
DEPTH = 1

PLE_DIM = 256
D_FF = 2816
LRU_WIDTH = 1024
LRU_BLOCKS = 8
LRU_BLOCK_W = LRU_WIDTH // LRU_BLOCKS
CONV_WIDTH = 4
LRU_C = 8.0
MLA_HEADS = 8
QK_NOPE = 128
QK_ROPE = 64
V_HEAD = 128
Q_LORA = 384
KV_LORA = 256
ROPE_THETA = 10000.0
Q_BLOCK = 128
EPS = 1e-6
IN_SPLITS = (LRU_WIDTH, LRU_WIDTH, Q_LORA, KV_LORA, QK_ROPE, D_MODEL, D_MODEL)
IN_COLS = sum(IN_SPLITS)

kernel_name = "hybrid_rglru_mla_macaron_block"


def rmsnorm(x, g):
    xf = x.astype(jnp.float32)
    y = xf * lax.rsqrt(jnp.mean(xf * xf, axis=-1, keepdims=True) + EPS)
    return (y * g.astype(jnp.float32)).astype(x.dtype)


def swiglu(x, w_gate, w_up, w_down):
    return (jax.nn.silu(x @ w_gate) * (x @ w_up)) @ w_down


def centred_depthwise_conv(x, w, b):
    s = x.shape[1]
    left = CONV_WIDTH // 2
    xp = jnp.pad(x, ((0, 0), (left, CONV_WIDTH - 1 - left), (0, 0)))
    y = b
    for k in range(CONV_WIDTH):
        y = y + xp[:, k:k + s, :] * w[k]
    return y


def _linear_recurrence_combine(left, right):
    a_l, b_l = left
    a_r, b_r = right
    return a_l * a_r, a_r * b_l + b_r


def rg_lru(x, w_r, b_r, w_i, b_i, lam, reverse):
    bsz, s, c = x.shape
    xb = x.reshape(bsz, s, LRU_BLOCKS, LRU_BLOCK_W)
    r = jax.nn.sigmoid(jnp.einsum('bsnc,ncd->bsnd', xb, w_r).reshape(bsz, s, c) + b_r)
    gi = jax.nn.sigmoid(jnp.einsum('bsnc,ncd->bsnd', xb, w_i).reshape(bsz, s, c) + b_i)
    log_a = -LRU_C * r.astype(jnp.float32) * jax.nn.softplus(-lam.astype(jnp.float32))
    a = jnp.exp(log_a)
    u = jnp.sqrt(-jnp.expm1(2.0 * log_a)) * (gi * x).astype(jnp.float32)
    _, h = lax.associative_scan(_linear_recurrence_combine, (a, u), reverse=reverse, axis=1)
    return h.astype(x.dtype)


def rope_cos_sin(positions, dim, dtype):
    inv_freq = ROPE_THETA ** (-jnp.arange(0, dim, 2, dtype=jnp.float32) / dim)
    ang = positions.astype(jnp.float32)[..., None] * inv_freq
    return (jnp.cos(ang)[:, :, None, :].astype(dtype), jnp.sin(ang)[:, :, None, :].astype(dtype))


def apply_rope(x, cos, sin):
    x1, x2 = jnp.split(x, 2, axis=-1)
    return jnp.concatenate([x1 * cos - x2 * sin, x2 * cos + x1 * sin], axis=-1)


def mla(c_q, c_kv, k_rope, positions, q_norm, w_uq, kv_norm, w_ukv):
    bsz, s, _ = c_q.shape
    cos, sin = rope_cos_sin(positions, QK_ROPE, c_q.dtype)
    q = (rmsnorm(c_q, q_norm) @ w_uq).reshape(bsz, s, MLA_HEADS, QK_NOPE + QK_ROPE)
    q = jnp.concatenate([q[..., :QK_NOPE], apply_rope(q[..., QK_NOPE:], cos, sin)], axis=-1)
    kv = (rmsnorm(c_kv, kv_norm) @ w_ukv).reshape(bsz, s, MLA_HEADS, QK_NOPE + V_HEAD)
    k_pe = jnp.broadcast_to(apply_rope(k_rope[:, :, None, :], cos, sin), (bsz, s, MLA_HEADS, QK_ROPE))
    k = jnp.concatenate([kv[..., :QK_NOPE], k_pe], axis=-1)
    v = kv[..., QK_NOPE:]
    scale = (QK_NOPE + QK_ROPE) ** -0.5
    n_blk = s // Q_BLOCK
    q_blocks = q.reshape(bsz, n_blk, Q_BLOCK, MLA_HEADS, QK_NOPE + QK_ROPE).transpose(1, 0, 2, 3, 4)

    def attend(qb):
        sc = jnp.einsum('bqhd,bkhd->bhqk', qb, k).astype(jnp.float32) * scale
        pr = jax.nn.softmax(sc, axis=-1).astype(v.dtype)
        return jnp.einsum('bhqk,bkhd->bqhd', pr, v)

    o = lax.map(attend, q_blocks)
    return o.transpose(1, 0, 2, 3, 4).reshape(bsz, s, MLA_HEADS * V_HEAD)


def _dense(k, shape, fan_in):
    return jax.random.normal(k, shape, jnp.float32) * (fan_in ** -0.5)


def _gain(k, shape):
    return 1.0 + 0.05 * jax.random.normal(k, shape, jnp.float32)


def _bias(k, shape):
    return 0.01 * jax.random.normal(k, shape, jnp.float32)


def setup_inputs(seed: int = 0) -> dict:
    key = jax.random.key(seed)
    ks = jax.random.split(key, 32)
    L = DEPTH
    a_init = jax.random.uniform(ks[15], (L, 2, LRU_WIDTH), jnp.float32, 0.9, 0.999) ** (1.0 / LRU_C)
    lru_lambda = jnp.log(a_init) - jnp.log1p(-a_init)
    return {
        "x": jax.random.normal(ks[0], (BATCH, SEQ, D_MODEL), jnp.float32),
        "p": jax.random.normal(ks[1], (L, BATCH, SEQ, PLE_DIM), jnp.float32),
        "positions": jnp.broadcast_to(jnp.arange(SEQ, dtype=jnp.int32), (BATCH, SEQ)),
        "ffn1_norm": _gain(ks[2], (L, D_MODEL)),
        "ffn1_w_gate": _dense(ks[3], (L, D_MODEL, D_FF), D_MODEL),
        "ffn1_w_up": _dense(ks[4], (L, D_MODEL, D_FF), D_MODEL),
        "ffn1_w_down": _dense(ks[5], (L, D_FF, D_MODEL), D_FF),
        "mix_norm": _gain(ks[6], (L, D_MODEL)),
        "w_in": _dense(ks[7], (L, D_MODEL, IN_COLS), D_MODEL),
        "conv_w": _dense(ks[8], (L, CONV_WIDTH, LRU_WIDTH), CONV_WIDTH),
        "conv_b": _bias(ks[9], (L, LRU_WIDTH)),
        "lru_w_r": _dense(ks[10], (L, 2, LRU_BLOCKS, LRU_BLOCK_W, LRU_BLOCK_W), LRU_BLOCK_W),
        "lru_b_r": _bias(ks[11], (L, 2, LRU_WIDTH)),
        "lru_w_i": _dense(ks[12], (L, 2, LRU_BLOCKS, LRU_BLOCK_W, LRU_BLOCK_W), LRU_BLOCK_W),
        "lru_b_i": _bias(ks[13], (L, 2, LRU_WIDTH)),
        "lru_lambda": lru_lambda,
        "w_lru_out": _dense(ks[14], (L, LRU_WIDTH, D_MODEL), LRU_WIDTH),
        "q_norm": _gain(ks[16], (L, Q_LORA)),
        "w_uq": _dense(ks[17], (L, Q_LORA, MLA_HEADS * (QK_NOPE + QK_ROPE)), Q_LORA),
        "kv_norm": _gain(ks[18], (L, KV_LORA)),
        "w_ukv": _dense(ks[19], (L, KV_LORA, MLA_HEADS * (QK_NOPE + V_HEAD)), KV_LORA),
        "w_mla_out": _dense(ks[20], (L, MLA_HEADS * V_HEAD, D_MODEL), MLA_HEADS * V_HEAD),
        "w_o": _dense(ks[21], (L, D_MODEL, D_MODEL), D_MODEL),
        "ffn2_norm": _gain(ks[22], (L, D_MODEL)),
        "ffn2_w_gate": _dense(ks[23], (L, D_MODEL, D_FF), D_MODEL),
        "ffn2_w_up": _dense(ks[24], (L, D_MODEL, D_FF), D_MODEL),
        "ffn2_w_down": _dense(ks[25], (L, D_FF, D_MODEL), D_FF),
        "ple_norm": _gain(ks[26], (L, D_MODEL)),
        "ple_w_gate": _dense(ks[27], (L, D_MODEL, D_MODEL), D_MODEL),
        "ple_w_proj": _dense(ks[28], (L, PLE_DIM, D_MODEL), PLE_DIM),
        "ple_proj_norm": _gain(ks[29], (L, D_MODEL)),
        "final_norm": _gain(ks[30], (D_MODEL,)),
    }


def reference(x, p, positions, ffn1_norm, ffn1_w_gate, ffn1_w_up, ffn1_w_down, mix_norm, w_in,
              conv_w, conv_b, lru_w_r, lru_b_r, lru_w_i, lru_b_i, lru_lambda, w_lru_out,
              q_norm, w_uq, kv_norm, w_ukv, w_mla_out, w_o, ffn2_norm, ffn2_w_gate, ffn2_w_up,
              ffn2_w_down, ple_norm, ple_w_gate, ple_w_proj, ple_proj_norm, final_norm):
    split_at = np.cumsum(IN_SPLITS)[:-1].tolist()
    h = x
    for i in range(DEPTH):
        h = h + 0.5 * swiglu(rmsnorm(h, ffn1_norm[i]), ffn1_w_gate[i], ffn1_w_up[i], ffn1_w_down[i])
        u = rmsnorm(h, mix_norm[i])
        z_x, z_g, c_q, c_kv, k_rope, gate_a, gate_b = jnp.split(u @ w_in[i], split_at, axis=-1)
        xa = centred_depthwise_conv(z_x, conv_w[i], conv_b[i])
        ya = (rg_lru(xa, lru_w_r[i, 0], lru_b_r[i, 0], lru_w_i[i, 0], lru_b_i[i, 0], lru_lambda[i, 0], False)
              + rg_lru(xa, lru_w_r[i, 1], lru_b_r[i, 1], lru_w_i[i, 1], lru_b_i[i, 1], lru_lambda[i, 1], True))
        ya = (ya * jax.nn.gelu(z_g)) @ w_lru_out[i]
        yb = mla(c_q, c_kv, k_rope, positions, q_norm[i], w_uq[i], kv_norm[i], w_ukv[i]) @ w_mla_out[i]
        merged = jax.nn.sigmoid(gate_a) * ya + jax.nn.sigmoid(gate_b) * yb
        h = h + merged @ w_o[i]
        h = h + 0.5 * swiglu(rmsnorm(h, ffn2_norm[i]), ffn2_w_gate[i], ffn2_w_up[i], ffn2_w_down[i])
        ple_gate = jax.nn.sigmoid(rmsnorm(h, ple_norm[i]) @ ple_w_gate[i])
        h = h + ple_gate * rmsnorm(p[i] @ ple_w_proj[i], ple_proj_norm[i])
    return rmsnorm(h, final_norm)
```

```python
import contextlib
import math
import numpy as np
import concourse.bass as bass
import concourse.mybir as mybir
from concourse.bass_utils import run_bass_kernel_spmd

F32 = mybir.dt.float32
BF16 = mybir.dt.bfloat16
I32 = mybir.dt.int32
U8 = mybir.dt.uint8
AF = mybir.ActivationFunctionType
ALU = mybir.AluOpType

D = 1024
DFF = 2816
NJ = DFF // 128
PLE = 256
NH = 8
QL = 384
KVL = 256
EPS = 1e-6
NCOL = 136
SM_SCALE = 192.0 ** -0.5
TWO_PI = 2.0 * math.pi
RSTD_MODE = "sqrt"
PE_EVERY = 8
DEN_MODE = "hybrid"

ENGS = ("pe", "act", "dve", "pool", "sp")
DMA_K = 8
SAME_ENGINE_RAW_ONLY = False
DMA_KQ = {}


class R:
    __slots__ = ("name", "w", "rs")

    def __init__(self, name=""):
        self.name = name
        self.w = None
        self.rs = {}


class Op:
    __slots__ = ("eng", "fn", "deps", "dma", "sig", "val", "sem", "n")


class Sched:
    def __init__(self, nc, es):
        self.nc = nc
        self.es = es
        self.ops = {e: [] for e in ENGS}
        self.dma_hist = {e: [] for e in ENGS}
        self.nops = 0

    def _rec(self, eng, fn, r, w, dma):
        o = Op()
        o.eng = eng; o.fn = fn; o.dma = dma; o.sig = False; o.val = 0; o.sem = None
        o.n = self.nops; self.nops += 1
        deps = []
        for x in r:
            if x.w is not None:
                deps.append(x.w)
        for x in w:
            if x.w is not None:
                deps.append(x.w)
            deps.extend(x.rs.values())
        fdeps = []
        seen = set()
        for p in deps:
            if p is o or id(p) in seen:
                continue
            seen.add(id(p))
            if (not p.dma) and (not dma) and p.eng == eng:
                if eng == "pe":
                    continue
                if SAME_ENGINE_RAW_ONLY and not any(x.w is p for x in r):
                    continue
            fdeps.append(p)
        o.deps = fdeps
        for x in r:
            x.rs[("dma", o.n) if dma else eng] = o
        for x in w:
            x.w = o
            x.rs = {}
        if dma:
            h = self.dma_hist[eng]
            kk = DMA_KQ.get(eng, DMA_K)
            if len(h) >= kk:
                o.deps.append(h[-kk])
            h.append(o)
        self.ops[eng].append(o)
        return o

    def op(self, eng, fn, r=(), w=()):
        return self._rec(eng, fn, r, w, False)

    def dma(self, eng, fn, r=(), w=()):
        return self._rec(eng, fn, r, w, True)

    def finish(self):
        nc = self.nc
        for e in ENGS:
            for o in self.ops[e]:
                for p in o.deps:
                    p.sig = True
        sems = {e: self.es.enter_context(nc.semaphore("s_" + e)) for e in ENGS}
        dsems = {}
        for e in ENGS:
            if self.dma_hist[e]:
                dsems[e] = [self.es.enter_context(nc.semaphore("d_%s%d" % (e, i))) for i in range(DMA_K)]
        for e in ENGS:
            c = 0
            nd = 0
            for o in self.ops[e]:
                if o.dma:
                    o.sem = dsems[e][nd % DMA_K]
                    o.val = 16 * (nd // DMA_K + 1)
                    nd += 1
                elif o.sig:
                    c += 1
                    o.val = c
                    o.sem = sems[e]
        block = self.es.enter_context(nc.Block())

        def emit(name, engine):
            waited = {}
            for o in self.ops[name]:
                for p in o.deps:
                    k = id(p.sem)
                    if waited.get(k, 0) >= p.val:
                        continue
                    waited[k] = p.val
                    engine.wait_ge(p.sem, p.val)
                inst = o.fn(engine)
                if o.dma:
                    inst.then_inc(o.sem, 16)
                elif o.sig:
                    inst.then_inc(o.sem, 1)

        @block.tensor
        def _(eng):
            emit("pe", eng)

        @block.scalar
        def _(eng):
            emit("act", eng)

        @block.vector
        def _(eng):
            emit("dve", eng)

        @block.gpsimd
        def _(eng):
            emit("pool", eng)

        @block.sync
        def _(eng):
            emit("sp", eng)
            for e in ENGS:
                for o in self.dma_hist[e][-DMA_K:]:
                    eng.wait_ge(o.sem, o.val)


class Arena:
    def __init__(self, nc, es, cap):
        self.t = es.enter_context(nc.sbuf_tensor("arena", [128, cap], U8))
        self.cap = cap
        self.live = []

    def take(self, name, off, fshape, dt, nres=1):
        esz = 2 if dt == BF16 else 4
        size = int(np.prod(fshape)) * esz
        assert off % 4 == 0 and off + size <= self.cap, (name, off, size, self.cap)
        inh = []
        seen = set()
        for (o, s, rl) in self.live:
            if o < off + size and off < o + s:
                for r in rl:
                    for q in ([r.w] if r.w is not None else []) + list(r.rs.values()):
                        if id(q) not in seen:
                            seen.add(id(q))
                            inh.append(q)
        rl = [R("%s%d" % (name, i)) for i in range(nres)]
        for r in rl:
            for i, o in enumerate(inh):
                r.rs[("inh", i)] = o
        self.live.append((off, size, rl))
        ap = self.t[:, off:off + size].bitcast(dt)
        if len(fshape) == 2:
            ap = ap.rearrange("p (a b) -> p a b", a=fshape[0])
        elif len(fshape) == 3:
            ap = ap.rearrange("p (a b c) -> p a b c", a=fshape[0], b=fshape[1])
        return ap, rl


def build(T, dbg=False):
    NT = T // 512
    NS = T // 128
    nc = bass.Bass("TRN2", target_bir_lowering=False)

    def din(name, shape, dt=F32):
        return nc.dram_tensor(name, shape, dt, kind="ExternalInput").ap()

    skind = "ExternalOutput" if dbg else "Internal"

    def dscr(name, shape, dt=F32):
        return nc.dram_tensor(name, shape, dt, kind=skind).ap()

    x_d = din("x", [T, D]); p_d = din("p", [T, PLE]); pos_d = din("pos", [64, T], I32)
    cols_d = din("cols", [128, NCOL]); ident_d = din("ident", [128, 128]); bc_d = din("bc", [128, 2 * D])
    f1g_d = din("ffn1_w_gate", [D, DFF]); f1u_d = din("ffn1_w_up", [D, DFF]); f1d_d = din("ffn1_w_down", [DFF, D])
    f2g_d = din("ffn2_w_gate", [D, DFF]); f2u_d = din("ffn2_w_up", [D, DFF]); f2d_d = din("ffn2_w_down", [DFF, D])
    win_d = din("w_in", [D, 4800])
    wr_d = din("lru_w_r", [16, 128, 128]); wi_d = din("lru_w_i", [16, 128, 128])
    wlo_d = din("w_lru_out", [D, D]); wuq_d = din("w_uq", [QL, 1536]); wukv_d = din("w_ukv", [KVL, 2048])
    wmo_d = din("w_mla_out", [D, D]); wo_d = din("w_o", [D, D])
    wpg_d = din("ple_w_gate", [D, D]); wpp_d = din("ple_w_proj", [PLE, D])
    out_d = nc.dram_tensor("out", [T, D], F32, kind="ExternalOutput").ap()
    h_s = dscr("h_s", [T, D])
    z_s = dscr("z_s", [2 * D, T])
    o_s = dscr("o_s", [D, T], BF16)
    ya_s = dscr("ya_s", [D, T], BF16)
    rope_s = dscr("rope_s", [2, 64, T])

    es = contextlib.ExitStack()
    with es:
        S = Sched(nc, es)
        CAP = 212000
        A = Arena(nc, es, CAP)
        pst = [es.enter_context(nc.psum_tensor("ps%d" % i, [128, 512], F32)) for i in range(8)]
        psR = [R("ps%d" % i) for i in range(8)]
        ps_ctr = {}

        def PS(lo=0, hi=8):
            c = ps_ctr.get((lo, hi), 0)
            ps_ctr[(lo, hi)] = c + 1
            i = lo + c % (hi - lo)
            return pst[i], psR[i]

        hR = [R("h%d" % i) for i in range(NT * 4)]
        zR = [[R("z%d_%d" % (c, i)) for i in range(NT)] for c in range(16)]
        oR = [[R("o%d_%d" % (h, i)) for i in range(NT)] for h in range(NH)]
        yaR = [[R("ya%d_%d" % (c, i)) for i in range(NT)] for c in range(8)]

        CB = CAP - 2304
        cols, (colsR,) = A.take("cols", CB, [NCOL], F32)
        ident, (identR,) = A.take("ident", CB + 544, [128], BF16)
        ones, (onesR,) = A.take("ones", CB + 800, [128], BF16)
        stat, statR = A.take("stat", CB + 1056, [64], F32, nres=8)
        S.dma("sp", lambda e: e.dma_start(out=cols, in_=cols_d), w=[colsR])
        S.dma("pool", lambda e: e.dma_start(out=ident, in_=ident_d), w=[identR])
        S.op("dve", lambda e: e.memset(ones, 1.0), w=[onesR])
        ones32, (ones32R,) = A.take("ones32", CB + 1312, [128], F32)
        S.op("dve", lambda e: e.memset(ones32, 1.0), w=[ones32R])

        def col(i, n=128):
            return cols[0:n, i:i + 1]

        def load_w(dst, dstR, src_rows, ncols_lo, ncols_hi, nk):
            for k in range(nk):
                S.dma("pool", lambda e, k=k: e.dma_start(
                    out=dst[:, k, :], in_=src_rows[k * 128:(k + 1) * 128, ncols_lo:ncols_hi]), w=[dstR[k]])

        def rstd_op(ssc, sdc, rsc, width, rr, mode=None):
            if (mode or RSTD_MODE) == "pow":
                S.op("dve", lambda e: e.tensor_scalar(out=sdc, in0=ssc, scalar1=1.0 / width, scalar2=EPS, op0=ALU.mult,
                                                      op1=ALU.add), r=[rr], w=[rr])
                S.op("pool", lambda e: e.tensor_tensor(out=rsc, in0=sdc, in1=col(133), op=ALU.pow), r=[rr, colsR], w=[rr])
            else:
                S.op("act", lambda e: e.activation(out=sdc, in_=ssc, func=AF.Sqrt, scale=1.0 / width, bias=col(127)),
                     r=[rr, colsR], w=[rr])
                S.op("dve", lambda e: e.reciprocal(out=rsc, in_=sdc), r=[rr], w=[rr])

        def rms_T(xt, xtR, sl, s, gcol0, xs, xsR, width=D, mode=None):
            xin = xt[:, sl, :]
            ssc = stat[:, s:s + 1]
            sdc = stat[:, 8 + s:9 + s]
            rsc = stat[:, 16 + s:17 + s]
            S.op("act", lambda e: e.activation(out=xs[:, s, 0:width], in_=xin, func=AF.Square, accum_out=ssc),
                 r=[xtR[sl]], w=[xsR[s], statR[s]])
            rstd_op(ssc, sdc, rsc, width, statR[s], mode)
            S.op("dve", lambda e: e.tensor_scalar(out=xs[:, s, 0:width], in0=xin, scalar1=rsc, scalar2=None,
                                                  op0=ALU.mult), r=[xtR[sl], statR[s]], w=[xsR[s]])

        def transposes(xs, xsR, nk, gcol0, dstT, dstTR, evac_engs=("act", "dve"), c0=0):
            if c0 or True:
                dstT = dstT[:, :, c0:c0 + 512]
            for k in range(nk):
                ps, pr = PS()
                for s in range(4):
                    S.op("pe", lambda e, s=s, k=k, ps=ps: e.matmul(
                        ps[:, s * 128:(s + 1) * 128], lhsT=xs[:, s, k * 128:(k + 1) * 128], rhs=ident,
                        start=True, stop=True), r=[xsR[s], identR], w=[pr])
                eng = evac_engs[k % len(evac_engs)]
                g = col(gcol0 + k)
                if eng == "act":
                    S.op("act", lambda e, k=k, ps=ps, g=g: e.activation(out=dstT[:, k, :], in_=ps[:, :],
                                                                      func=AF.Copy, scale=g),
                         r=[pr, colsR], w=[dstTR[k]])
                else:
                    S.op("dve", lambda e, k=k, ps=ps, g=g: e.tensor_scalar(out=dstT[:, k, :], in0=ps[:, :], scalar1=g,
                                                                         scalar2=None, op0=ALU.mult),
                         r=[pr, colsR], w=[dstTR[k]])

        def ffn_phase(wg_d, wu_d, wd_d, gcol0, src_d, srcR, dst_d, dstR, pre_hook=None):
            Wg, WgR = A.take("Wg", 0, [8, DFF], BF16, nres=8)
            Wu, WuR = A.take("Wu", 45056, [8, DFF], BF16, nres=8)
            Wd, WdR = A.take("Wd", 90112, [NJ, D], BF16, nres=NJ)
            o0 = 135168
            NSL = 8
            xt, xtR = A.take("xt", o0, [NSL, D], F32, nres=NSL); o0 += NSL * 4096
            xs, xsR = A.take("xs", o0, [4, D], BF16, nres=4); o0 += 8192
            xnT, xnTR = A.take("xnT", o0, [8, 512], BF16, nres=8); o0 += 8192
            o_hff = o0
            wg_v = wg_d.rearrange("(k p) n -> p k n", p=128)
            wu_v = wu_d.rearrange("(k p) n -> p k n", p=128)
            for blk in range(6):
                lo, hi = blk * 512, min(DFF, blk * 512 + 512)
                S.dma("pool", lambda e, lo=lo, hi=hi: e.dma_start(out=Wg[:, :, lo:hi], in_=wg_v[:, :, lo:hi]), w=[WgR[blk]])
                S.dma("pool", lambda e, lo=lo, hi=hi: e.dma_start(out=Wu[:, :, lo:hi], in_=wu_v[:, :, lo:hi]), w=[WuR[blk]])
            load_w(Wd, WdR, wd_d, 0, D, NJ)
            slot = [0]

            def load_tile(i):
                sls = []
                for s in range(4):
                    sl = slot[0] % NSL
                    slot[0] += 1
                    r0 = i * 512 + s * 128
                    S.dma("sp", lambda e, sl=sl, r0=r0: e.dma_start(out=xt[:, sl, :], in_=src_d[r0:r0 + 128, :]),
                          r=[srcR[i * 4 + s]], w=[xtR[sl]])
                    sls.append(sl)
                return sls

            def front(i, sls):
                for s in range(4):
                    rms_T(xt, xtR, sls[s], s, gcol0, xs, xsR)
                transposes(xs, xsR, 8, gcol0, xnT, xnTR)

            def gate_up(i):
                for j in range(NJ):
                    pg, pgR = PS()
                    pu, puR = PS()
                    for k in range(8):
                        S.op("pe", lambda e, k=k, j=j, pg=pg: e.matmul(
                            pg[:, :], lhsT=Wg[:, k, j * 128:(j + 1) * 128], rhs=xnT[:, k, :],
                            start=(k == 0), stop=(k == 7)), r=[WgR[j // 4], xnTR[k]], w=[pgR])
                    for k in range(8):
                        S.op("pe", lambda e, k=k, j=j, pu=pu: e.matmul(
                            pu[:, :], lhsT=Wu[:, k, j * 128:(j + 1) * 128], rhs=xnT[:, k, :],
                            start=(k == 0), stop=(k == 7)), r=[WuR[j // 4], xnTR[k]], w=[puR])
                    b = j % 2
                    S.op("act", lambda e, b=b, pg=pg: e.activation(out=sg[:, b, :], in_=pg[:, :], func=AF.Silu),
                         r=[pgR], w=[sgR[b]])
                    S.op("dve", lambda e, b=b, j=j, pu=pu: e.tensor_tensor(out=hff[:, j, :], in0=pu[:, :], in1=sg[:, b, :],
                                                                         op=ALU.mult),
                         r=[puR, sgR[b]], w=[hffR[j]])

            def down(i, sls):
                for s in range(4):
                    sl = sls[s]
                    for c in range(2):
                        pd, pdR = PS()
                        for j in range(NJ):
                            S.op("pe", lambda e, j=j, s=s, c=c, pd=pd: e.matmul(
                                pd[:, :], lhsT=hff[:, j, s * 128:(s + 1) * 128], rhs=Wd[:, j, c * 512:(c + 1) * 512],
                                start=(j == 0), stop=(j == NJ - 1)), r=[hffR[j], WdR[j]], w=[pdR])
                        S.op("dve", lambda e, sl=sl, c=c, pd=pd: e.scalar_tensor_tensor(
                            out=xt[:, sl, c * 512:(c + 1) * 512], in0=pd[:, :], scalar=0.5,
                            in1=xt[:, sl, c * 512:(c + 1) * 512], op0=ALU.mult, op1=ALU.add),
                             r=[pdR, xtR[sl]], w=[xtR[sl]])
                    r0 = i * 512 + s * 128
                    S.dma("sp", lambda e, sl=sl, r0=r0: e.dma_start(out=dst_d[r0:r0 + 128, :], in_=xt[:, sl, :]),
                          r=[xtR[sl]], w=[dstR[i * 4 + s]])

            sl_cur = load_tile(0)
            front(0, sl_cur)
            if pre_hook is not None:
                pre_hook()
            o0 = o_hff
            hff, hffR = A.take("hff", o0, [NJ, 512], BF16, nres=NJ); o0 += NJ * 1024
            sg, sgR = A.take("sg", o0, [2, 512], BF16, nres=2); o0 += 2048
            assert o0 <= CB, (o0, CB)
            for i in range(NT):
                gate_up(i)
                if i + 1 < NT:
                    sl_nxt = load_tile(i + 1)
                    front(i + 1, sl_nxt)
                down(i, sl_cur)
                if i + 1 < NT:
                    sl_cur = sl_nxt

        ropeR = [R("rope%d" % i) for i in range(2 * NT)]

        def rope_tables():
            base = 184320
            sets = []
            for b in range(2):
                o0 = base + b * 12288
                bufs = []
                for nm, dt in (("posi", I32), ("ang", F32), ("ki", I32), ("kf", F32), ("rc", F32), ("rs", F32)):
                    ap, (rr,) = A.take("%s%d" % (nm, b), o0, [512], dt)
                    o0 += 2048
                    bufs.append((ap, rr))
                sets.append(bufs)

            def reduce_and_sin(ang, angR, ki, kiR, kf, kfR, dst, dstR, offs, scale):
                S.op("dve", lambda e: e.tensor_scalar(out=dst[0:64, :], in0=ang[0:64, :], scalar1=col(125, 64),
                                                      scalar2=offs, op0=ALU.mult, op1=ALU.add),
                     r=[angR, colsR], w=[dstR])
                S.op("dve", lambda e: e.scalar_tensor_tensor(out=dst[0:64, :], in0=ang[0:64, :], scalar=col(134, 64),
                                                             in1=dst[0:64, :], op0=ALU.mult, op1=ALU.add),
                     r=[angR, colsR, dstR], w=[dstR])
                S.op("dve", lambda e: e.tensor_copy(out=ki[0:64, :], in_=dst[0:64, :]), r=[dstR], w=[kiR])
                S.op("dve", lambda e: e.tensor_copy(out=kf[0:64, :], in_=ki[0:64, :]), r=[kiR], w=[kfR])
                S.op("dve", lambda e: e.tensor_tensor(out=dst[0:64, :], in0=dst[0:64, :], in1=kf[0:64, :], op=ALU.subtract),
                     r=[dstR, kfR], w=[dstR])
                S.op("dve", lambda e: e.tensor_single_scalar(out=kf[0:64, :], in_=dst[0:64, :], scalar=0.5, op=ALU.is_gt),
                     r=[dstR], w=[kfR])
                S.op("dve", lambda e: e.tensor_tensor(out=dst[0:64, :], in0=dst[0:64, :], in1=kf[0:64, :], op=ALU.subtract),
                     r=[dstR, kfR], w=[dstR])
                S.op("act", lambda e: e.activation(out=dst[0:64, :], in_=dst[0:64, :], func=AF.Sin, scale=scale),
                     r=[dstR, colsR], w=[dstR])

            def piece(i):
                (posi, posiR), (ang, angR), (ki, kiR), (kf, kfR), (rc, rcR), (rs, rsR) = sets[i % 2]
                c0 = i * 512
                S.dma("sp", lambda e: e.dma_start(out=posi[0:64, :], in_=pos_d[:, c0:c0 + 512]), w=[posiR])
                S.op("dve", lambda e: e.tensor_copy(out=ang[0:64, :], in_=posi[0:64, :]), r=[posiR], w=[angR])
                reduce_and_sin(ang, angR, ki, kiR, kf, kfR, rc, rcR, 0.25, TWO_PI)
                reduce_and_sin(ang, angR, ki, kiR, kf, kfR, rs, rsR, 0.0, col(128, 64))
                S.dma("sp", lambda e: e.dma_start(out=rope_s[0, :, c0:c0 + 512], in_=rc[0:64, :]), r=[rcR], w=[ropeR[i]])
                S.dma("sp", lambda e: e.dma_start(out=rope_s[1, :, c0:c0 + 512], in_=rs[0:64, :]), r=[rsR], w=[ropeR[NT + i]])

            for i in range(NT):
                piece(i)


        xR_in = [R("xin%d" % i) for i in range(NT * 4)]
        ffn_phase(f1g_d, f1u_d, f1d_d, 0, x_d, xR_in, h_s, hR, pre_hook=rope_tables)


        PB = CB - 14 * T - 8 * T
        cqnT, cqnTR = A.take("cqnT", PB, [3, T], BF16, nres=3 * NT)
        ckvnT, ckvnTR = A.take("ckvnT", PB + 6 * T, [2, T], BF16, nres=2 * NT)
        kpeT, kpeTR = A.take("kpeT", PB + 10 * T, [T], BF16, nres=NT)
        cosT, (cosR,) = A.take("cosT", PB + 14 * T, [T], F32)
        sinT, (sinR,) = A.take("sinT", PB + 18 * T, [T], F32)

        def phase_b():
            S.dma("sp", lambda e: e.dma_start(out=cosT[0:64, :], in_=rope_s[0]), r=ropeR, w=[cosR])
            S.dma("sp", lambda e: e.dma_start(out=sinT[0:64, :], in_=rope_s[1]), r=ropeR, w=[sinR])
            Win, WinR = A.take("Win", 0, [8, 2752], BF16, nres=8)
            Wsw, (WswR,) = A.take("Wsw", 44032, [8, 64], BF16)
            o0 = 44032 + 1024
            NSL = 8
            xt, xtR = A.take("xt", o0, [NSL, D], F32, nres=NSL); o0 += NSL * 4096
            xs, xsR = A.take("xs", o0, [4, D], BF16, nres=4); o0 += 8192
            uT2, uT2R = A.take("uT", o0, [2, 8, 512], BF16, nres=16); o0 += 16384
            zt, ztR = A.take("zt", o0, [3, 512], F32, nres=3); o0 += 6144
            cqs, cqsR = A.take("cqs", o0, [4, QL], BF16, nres=4); o0 += 4 * QL * 2
            cks, cksR = A.take("cks", o0, [4, KVL], BF16, nres=4); o0 += 4 * KVL * 2
            tmp, tmpR = A.take("tmpB", o0, [2, 512], F32, nres=2); o0 += 4096
            assert o0 <= PB, (o0, PB)
            load_w(Win, WinR, win_d, 0, 2752, 8)
            S.op("dve", lambda e: e.tensor_copy(out=Wsw[:, :, 0:32], in_=Win[:, :, 2720:2752]), r=WinR, w=[WswR])
            S.op("dve", lambda e: e.tensor_copy(out=Wsw[:, :, 32:64], in_=Win[:, :, 2688:2720]), r=WinR, w=[WswR])
            slot = [0]
            for i in range(NT):
                S.op("pool", lambda e, i=i: e.memset(kpeT[64:128, i * 512:(i + 1) * 512], 0.0), w=[kpeTR[i]])

            def load_tile(i):
                sls = []
                for s in range(4):
                    sl = slot[0] % NSL
                    slot[0] += 1
                    r0 = i * 512 + s * 128
                    S.dma("sp", lambda e, sl=sl, r0=r0: e.dma_start(out=xt[:, sl, :], in_=h_s[r0:r0 + 128, :]),
                          r=[hR[i * 4 + s]], w=[xtR[sl]])
                    sls.append(sl)
                return sls

            def front_a(i, sls):
                for s in range(4):
                    rms_T(xt, xtR, sls[s], s, 8, xs, xsR)

            def front_b(i):
                transposes(xs, xsR, 8, 8, uT2[:, i % 2], uT2R[(i % 2) * 8:(i % 2) * 8 + 8])

            zb = [0]

            def body(i, part):
                c0 = i * 512
                uT = uT2[:, i % 2]
                uTR = uT2R[(i % 2) * 8:(i % 2) * 8 + 8]
                if part == 1:
                    body1(i, c0, uT, uTR)
                else:
                    body2(i, c0, uT, uTR)

            def body1(i, c0, uT, uTR):
                for c in range(16):
                    ps, pr = PS()
                    for k in range(8):
                        S.op("pe", lambda e, k=k, c=c, ps=ps: e.matmul(
                            ps[:, :], lhsT=Win[:, k, c * 128:(c + 1) * 128], rhs=uT[:, k, :],
                            start=(k == 0), stop=(k == 7)), r=[WinR[k], uTR[k]], w=[pr])
                    b = zb[0] % 3
                    zb[0] += 1
                    eng = "act" if c % 2 == 0 else "dve"
                    if eng == "act":
                        S.op("act", lambda e, b=b, ps=ps: e.activation(out=zt[:, b, :], in_=ps[:, :], func=AF.Copy),
                             r=[pr], w=[ztR[b]])
                    else:
                        S.op("dve", lambda e, b=b, ps=ps: e.tensor_copy(out=zt[:, b, :], in_=ps[:, :]), r=[pr], w=[ztR[b]])
                    S.dma("sp", lambda e, b=b, c=c, c0=c0: e.dma_start(out=z_s[c * 128:(c + 1) * 128, c0:c0 + 512],
                                                                     in_=zt[:, b, :]), r=[ztR[b]], w=[zR[c][i]])

            def body2(i, c0, uT, uTR):
                p1, p1R = PS()
                p2, p2R = PS()
                for k in range(8):
                    S.op("pe", lambda e, k=k, p1=p1: e.matmul(p1[0:64, :], lhsT=Win[:, k, 2688:2752], rhs=uT[:, k, :],
                                                            start=(k == 0), stop=(k == 7)), r=[WinR[k], uTR[k]], w=[p1R])
                for k in range(8):
                    S.op("pe", lambda e, k=k, p2=p2: e.matmul(p2[0:64, :], lhsT=Wsw[:, k, :], rhs=uT[:, k, :],
                                                            start=(k == 0), stop=(k == 7)), r=[WswR, uTR[k]], w=[p2R])
                S.op("dve", lambda e, p1=p1: e.tensor_tensor(out=tmp[0:64, 0, :], in0=p1[0:64, :], in1=cosT[0:64, c0:c0 + 512],
                                                           op=ALU.mult), r=[p1R, cosR], w=[tmpR[0]])
                S.op("dve", lambda e, p2=p2: e.tensor_tensor(out=tmp[0:64, 1, :], in0=p2[0:64, :], in1=sinT[0:64, c0:c0 + 512],
                                                           op=ALU.mult), r=[p2R, sinR], w=[tmpR[1]])
                S.op("dve", lambda e: e.tensor_tensor(out=kpeT[0:64, c0:c0 + 512], in0=tmp[0:64, 0, :], in1=tmp[0:64, 1, :],
                                                      op=ALU.add), r=[tmpR[0], tmpR[1]], w=[kpeTR[i]])
                for s in range(4):
                    pq, pqR = PS()
                    pk, pkR = PS()
                    for k in range(8):
                        S.op("pe", lambda e, k=k, s=s, pq=pq: e.matmul(
                            pq[:, 0:QL], lhsT=uT[:, k, s * 128:(s + 1) * 128], rhs=Win[:, k, 2048:2048 + QL],
                            start=(k == 0), stop=(k == 7)), r=[WinR[k], uTR[k]], w=[pqR])
                    for k in range(8):
                        S.op("pe", lambda e, k=k, s=s, pk=pk: e.matmul(
                            pk[:, 0:KVL], lhsT=uT[:, k, s * 128:(s + 1) * 128], rhs=Win[:, k, 2432:2432 + KVL],
                            start=(k == 0), stop=(k == 7)), r=[WinR[k], uTR[k]], w=[pkR])
                    for (pp, ppR, dst, dstR, wd) in ((pq, pqR, cqs, cqsR, QL), (pk, pkR, cks, cksR, KVL)):
                        ssc = stat[:, 24 + s:25 + s]
                        sdc = stat[:, 32 + s:33 + s]
                        rsc = stat[:, 40 + s:41 + s]
                        S.op("act", lambda e, pp=pp, dst=dst, wd=wd, ssc=ssc, s=s: e.activation(
                            out=dst[:, s, :], in_=pp[:, 0:wd], func=AF.Square, accum_out=ssc),
                             r=[ppR], w=[dstR[s], statR[s]])
                        rstd_op(ssc, sdc, rsc, wd, statR[s])
                        S.op("dve", lambda e, pp=pp, dst=dst, wd=wd, rsc=rsc, s=s: e.tensor_scalar(
                            out=dst[:, s, :], in0=pp[:, 0:wd], scalar1=rsc, scalar2=None, op0=ALU.mult),
                             r=[ppR, statR[s]], w=[dstR[s]])
                transposes(cqs, cqsR, 3, 32, cqnT, [cqnTR[k * NT + i] for k in range(3)], c0=c0)
                transposes(cks, cksR, 2, 35, ckvnT, [ckvnTR[k * NT + i] for k in range(2)], c0=c0)

            sl_cur = load_tile(0)
            front_a(0, sl_cur)
            front_b(0)
            for i in range(NT):
                if i + 1 < NT:
                    sl_cur = load_tile(i + 1)
                    front_a(i + 1, sl_cur)
                body(i, 1)
                if i + 1 < NT:
                    front_b(i + 1)
                body(i, 2)

        phase_b()

        def phase_d():
            Wuq, WuqR = A.take("Wuq", 0, [3, 1536], BF16, nres=3)
            Wqs, (WqsR,) = A.take("Wqs", 9216, [3, NH, 64], BF16)
            Wukv, WukvR = A.take("Wukv", 12288, [2, 2048], BF16, nres=2)
            o0 = 20480
            hb = []
            for b in range(2):
                qn, qnR = A.take("qn%d" % b, o0, [T], BF16, nres=NT); o0 += 2 * T
                qp, qpR = A.take("qp%d" % b, o0, [T], BF16, nres=NT); o0 += 2 * T
                kn, knR = A.take("kn%d" % b, o0, [T], BF16, nres=NT); o0 += 2 * T
                vv, vvR = A.take("vv%d" % b, o0, [NS, 128], BF16, nres=NT); o0 += 2 * T
                hb.append((qn, qnR, qp, qpR, kn, knR, vv, vvR))
                for i in range(NT):
                    S.op("pool", lambda e, i=i, qp=qp: e.memset(qp[64:128, i * 512:(i + 1) * 512], 0.0), w=[qpR[i]])
            NPT = 6
            PT, PTR = A.take("PT", o0, [NPT, 512], BF16, nres=NPT); o0 += NPT * 1024
            acc, accR = A.take("acc", o0, [2, 2, 512], F32, nres=4); o0 += 8192
            tmp, tmpR = A.take("tmpD", o0, [2, 512], F32, nres=2); o0 += 4096
            rl, rlR = A.take("rl", o0, [2, 512], F32, nres=2); o0 += 4096
            ot, otR = A.take("ot", o0, [2, 512], BF16, nres=2); o0 += 2048
            assert o0 <= PB, (o0, PB)
            load_w(Wuq, WuqR, wuq_d, 0, 1536, 3)
            load_w(Wukv, WukvR, wukv_d, 0, 2048, 2)
            Wuq4 = Wuq.rearrange("p k (h d) -> p k h d", h=NH)
            S.op("dve", lambda e: e.tensor_copy(out=Wqs[:, :, :, 0:32], in_=Wuq4[:, :, :, 160:192]), r=WuqR, w=[WqsR])
            S.op("dve", lambda e: e.tensor_copy(out=Wqs[:, :, :, 32:64], in_=Wuq4[:, :, :, 128:160]), r=WuqR, w=[WqsR])

            def prep(h):
                for i in range(NT):
                    prep_tile(h, i)

            def prep_tile(h, i):
                qn, qnR, qp, qpR, kn, knR, vv, vvR = hb[h % 2]
                if True:
                    c0 = i * 512
                    lat_q = [cqnTR[k * NT + i] for k in range(3)]
                    lat_k = [ckvnTR[k * NT + i] for k in range(2)]
                    ps, pr = PS(0, 4)
                    for k in range(3):
                        S.op("pe", lambda e, k=k, ps=ps: e.matmul(
                            ps[:, :], lhsT=Wuq[:, k, h * 192:h * 192 + 128], rhs=cqnT[:, k, c0:c0 + 512],
                            start=(k == 0), stop=(k == 2)), r=[WuqR[k], lat_q[k]], w=[pr])
                    S.op("act", lambda e, ps=ps: e.activation(out=qn[:, c0:c0 + 512], in_=ps[:, :], func=AF.Copy),
                         r=[pr], w=[qnR[i]])
                    p1, p1R = PS(0, 4)
                    p2, p2R = PS(0, 4)
                    for k in range(3):
                        S.op("pe", lambda e, k=k, p1=p1: e.matmul(
                            p1[0:64, :], lhsT=Wuq[:, k, h * 192 + 128:h * 192 + 192], rhs=cqnT[:, k, c0:c0 + 512],
                            start=(k == 0), stop=(k == 2)), r=[WuqR[k], lat_q[k]], w=[p1R])
                    for k in range(3):
                        S.op("pe", lambda e, k=k, p2=p2: e.matmul(
                            p2[0:64, :], lhsT=Wqs[:, k, h, :], rhs=cqnT[:, k, c0:c0 + 512],
                            start=(k == 0), stop=(k == 2)), r=[WqsR, lat_q[k]], w=[p2R])
                    S.op("dve", lambda e, p1=p1: e.tensor_tensor(out=tmp[0:64, 0, :], in0=p1[0:64, :],
                                                               in1=cosT[0:64, c0:c0 + 512], op=ALU.mult),
                         r=[p1R, cosR], w=[tmpR[0]])
                    S.op("dve", lambda e, p2=p2: e.tensor_tensor(out=tmp[0:64, 1, :], in0=p2[0:64, :],
                                                               in1=sinT[0:64, c0:c0 + 512], op=ALU.mult),
                         r=[p2R, sinR], w=[tmpR[1]])
                    S.op("dve", lambda e: e.tensor_tensor(out=qp[0:64, c0:c0 + 512], in0=tmp[0:64, 0, :],
                                                          in1=tmp[0:64, 1, :], op=ALU.add),
                         r=[tmpR[0], tmpR[1]], w=[qpR[i]])
                    pk, pkR = PS(0, 4)
                    for k in range(2):
                        S.op("pe", lambda e, k=k, pk=pk: e.matmul(
                            pk[:, :], lhsT=Wukv[:, k, h * 256:h * 256 + 128], rhs=ckvnT[:, k, c0:c0 + 512],
                            start=(k == 0), stop=(k == 1)), r=[WukvR[k], lat_k[k]], w=[pkR])
                    S.op("act", lambda e, pk=pk: e.activation(out=kn[:, c0:c0 + 512], in_=pk[:, :], func=AF.Copy),
                         r=[pkR], w=[knR[i]])
                    pv, pvR = PS(0, 4)
                    for n in range(4):
                        for k in range(2):
                            S.op("pe", lambda e, k=k, n=n, pv=pv: e.matmul(
                                pv[:, n * 128:(n + 1) * 128], lhsT=ckvnT[:, k, c0 + n * 128:c0 + (n + 1) * 128],
                                rhs=Wukv[:, k, h * 256 + 128:h * 256 + 256], start=(k == 0), stop=(k == 1)),
                                 r=[WukvR[k], lat_k[k]], w=[pvR])
                    S.op("dve", lambda e, pv=pv: e.tensor_copy(
                        out=vv[:, i * 4:(i + 1) * 4, :], in_=pv[:, :].rearrange("p (n d) -> p n d", n=4)),
                         r=[pvR], w=[vvR[i]])

            ptc = [0]

            def attend(h):
                for qi in range(NT):
                    attend_q(h, qi)

            def attend_q(h, qi):
                qn, qnR, qp, qpR, kn, knR, vv, vvR = hb[h % 2]
                if True:
                    q0 = qi * 512
                    pO, pOR = PS(4, 6)
                    pL, pLR = PS(6, 8)

                    def scores(kj):
                        ps, pr = PS(0, 4)
                        ki = kj // 4
                        S.op("pe", lambda e, ps=ps: e.matmul(ps[:, :], lhsT=kn[:, kj * 128:(kj + 1) * 128],
                                                           rhs=qn[:, q0:q0 + 512], start=True, stop=False),
                             r=[knR[ki], qnR[qi]], w=[pr])
                        S.op("pe", lambda e, ps=ps: e.matmul(ps[:, :], lhsT=kpeT[:, kj * 128:(kj + 1) * 128],
                                                           rhs=qp[:, q0:q0 + 512], start=False, stop=True),
                             r=[kpeTR[ki], qpR[qi]], w=[pr])
                        b = ptc[0] % NPT
                        ptc[0] += 1
                        S.op("act", lambda e, ps=ps, b=b: e.activation(out=PT[:, b, :], in_=ps[:, :], func=AF.Exp,
                                                                     scale=SM_SCALE), r=[pr], w=[PTR[b]])
                        return b

                    LA = 2
                    pend = [scores(j) for j in range(min(LA, NS))]
                    ab = qi % 2
                    nd = [0]
                    for kj in range(NS):
                        b = pend.pop(0)
                        if kj + LA < NS:
                            pend.append(scores(kj + LA))
                        S.op("pe", lambda e, b=b, kj=kj, pO=pO: e.matmul(
                            pO[:, :], lhsT=vv[:, kj, :], rhs=PT[:, b, :], start=(kj == 0), stop=(kj == NS - 1)),
                             r=[vvR[kj // 4], PTR[b]], w=[pOR])
                        on_pe = (DEN_MODE == "pe") or (DEN_MODE == "hybrid" and kj % PE_EVERY == PE_EVERY - 1)
                        if on_pe:
                            first_pe = (kj == (0 if DEN_MODE == "pe" else PE_EVERY - 1))
                            last_pe = (kj == NS - 1) and DEN_MODE == "pe"
                            S.op("pe", lambda e, b=b, pL=pL, first_pe=first_pe, last_pe=last_pe: e.matmul(
                                pL[:, :], lhsT=ones, rhs=PT[:, b, :], start=first_pe, stop=last_pe),
                                 r=[onesR, PTR[b]], w=[pLR])
                            continue
                        par = nd[0] % 2
                        first = nd[0] < 2
                        nd[0] += 1
                        if first:
                            S.op("dve", lambda e, b=b, par=par: e.tensor_copy(out=acc[:, ab, par, :], in_=PT[:, b, :]),
                                 r=[PTR[b]], w=[accR[ab * 2 + par]])
                        else:
                            S.op("dve", lambda e, b=b, par=par: e.tensor_tensor(out=acc[:, ab, par, :], in0=acc[:, ab, par, :],
                                                                             in1=PT[:, b, :], op=ALU.add),
                                 r=[PTR[b], accR[ab * 2 + par]], w=[accR[ab * 2 + par]])
                    if DEN_MODE != "pe":
                        S.op("dve", lambda e: e.tensor_tensor(out=acc[:, ab, 0, :], in0=acc[:, ab, 0, :], in1=acc[:, ab, 1, :],
                                                              op=ALU.add), r=[accR[ab * 2], accR[ab * 2 + 1]], w=[accR[ab * 2]])
                        S.op("pe", lambda e, pL=pL: e.matmul(pL[:, :], lhsT=ones32, rhs=acc[:, ab, 0, :],
                                                           start=(DEN_MODE == "dve"), stop=True),
                             r=[ones32R, accR[ab * 2]], w=[pLR])
                    ob = qi % 2
                    S.op("dve", lambda e, ob=ob, pL=pL: e.reciprocal(out=rl[:, ob, :], in_=pL[:, :]), r=[pLR], w=[rlR[ob]])
                    S.op("dve", lambda e, ob=ob, pO=pO: e.tensor_tensor(out=ot[:, ob, :], in0=pO[:, :], in1=rl[:, ob, :],
                                                                       op=ALU.mult), r=[pOR, rlR[ob]], w=[otR[ob]])
                    S.dma("sp", lambda e, ob=ob: e.dma_start(out=o_s[h * 128:(h + 1) * 128, q0:q0 + 512], in_=ot[:, ob, :]),
                          r=[otR[ob]], w=[oR[h][qi]])

            prep(0)
            for h in range(NH):
                if h + 1 < NH:
                    prep(h + 1)
                attend(h)

        phase_d()

        def phase_c():
            Wr, (WrR,) = A.take("Wr", 0, [16, 128], BF16)
            Wi, (WiR,) = A.take("Wi", 4096, [16, 128], BF16)
            lc, (lcR,) = A.take("lc", 8192, [8, 16], F32)
            id32, (id32R,) = A.take("id32", 8704, [128], F32)
            dg, (dgR,) = A.take("dg", 9216, [32, 128], F32)
            o0 = 9216 + 16384
            TP = T + 4
            sets = []
            for b in range(2):
                zx, (zxR,) = A.take("zx%d" % b, o0, [TP], F32); o0 += 4 * TP
                zg, zgR = None, None
                xa, xaR = A.take("xa%d" % b, o0, [T], F32, nres=NT); o0 += 4 * T
                xab, xabR = A.take("xab%d" % b, o0, [T], BF16, nres=NT); o0 += 2 * T
                hf, hfR = A.take("hf%d" % b, o0, [T], F32, nres=NT); o0 += 4 * T
                sets.append((zx, zxR, zg, zgR, xa, xaR, xab, xabR, hf, hfR))
            GRP = 3
            NRG = 2 * GRP
            tA, tAR = A.take("tA", o0, [NRG, 512], F32, nres=NRG); o0 += NRG * 2048
            tB, tBR = A.take("tB", o0, [NRG, 512], F32, nres=NRG); o0 += NRG * 2048
            tM, tMR = A.take("tM", o0, [NRG, 512], F32, nres=NRG); o0 += NRG * 2048
            NR = 3
            tZ, tZR = A.take("tZ", o0, [NR, 512], F32, nres=NR); o0 += NR * 2048
            hbc, (hbcR,) = A.take("hbc", o0, [32], F32); o0 += 128
            NRH = GRP + 2
            tH, tHR = A.take("tH", o0, [NRH, 512], F32, nres=NRH); o0 += NRH * 2048
            tG, tGR = A.take("tG", o0, [NR, 512], F32, nres=NR); o0 += NR * 2048
            tS, tSR = A.take("tS", o0, [NR, 512], F32, nres=NR); o0 += NR * 2048
            tY, tYR = A.take("tY", o0, [NR, 512], BF16, nres=NR); o0 += NR * 1024
            assert o0 <= CB, (o0, CB)
            S.dma("sp", lambda e: e.dma_start(out=id32, in_=ident_d), w=[id32R])
            for j in range(32):
                S.op("dve", lambda e, j=j: e.tensor_scalar(out=dg[:, j, :], in0=id32, scalar1=col(37 + j), scalar2=None,
                                                           op0=ALU.mult), r=[id32R, colsR], w=[dgR])
            S.dma("pool", lambda e: e.dma_start(out=Wr, in_=wr_d.rearrange("g k m -> k g m")), w=[WrR])
            S.dma("pool", lambda e: e.dma_start(out=Wi, in_=wi_d.rearrange("g k m -> k g m")), w=[WiR])
            lam = cols[:, 109:125]
            L = lambda j: lc[:, j, :]
            S.op("dve", lambda e: e.tensor_scalar(out=L(0), in0=lam, scalar1=-1.0, scalar2=None, op0=ALU.mult),
                 r=[colsR], w=[lcR])
            S.op("dve", lambda e: e.tensor_tensor(out=L(0), in0=L(0), in1=lam, op=ALU.max), r=[colsR, lcR], w=[lcR])
            S.op("act", lambda e: e.activation(out=L(1), in_=L(0), func=AF.Exp, scale=-1.0), r=[lcR], w=[lcR])
            S.op("dve", lambda e: e.tensor_scalar(out=L(2), in0=L(1), scalar1=2.0, scalar2=None, op0=ALU.add), r=[lcR], w=[lcR])
            S.op("dve", lambda e: e.reciprocal(out=L(2), in_=L(2)), r=[lcR], w=[lcR])
            S.op("dve", lambda e: e.tensor_tensor(out=L(3), in0=L(1), in1=L(2), op=ALU.mult), r=[lcR], w=[lcR])
            S.op("dve", lambda e: e.tensor_tensor(out=L(4), in0=L(3), in1=L(3), op=ALU.mult), r=[lcR], w=[lcR])
            S.op("dve", lambda e: e.tensor_scalar(out=L(5), in0=L(4), scalar1=1.0 / 13, scalar2=1.0 / 11, op0=ALU.mult,
                                                  op1=ALU.add), r=[lcR], w=[lcR])
            for cst in (1.0 / 9, 1.0 / 7, 1.0 / 5, 1.0 / 3, 1.0):
                S.op("dve", lambda e: e.tensor_tensor(out=L(5), in0=L(5), in1=L(4), op=ALU.mult), r=[lcR], w=[lcR])
                S.op("dve", lambda e, cst=cst: e.tensor_scalar(out=L(5), in0=L(5), scalar1=cst, scalar2=None, op0=ALU.add),
                     r=[lcR], w=[lcR])
            S.op("dve", lambda e: e.tensor_tensor(out=L(5), in0=L(5), in1=L(3), op=ALU.mult), r=[lcR], w=[lcR])
            S.op("dve", lambda e: e.tensor_scalar(out=L(6), in0=lam, scalar1=-1.0, scalar2=0.0, op0=ALU.mult, op1=ALU.max),
                 r=[colsR], w=[lcR])
            S.op("dve", lambda e: e.scalar_tensor_tensor(out=L(6), in0=L(5), scalar=2.0, in1=L(6), op0=ALU.mult, op1=ALU.add),
                 r=[lcR], w=[lcR])
            S.op("dve", lambda e: e.tensor_scalar(out=L(7), in0=L(6), scalar1=-8.0, scalar2=None, op0=ALU.mult),
                 r=[lcR], w=[lcR])
            S.op("dve", lambda e: e.tensor_scalar(out=L(6), in0=L(6), scalar1=-4.0, scalar2=None, op0=ALU.mult),
                 r=[lcR], w=[lcR])
            S.op("dve", lambda e: e.tensor_scalar(out=hbc, in0=cols[:, 77:109], scalar1=0.5, scalar2=None, op0=ALU.mult),
                 r=[colsR], w=[hbcR])
            for b in range(2):
                zx, zxR = sets[b][0], sets[b][1]
                S.op("pool", lambda e, zx=zx: e.memset(zx[:, 0:2], 0.0), w=[zxR])
                S.op("pool", lambda e, zx=zx: e.memset(zx[:, T + 2:T + 4], 0.0), w=[zxR])

            def load(n):
                zx, zxR = sets[n % 2][0:2]
                S.dma("sp", lambda e: e.dma_start(out=zx[:, 2:T + 2], in_=z_s[n * 128:(n + 1) * 128, :]),
                      r=zR[n], w=[zxR])

            GC = 2.0 * math.sqrt(2.0 / math.pi)
            ctr = {"g": 0, "h": 0, "t": 0}

            def conv(n, i):
                zx, zxR, zg, zgR, xa, xaR, xab, xabR, hf, hfR = sets[n % 2]
                sl = slice(i * 512, (i + 1) * 512)
                c0 = i * 512
                ps, pr = PS()
                for k in range(4):
                    S.op("pe", lambda e, k=k: e.matmul(ps[:, :], lhsT=dg[:, n * 4 + k, :], rhs=zx[:, c0 + k:c0 + k + 512],
                                                      start=(k == 0), stop=(k == 3)), r=[dgR, zxR], w=[pr])
                S.op("act", lambda e: e.activation(out=xa[:, sl], in_=ps[:, :], func=AF.Identity, bias=col(69 + n)),
                     r=[pr, colsR], w=[xaR[i]])
                S.op("dve", lambda e: e.tensor_copy(out=xab[:, sl], in_=xa[:, sl]), r=[xaR[i]], w=[xabR[i]])

            def gates_group(n, d, tiles):
                zx, zxR, zg, zgR, xa, xaR, xab, xabR, hf, hfR = sets[n % 2]
                g = d * 8 + n
                slots = []
                pss = []
                for i in tiles:
                    sl = slice(i * 512, (i + 1) * 512)
                    s_ = ctr["g"] % NRG
                    ctr["g"] += 1
                    slots.append(s_)
                    pr_, prR = PS()
                    pi_, piR = PS()
                    pss.append((pr_, prR, pi_, piR))
                    S.op("pe", lambda e, pr_=pr_, sl=sl: e.matmul(pr_[:, :], lhsT=Wr[:, g, :], rhs=xab[:, sl], start=True, stop=True),
                         r=[WrR, xabR[i]], w=[prR])
                    S.op("pe", lambda e, pi_=pi_, sl=sl: e.matmul(pi_[:, :], lhsT=Wi[:, g, :], rhs=xab[:, sl], start=True, stop=True),
                         r=[WiR, xabR[i]], w=[piR])
                for (i, s_, (pr_, prR, pi_, piR)) in zip(tiles, slots, pss):
                    S.op("act", lambda e, pr_=pr_, s_=s_: e.activation(out=tA[:, s_, :], in_=pr_[:, :], func=AF.Tanh, scale=0.5,
                                                                       bias=hbc[:, g:g + 1]), r=[prR, hbcR], w=[tAR[s_]])
                    S.op("act", lambda e, pi_=pi_, s_=s_: e.activation(out=tB[:, s_, :], in_=pi_[:, :], func=AF.Tanh, scale=0.5,
                                                                       bias=hbc[:, 16 + g:17 + g]), r=[piR, hbcR], w=[tBR[s_]])
                for (i, s_) in zip(tiles, slots):
                    S.op("act", lambda e, s_=s_: e.activation(out=tM[:, s_, :], in_=tA[:, s_, :], func=AF.Exp,
                                                             scale=lc[:, 7, g:g + 1], bias=lc[:, 7, g:g + 1]),
                         r=[tAR[s_], lcR], w=[tMR[s_]])
                    S.op("act", lambda e, s_=s_: e.activation(out=tA[:, s_, :], in_=tA[:, s_, :], func=AF.Exp,
                                                             scale=lc[:, 6, g:g + 1], bias=lc[:, 6, g:g + 1]),
                         r=[tAR[s_], lcR], w=[tAR[s_]])
                for (i, s_) in zip(tiles, slots):
                    S.op("act", lambda e, s_=s_: e.activation(out=tM[:, s_, :], in_=tM[:, s_, :], func=AF.Sqrt, scale=-0.25,
                                                             bias=col(132)), r=[tMR[s_], colsR], w=[tMR[s_]])
                for (i, s_) in zip(tiles, slots):
                    sl = slice(i * 512, (i + 1) * 512)
                    S.op("dve", lambda e, s_=s_, sl=sl: e.tensor_tensor(out=tM[:, s_, :], in0=tM[:, s_, :], in1=xa[:, sl], op=ALU.mult),
                         r=[tMR[s_], xaR[i]], w=[tMR[s_]])
                    S.op("dve", lambda e, s_=s_: e.scalar_tensor_tensor(out=tB[:, s_, :], in0=tB[:, s_, :], scalar=1.0, in1=tM[:, s_, :],
                                                                       op0=ALU.add, op1=ALU.mult),
                         r=[tBR[s_], tMR[s_]], w=[tBR[s_]])
                return slots

            def scan_f(n, i, s_):
                hf, hfR = sets[n % 2][8], sets[n % 2][9]
                c0 = i * 512
                init = 0.0 if i == 0 else hf[:, c0 - 1:c0]
                rr = [tAR[s_], tBR[s_]] + ([hfR[i - 1]] if i > 0 else [])
                S.op("dve", lambda e: e.tensor_tensor_scan(out=hf[:, c0:c0 + 512], data0=tA[:, s_, :], data1=tB[:, s_, :],
                                                           initial=init, op0=ALU.mult, op1=ALU.add), r=rr, w=[hfR[i]])

            def scan_b(n, i, s_, prev_h):
                hs = ctr["h"] % NRH
                ctr["h"] += 1
                init = 0.0 if prev_h is None else tH[:, prev_h, 0:1]
                rr = [tAR[s_], tBR[s_]] + ([tHR[prev_h]] if prev_h is not None else [])
                S.op("dve", lambda e: e.tensor_tensor_scan(out=tH[:, hs, ::-1], data0=tA[:, s_, ::-1], data1=tB[:, s_, ::-1],
                                                           initial=init, op0=ALU.mult, op1=ALU.add), r=rr, w=[tHR[hs]])
                return hs

            def tail(n, i, hs):
                zx, zxR, zg, zgR, xa, xaR, xab, xabR, hf, hfR = sets[n % 2]
                sl = slice(i * 512, (i + 1) * 512)
                t_ = ctr["t"] % NR
                ctr["t"] += 1
                S.dma("sp", lambda e: e.dma_start(out=tZ[:, t_, :], in_=z_s[(8 + n) * 128:(9 + n) * 128, sl]),
                      r=[zR[8 + n][i]], w=[tZR[t_]])
                S.op("dve", lambda e: e.tensor_tensor(out=tG[:, t_, :], in0=hf[:, sl], in1=tH[:, hs, :], op=ALU.add),
                     r=[hfR[i], tHR[hs]], w=[tGR[t_]])
                S.op("act", lambda e: e.activation(out=tS[:, t_, :], in_=tZ[:, t_, :], func=AF.Square,
                                                   scale=math.sqrt(0.044715)), r=[tZR[t_]], w=[tSR[t_]])
                S.op("dve", lambda e: e.scalar_tensor_tensor(out=tS[:, t_, :], in0=tS[:, t_, :], scalar=1.0, in1=tZ[:, t_, :],
                                                             op0=ALU.add, op1=ALU.mult), r=[tSR[t_], tZR[t_]], w=[tSR[t_]])
                S.op("act", lambda e: e.activation(out=tS[:, t_, :], in_=tS[:, t_, :], func=AF.Tanh, scale=0.5 * GC),
                     r=[tSR[t_]], w=[tSR[t_]])
                S.op("dve", lambda e: e.scalar_tensor_tensor(out=tG[:, t_, :], in0=tG[:, t_, :], scalar=0.5, in1=tZ[:, t_, :],
                                                             op0=ALU.mult, op1=ALU.mult), r=[tGR[t_], tZR[t_]], w=[tGR[t_]])
                S.op("dve", lambda e: e.scalar_tensor_tensor(out=tY[:, t_, :], in0=tS[:, t_, :], scalar=1.0, in1=tG[:, t_, :],
                                                             op0=ALU.add, op1=ALU.mult), r=[tSR[t_], tGR[t_]], w=[tYR[t_]])
                S.dma("sp", lambda e: e.dma_start(out=ya_s[n * 128:(n + 1) * 128, sl], in_=tY[:, t_, :]),
                      r=[tYR[t_]], w=[yaR[n][i]])

            def chunk(n):
                for i in range(NT):
                    conv(n, i)
                groups = [list(range(a, min(a + GRP, NT))) for a in range(0, NT, GRP)]
                for tiles in groups:
                    slots = gates_group(n, 0, tiles)
                    for i, s_ in zip(tiles, slots):
                        scan_f(n, i, s_)
                prev_h = None
                for tiles in reversed(groups):
                    tiles = tiles[::-1]
                    slots = gates_group(n, 1, tiles)
                    hss = []
                    for i, s_ in zip(tiles, slots):
                        prev_h = scan_b(n, i, s_, prev_h)
                        hss.append((i, prev_h))
                    for (ti, th) in hss:
                        tail(n, ti, th)

            load(0)
            for n in range(8):
                if n + 1 < 8:
                    load(n + 1)
                chunk(n)

        phase_c()

        def phase_e1():
            Wga, WgaR = A.take("Wga", 0, [8, 2048], BF16, nres=8)
            Wlo, WloR = A.take("Wlo", 32768, [8, D], BF16, nres=8)
            Wmo, WmoR = A.take("Wmo", 49152, [8, D], BF16, nres=8)
            Wo, WoR = A.take("Wo", 65536, [8, D], BF16, nres=8)
            o0 = 81920
            NSL = 8
            xt, xtR = A.take("xt", o0, [NSL, D], F32, nres=NSL); o0 += NSL * 4096
            xs, xsR = A.take("xs", o0, [4, D], BF16, nres=4); o0 += 8192
            uT2, uT2R = A.take("uT", o0, [2, 8, 512], BF16, nres=16); o0 += 16384
            yat, yatR = A.take("yat", o0, [2, 8, 512], BF16, nres=2); o0 += 16384
            ott, ottR = A.take("ott", o0, [2, 8, 512], BF16, nres=2); o0 += 16384
            mT, mTR = A.take("mT", o0, [8, 512], BF16, nres=8); o0 += 8192
            sgt, sgtR = A.take("sgt", o0, [4, 512], F32, nres=4); o0 += 8192
            assert o0 <= CB
            win_v = win_d.rearrange("(k p) n -> p k n", p=128)
            wlo_v = wlo_d.rearrange("(k p) n -> p k n", p=128)
            wmo_v = wmo_d.rearrange("(k p) n -> p k n", p=128)
            for blk in range(4):
                lo, hi = blk * 256, blk * 256 + 256
                S.dma("pool", lambda e, lo=lo, hi=hi: e.dma_start(out=Wga[:, :, lo:hi], in_=win_v[:, :, 2752 + lo:2752 + hi]),
                      w=[WgaR[blk]])
                S.dma("pool", lambda e, lo=lo, hi=hi: e.dma_start(out=Wga[:, :, 1024 + lo:1024 + hi],
                                                                 in_=win_v[:, :, 3776 + lo:3776 + hi]), w=[WgaR[4 + blk]])
                S.dma("pool", lambda e, lo=lo, hi=hi: e.dma_start(out=Wlo[:, :, lo:hi], in_=wlo_v[:, :, lo:hi]), w=[WloR[blk]])
                S.dma("pool", lambda e, lo=lo, hi=hi: e.dma_start(out=Wmo[:, :, lo:hi], in_=wmo_v[:, :, lo:hi]), w=[WmoR[blk]])
            load_w(Wo, WoR, wo_d, 0, D, 8)
            ya_v = ya_s.rearrange("(k p) t -> p k t", p=128)
            o_v = o_s.rearrange("(k p) t -> p k t", p=128)
            slot = [0]

            def load_tile(i):
                sls = []
                for s in range(4):
                    sl = slot[0] % NSL
                    slot[0] += 1
                    r0 = i * 512 + s * 128
                    S.dma("sp", lambda e, sl=sl, r0=r0: e.dma_start(out=xt[:, sl, :], in_=h_s[r0:r0 + 128, :]),
                          r=[hR[i * 4 + s]], w=[xtR[sl]])
                    sls.append(sl)
                b = i % 2
                S.dma("sp", lambda e: e.dma_start(out=yat[:, b], in_=ya_v[:, :, i * 512:(i + 1) * 512]),
                      r=[yaR[c][i] for c in range(8)], w=[yatR[b]])
                S.dma("sp", lambda e: e.dma_start(out=ott[:, b], in_=o_v[:, :, i * 512:(i + 1) * 512]),
                      r=[oR[h][i] for h in range(NH)], w=[ottR[b]])
                return sls

            def front_a(i, sls):
                for s in range(4):
                    rms_T(xt, xtR, sls[s], s, 8, xs, xsR)

            def front_b(i):
                transposes(xs, xsR, 8, 8, uT2[:, i % 2], uT2R[(i % 2) * 8:(i % 2) * 8 + 8])

            sc = [0]

            def body(i, sls, part):
                b = i % 2
                uT = uT2[:, i % 2]
                uTR = uT2R[(i % 2) * 8:(i % 2) * 8 + 8]
                if part == 1:
                    body1(i, sls, b, uT, uTR)
                else:
                    body2(i, sls)

            def body1(i, sls, b, uT, uTR):
                for c in range(8):
                    pA, pAR = PS(); pB, pBR = PS(); pYA, pYAR = PS(); pYB, pYBR = PS()
                    for (pp, ppR, W, WR, cc, src, srcR) in (
                            (pA, pAR, Wga, WgaR, c * 128, None, None), (pB, pBR, Wga, WgaR, 1024 + c * 128, None, None),
                            (pYA, pYAR, Wlo, WloR, c * 128, yat, yatR), (pYB, pYBR, Wmo, WmoR, c * 128, ott, ottR)):
                        for k in range(8):
                            if src is None:
                                rhs = uT[:, k, :]; rr = uTR[k]
                            else:
                                rhs = src[:, b, k, :]; rr = srcR[b]
                            S.op("pe", lambda e, k=k, pp=pp, W=W, cc=cc, rhs=rhs: e.matmul(
                                pp[:, :], lhsT=W[:, k, cc:cc + 128], rhs=rhs, start=(k == 0), stop=(k == 7)),
                                 r=[WR[cc // 256], rr], w=[ppR])
                    t0 = sc[0] % 2 * 2
                    sc[0] += 1
                    S.op("act", lambda e, pA=pA, t0=t0: e.activation(out=sgt[:, t0, :], in_=pA[:, :], func=AF.Sigmoid),
                         r=[pAR], w=[sgtR[t0]])
                    S.op("act", lambda e, pB=pB, t0=t0: e.activation(out=sgt[:, t0 + 1, :], in_=pB[:, :], func=AF.Sigmoid),
                         r=[pBR], w=[sgtR[t0 + 1]])
                    S.op("dve", lambda e, pYA=pYA, t0=t0: e.tensor_tensor(out=sgt[:, t0, :], in0=pYA[:, :], in1=sgt[:, t0, :],
                                                                         op=ALU.mult), r=[pYAR, sgtR[t0]], w=[sgtR[t0]])
                    S.op("dve", lambda e, pYB=pYB, t0=t0: e.tensor_tensor(out=sgt[:, t0 + 1, :], in0=pYB[:, :],
                                                                         in1=sgt[:, t0 + 1, :], op=ALU.mult),
                         r=[pYBR, sgtR[t0 + 1]], w=[sgtR[t0 + 1]])
                    S.op("pool", lambda e, c=c, t0=t0: e.tensor_tensor(out=mT[:, c, :], in0=sgt[:, t0, :], in1=sgt[:, t0 + 1, :],
                                                                      op=ALU.add), r=[sgtR[t0], sgtR[t0 + 1]], w=[mTR[c]])

            def body2(i, sls):
                for s in range(4):
                    sl = sls[s]
                    for c2 in range(2):
                        pd, pdR = PS()
                        for k in range(8):
                            S.op("pe", lambda e, k=k, s=s, c2=c2, pd=pd: e.matmul(
                                pd[:, :], lhsT=mT[:, k, s * 128:(s + 1) * 128], rhs=Wo[:, k, c2 * 512:(c2 + 1) * 512],
                                start=(k == 0), stop=(k == 7)), r=[mTR[k], WoR[k]], w=[pdR])
                        S.op("dve", lambda e, sl=sl, c2=c2, pd=pd: e.tensor_tensor(
                            out=xt[:, sl, c2 * 512:(c2 + 1) * 512], in0=pd[:, :], in1=xt[:, sl, c2 * 512:(c2 + 1) * 512],
                            op=ALU.add), r=[pdR, xtR[sl]], w=[xtR[sl]])
                    r0 = i * 512 + s * 128
                    S.dma("sp", lambda e, sl=sl, r0=r0: e.dma_start(out=h_s[r0:r0 + 128, :], in_=xt[:, sl, :]),
                          r=[xtR[sl]], w=[hR[i * 4 + s]])

            sl_cur = load_tile(0)
            front_a(0, sl_cur)
            front_b(0)
            for i in range(NT):
                cur = sl_cur
                if i + 1 < NT:
                    sl_cur = load_tile(i + 1)
                    front_a(i + 1, sl_cur)
                body(i, cur, 1)
                if i + 1 < NT:
                    front_b(i + 1)
                body(i, cur, 2)

        phase_e1()

        ffn_phase(f2g_d, f2u_d, f2d_d, 16, h_s, hR, h_s, hR)

        def phase_e3():
            Wpg, WpgR = A.take("Wpg", 0, [8, D], BF16, nres=8)
            Wpp, WppR = A.take("Wpp", 16384, [2, D], BF16, nres=2)
            bc, (bcR,) = A.take("bc", 20480, [2 * D], F32)
            o0 = 28672
            NSL = 8
            xt, xtR = A.take("xt", o0, [NSL, D], F32, nres=NSL); o0 += NSL * 4096
            xs, xsR = A.take("xs", o0, [4, D], BF16, nres=4); o0 += 8192
            hT, hTR = A.take("hT", o0, [8, 512], BF16, nres=8); o0 += 8192
            gg, ggR = A.take("gg", o0, [4, D], F32, nres=4); o0 += 16384
            pin, pinR = A.take("pin", o0, [2, 4, PLE], F32, nres=2); o0 += 8192
            pbs, pbsR = A.take("pbs", o0, [4, PLE], BF16, nres=4); o0 += 2048
            pT, pTR = A.take("pT", o0, [2, 512], BF16, nres=2); o0 += 2048
            t1, t1R = A.take("t1", o0, [2, D], F32, nres=2); o0 += 8192
            outt, outtR = A.take("outt", o0, [2, D], F32, nres=2); o0 += 8192
            load_w(Wpg, WpgR, wpg_d, 0, D, 8)
            load_w(Wpp, WppR, wpp_d, 0, D, 2)
            S.dma("sp", lambda e: e.dma_start(out=bc, in_=bc_d), w=[bcR])
            p_v = p_d.rearrange("(n s p) c -> n p s c", s=4, p=128)
            slot = [0]

            def load_tile(i):
                sls = []
                for s in range(4):
                    sl = slot[0] % NSL
                    slot[0] += 1
                    r0 = i * 512 + s * 128
                    S.dma("sp", lambda e, sl=sl, r0=r0: e.dma_start(out=xt[:, sl, :], in_=h_s[r0:r0 + 128, :]),
                          r=[hR[i * 4 + s]], w=[xtR[sl]])
                    sls.append(sl)
                b = i % 2
                S.dma("sp", lambda e: e.dma_start(out=pin[:, b], in_=p_v[i]), w=[pinR[b]])
                return sls

            def front(i, sls):
                b = i % 2
                for s in range(4):
                    rms_T(xt, xtR, sls[s], s, 24, xs, xsR, mode="pow")
                transposes(xs, xsR, 8, 24, hT, hTR)
                for s in range(4):
                    S.op("act", lambda e, s=s: e.activation(out=pbs[:, s, :], in_=pin[:, b, s, :], func=AF.Copy),
                         r=[pinR[b]], w=[pbsR[s]])
                for k in range(2):
                    ps, pr = PS()
                    for s in range(4):
                        S.op("pe", lambda e, s=s, k=k, ps=ps: e.matmul(
                            ps[:, s * 128:(s + 1) * 128], lhsT=pbs[:, s, k * 128:(k + 1) * 128], rhs=ident,
                            start=True, stop=True), r=[pbsR[s], identR], w=[pr])
                    S.op("dve", lambda e, k=k, ps=ps: e.tensor_copy(out=pT[:, k, :], in_=ps[:, :]), r=[pr], w=[pTR[k]])

            oc = [0]

            pend = []

            def body(i, sls):
                for s in range(4):
                    ctx = body_a(i, sls, s)
                    if pend:
                        body_b(*pend.pop(0))
                    pend.append(ctx)

            def body_a(i, sls, s):
                if True:
                    sl = sls[s]
                    pps = []
                    for c2 in range(2):
                        pg, pgR = PS()
                        for k in range(8):
                            S.op("pe", lambda e, k=k, c2=c2, pg=pg: e.matmul(
                                pg[:, :], lhsT=hT[:, k, s * 128:(s + 1) * 128], rhs=Wpg[:, k, c2 * 512:(c2 + 1) * 512],
                                start=(k == 0), stop=(k == 7)), r=[hTR[k], WpgR[k]], w=[pgR])
                        S.op("act", lambda e, c2=c2, pg=pg: e.activation(out=gg[:, s, c2 * 512:(c2 + 1) * 512], in_=pg[:, :],
                                                                       func=AF.Sigmoid), r=[pgR], w=[ggR[s]])
                        pp, ppR = PS()
                        for k in range(2):
                            S.op("pe", lambda e, k=k, c2=c2, pp=pp: e.matmul(
                                pp[:, :], lhsT=pT[:, k, s * 128:(s + 1) * 128], rhs=Wpp[:, k, c2 * 512:(c2 + 1) * 512],
                                start=(k == 0), stop=(k == 1)), r=[pTR[k], WppR[k]], w=[ppR])
                        pps.append((pp, ppR))
                    tb = oc[0] % 2
                    oc[0] += 1
                    c_a = stat[:, 48 + s:49 + s]; c_b = stat[:, 52 + s:53 + s]; c_r = stat[:, 56 + s:57 + s]
                    for c2 in range(2):
                        pp, ppR = pps[c2]
                        S.op("act", lambda e, pp=pp, c2=c2, cc=(c_a if c2 == 0 else c_b): e.activation(
                            out=t1[:, tb, c2 * 512:(c2 + 1) * 512], in_=pp[:, :], func=AF.Square, accum_out=cc),
                             r=[ppR], w=[t1R[tb], statR[s]])
                    S.op("dve", lambda e: e.tensor_tensor(out=c_a, in0=c_a, in1=c_b, op=ALU.add), r=[statR[s]], w=[statR[s]])
                    rstd_op(c_a, c_b, c_r, D, statR[s], "pow")
                    return (i, s, sl, pps, tb, c_r)

            def body_b(i, s, sl, pps, tb, c_r):
                if True:
                    for c2 in range(2):
                        pp, ppR = pps[c2]
                        S.op("dve", lambda e, pp=pp, c2=c2: e.scalar_tensor_tensor(
                            out=t1[:, tb, c2 * 512:(c2 + 1) * 512], in0=pp[:, :], scalar=c_r,
                            in1=bc[:, c2 * 512:(c2 + 1) * 512], op0=ALU.mult, op1=ALU.mult),
                             r=[ppR, statR[s], bcR], w=[t1R[tb]])
                    S.op("dve", lambda e: e.tensor_tensor(out=t1[:, tb, :], in0=t1[:, tb, :], in1=gg[:, s, :], op=ALU.mult),
                         r=[t1R[tb], ggR[s]], w=[t1R[tb]])
                    S.op("dve", lambda e: e.tensor_tensor(out=xt[:, sl, :], in0=xt[:, sl, :], in1=t1[:, tb, :], op=ALU.add),
                         r=[t1R[tb], xtR[sl]], w=[xtR[sl]])
                    f_s = stat[:, 4 + s:5 + s]; f_d = stat[:, 12 + s:13 + s]; f_r = stat[:, 20 + s:21 + s]
                    S.op("act", lambda e: e.activation(out=outt[:, tb, :], in_=xt[:, sl, :], func=AF.Square, accum_out=f_s),
                         r=[xtR[sl]], w=[outtR[tb], statR[4 + s]])
                    rstd_op(f_s, f_d, f_r, D, statR[4 + s], "pow")
                    S.op("dve", lambda e: e.scalar_tensor_tensor(out=outt[:, tb, :], in0=xt[:, sl, :], scalar=f_r,
                                                                 in1=bc[:, D:2 * D], op0=ALU.mult, op1=ALU.mult),
                         r=[xtR[sl], statR[4 + s], bcR], w=[outtR[tb]])
                    r0 = i * 512 + s * 128
                    S.dma("sp", lambda e, r0=r0: e.dma_start(out=out_d[r0:r0 + 128, :], in_=outt[:, tb, :]),
                          r=[outtR[tb]], w=[])

            sl_cur = load_tile(0)
            front(0, sl_cur)
            for i in range(NT):
                cur = sl_cur
                while pend:
                    body_b(*pend.pop(0))
                if i + 1 < NT:
                    sl_cur = load_tile(i + 1)
                body(i, cur)
                while pend:
                    body_b(*pend.pop(0))
                if i + 1 < NT:
                    front(i + 1, sl_cur)
            while pend:
                body_b(*pend.pop(0))

        phase_e3()

        S.finish()
    return nc


def _cols(inp):
    c = np.zeros((128, NCOL), np.float32)

    def chunks(v):
        v = np.asarray(v, np.float32).reshape(-1, 128)
        return v.T

    c[:, 0:8] = chunks(inp["ffn1_norm"][0])
    c[:, 8:16] = chunks(inp["mix_norm"][0])
    c[:, 16:24] = chunks(inp["ffn2_norm"][0])
    c[:, 24:32] = chunks(inp["ple_norm"][0])
    c[:, 32:35] = chunks(inp["q_norm"][0])
    c[:, 35:37] = chunks(inp["kv_norm"][0])
    cw = np.asarray(inp["conv_w"][0], np.float32)
    c[:, 37:69] = cw.reshape(4, 8, 128).transpose(2, 1, 0).reshape(128, 32)
    c[:, 69:77] = chunks(inp["conv_b"][0])
    c[:, 77:93] = chunks(np.asarray(inp["lru_b_r"][0]).reshape(-1))
    c[:, 93:109] = chunks(np.asarray(inp["lru_b_i"][0]).reshape(-1))
    c[:, 109:125] = chunks(np.asarray(inp["lru_lambda"][0]).reshape(-1))
    invf64 = 10000.0 ** (-(np.arange(32, dtype=np.float64)) / 32.0)
    c64 = np.concatenate([invf64, invf64]) / TWO_PI
    c_hi = c64.astype(np.float32)
    c[0:64, 125] = c_hi
    c[0:64, 134] = (c64 - c_hi.astype(np.float64)).astype(np.float32)
    c[:, 127] = EPS
    c[0:32, 128] = -TWO_PI
    c[32:64, 128] = TWO_PI
    c[0:32, 129] = math.pi
    c[32:64, 129] = -math.pi
    c[:, 130] = -math.pi
    c[:, 131] = 1.0
    c[:, 132] = 0.25
    c[:, 133] = -0.5
    return c


def make_in_maps(inp, T, ncores):
    shared = {
        "cols": _cols(inp),
        "ident": np.eye(128, dtype=np.float32),
        "bc": np.ascontiguousarray(np.broadcast_to(
            np.concatenate([np.asarray(inp["ple_proj_norm"][0], np.float32),
                            np.asarray(inp["final_norm"], np.float32)])[None, :], (128, 2 * D))),
    }
    for k in ("ffn1_w_gate", "ffn1_w_up", "ffn1_w_down", "ffn2_w_gate", "ffn2_w_up", "ffn2_w_down", "w_in",
              "w_lru_out", "w_uq", "w_ukv", "w_mla_out", "w_o", "ple_w_gate", "ple_w_proj"):
        shared[k] = np.ascontiguousarray(np.asarray(inp[k], np.float32)[0])
    shared["lru_w_r"] = np.ascontiguousarray(np.asarray(inp["lru_w_r"], np.float32)[0].reshape(16, 128, 128))
    shared["lru_w_i"] = np.ascontiguousarray(np.asarray(inp["lru_w_i"], np.float32)[0].reshape(16, 128, 128))
    maps = []
    for b in range(ncores):
        m = dict(shared)
        m["x"] = np.ascontiguousarray(np.asarray(inp["x"], np.float32)[b, :T])
        m["p"] = np.ascontiguousarray(np.asarray(inp["p"], np.float32)[0, b, :T])
        m["pos"] = np.ascontiguousarray(np.broadcast_to(np.asarray(inp["positions"], np.int32)[b, :T].reshape(1, T), (64, T)))
        maps.append(m)
    return maps


_NC_CACHE = {}


def kernel(**inputs):
    T = 4096
    n = 8
    if T not in _NC_CACHE:
        _NC_CACHE[T] = build(T)
    nc = _NC_CACHE[T]
    maps = make_in_maps(inputs, T, n)
    res = run_bass_kernel_spmd(nc, maps, core_ids=list(range(n)))
    return np.stack([np.asarray(r["out"], np.float32) for r in res.results], axis=0)
```

```python
import contextlib
import math
import numpy as np
import concourse.bass as bass
import concourse.mybir as mybir
from concourse.bass_utils import run_bass_kernel_spmd

F32 = mybir.dt.float32
BF16 = mybir.dt.bfloat16
I32 = mybir.dt.int32
U8 = mybir.dt.uint8
AF = mybir.ActivationFunctionType
ALU = mybir.AluOpType

D = 1024
DFF = 2816
NJ = DFF // 128
PLE = 256
NH = 8
QL = 384
KVL = 256
EPS = 1e-6
NCOL = 136
SM_SCALE = 192.0 ** -0.5
TWO_PI = 2.0 * math.pi
RSTD_MODE = "sqrt"
PE_EVERY = 6
DEN_MODE = "hybrid"

ENGS = ("pe", "act", "dve", "pool", "sp")
DMA_K = 8
SAME_ENGINE_RAW_ONLY = False
DMA_KQ = {}


class R:
    __slots__ = ("name", "w", "rs")

    def __init__(self, name=""):
        self.name = name
        self.w = None
        self.rs = {}


class Op:
    __slots__ = ("eng", "fn", "deps", "dma", "sig", "val", "sem", "n")


class Sched:
    def __init__(self, nc, es):
        self.nc = nc
        self.es = es
        self.ops = {e: [] for e in ENGS}
        self.dma_hist = {e: [] for e in ENGS}
        self.nops = 0

    def _rec(self, eng, fn, r, w, dma):
        o = Op()
        o.eng = eng; o.fn = fn; o.dma = dma; o.sig = False; o.val = 0; o.sem = None
        o.n = self.nops; self.nops += 1
        deps = []
        for x in r:
            if x.w is not None:
                deps.append(x.w)
        for x in w:
            if x.w is not None:
                deps.append(x.w)
            deps.extend(x.rs.values())
        fdeps = []
        seen = set()
        for p in deps:
            if p is o or id(p) in seen:
                continue
            seen.add(id(p))
            if (not p.dma) and (not dma) and p.eng == eng:
                if eng == "pe":
                    continue
                if SAME_ENGINE_RAW_ONLY and not any(x.w is p for x in r):
                    continue
            fdeps.append(p)
        o.deps = fdeps
        for x in r:
            x.rs[("dma", o.n) if dma else eng] = o
        for x in w:
            x.w = o
            x.rs = {}
        if dma:
            h = self.dma_hist[eng]
            kk = DMA_KQ.get(eng, DMA_K)
            if len(h) >= kk:
                o.deps.append(h[-kk])
            h.append(o)
        self.ops[eng].append(o)
        return o

    def op(self, eng, fn, r=(), w=()):
        return self._rec(eng, fn, r, w, False)

    def dma(self, eng, fn, r=(), w=()):
        return self._rec(eng, fn, r, w, True)

    def finish(self):
        nc = self.nc
        for e in ENGS:
            for o in self.ops[e]:
                for p in o.deps:
                    p.sig = True
        sems = {e: self.es.enter_context(nc.semaphore("s_" + e)) for e in ENGS}
        dsems = {}
        for e in ENGS:
            if self.dma_hist[e]:
                dsems[e] = [self.es.enter_context(nc.semaphore("d_%s%d" % (e, i))) for i in range(DMA_K)]
        for e in ENGS:
            c = 0
            nd = 0
            for o in self.ops[e]:
                if o.dma:
                    o.sem = dsems[e][nd % DMA_K]
                    o.val = 16 * (nd // DMA_K + 1)
                    nd += 1
                elif o.sig:
                    c += 1
                    o.val = c
                    o.sem = sems[e]
        block = self.es.enter_context(nc.Block())

        def emit(name, engine):
            waited = {}
            for o in self.ops[name]:
                for p in o.deps:
                    k = id(p.sem)
                    if waited.get(k, 0) >= p.val:
                        continue
                    waited[k] = p.val
                    engine.wait_ge(p.sem, p.val)
                inst = o.fn(engine)
                if o.dma:
                    inst.then_inc(o.sem, 16)
                elif o.sig:
                    inst.then_inc(o.sem, 1)

        @block.tensor
        def _(eng):
            emit("pe", eng)

        @block.scalar
        def _(eng):
            emit("act", eng)

        @block.vector
        def _(eng):
            emit("dve", eng)

        @block.gpsimd
        def _(eng):
            emit("pool", eng)

        @block.sync
        def _(eng):
            emit("sp", eng)
            for e in ENGS:
                for o in self.dma_hist[e][-DMA_K:]:
                    eng.wait_ge(o.sem, o.val)


class Arena:
    def __init__(self, nc, es, cap):
        self.t = es.enter_context(nc.sbuf_tensor("arena", [128, cap], U8))
        self.cap = cap
        self.live = []

    def take(self, name, off, fshape, dt, nres=1):
        esz = 2 if dt == BF16 else 4
        size = int(np.prod(fshape)) * esz
        assert off % 4 == 0 and off + size <= self.cap, (name, off, size, self.cap)
        inh = []
        seen = set()
        for (o, s, rl) in self.live:
            if o < off + size and off < o + s:
                for r in rl:
                    for q in ([r.w] if r.w is not None else []) + list(r.rs.values()):
                        if id(q) not in seen:
                            seen.add(id(q))
                            inh.append(q)
        rl = [R("%s%d" % (name, i)) for i in range(nres)]
        for r in rl:
            for i, o in enumerate(inh):
                r.rs[("inh", i)] = o
        self.live.append((off, size, rl))
        ap = self.t[:, off:off + size].bitcast(dt)
        if len(fshape) == 2:
            ap = ap.rearrange("p (a b) -> p a b", a=fshape[0])
        elif len(fshape) == 3:
            ap = ap.rearrange("p (a b c) -> p a b c", a=fshape[0], b=fshape[1])
        return ap, rl


def build(T, dbg=False):
    NT = T // 512
    NS = T // 128
    nc = bass.Bass("TRN2", target_bir_lowering=False)

    def din(name, shape, dt=F32):
        return nc.dram_tensor(name, shape, dt, kind="ExternalInput").ap()

    skind = "ExternalOutput" if dbg else "Internal"

    def dscr(name, shape, dt=F32):
        return nc.dram_tensor(name, shape, dt, kind=skind).ap()

    x_d = din("x", [T, D]); p_d = din("p", [T, PLE]); pos_d = din("pos", [64, T], I32)
    cols_d = din("cols", [128, NCOL]); ident_d = din("ident", [128, 128]); bc_d = din("bc", [128, 2 * D])
    f1g_d = din("ffn1_w_gate", [D, DFF]); f1u_d = din("ffn1_w_up", [D, DFF]); f1d_d = din("ffn1_w_down", [DFF, D])
    f2g_d = din("ffn2_w_gate", [D, DFF]); f2u_d = din("ffn2_w_up", [D, DFF]); f2d_d = din("ffn2_w_down", [DFF, D])
    win_d = din("w_in", [D, 4800])
    wr_d = din("lru_w_r", [16, 128, 128]); wi_d = din("lru_w_i", [16, 128, 128])
    wlo_d = din("w_lru_out", [D, D]); wuq_d = din("w_uq", [QL, 1536]); wukv_d = din("w_ukv", [KVL, 2048])
    wmo_d = din("w_mla_out", [D, D]); wo_d = din("w_o", [D, D])
    wpg_d = din("ple_w_gate", [D, D]); wpp_d = din("ple_w_proj", [PLE, D])
    out_d = nc.dram_tensor("out", [T, D], F32, kind="ExternalOutput").ap()
    h_s = dscr("h_s", [T, D])
    z_s = dscr("z_s", [2 * D, T])
    o_s = dscr("o_s", [D, T], BF16)
    ya_s = dscr("ya_s", [D, T], BF16)
    rope_s = dscr("rope_s", [2, 64, T])

    es = contextlib.ExitStack()
    with es:
        S = Sched(nc, es)
        CAP = 212000
        A = Arena(nc, es, CAP)
        pst = [es.enter_context(nc.psum_tensor("ps%d" % i, [128, 512], F32)) for i in range(8)]
        psR = [R("ps%d" % i) for i in range(8)]
        ps_ctr = {}

        def PS(lo=0, hi=8):
            c = ps_ctr.get((lo, hi), 0)
            ps_ctr[(lo, hi)] = c + 1
            i = lo + c % (hi - lo)
            return pst[i], psR[i]

        hR = [R("h%d" % i) for i in range(NT * 4)]
        zR = [[R("z%d_%d" % (c, i)) for i in range(NT)] for c in range(16)]
        oR = [[R("o%d_%d" % (h, i)) for i in range(NT)] for h in range(NH)]
        yaR = [[R("ya%d_%d" % (c, i)) for i in range(NT)] for c in range(8)]

        CB = CAP - 2304
        cols, (colsR,) = A.take("cols", CB, [NCOL], F32)
        ident, (identR,) = A.take("ident", CB + 544, [128], BF16)
        ones, (onesR,) = A.take("ones", CB + 800, [128], BF16)
        stat, statR = A.take("stat", CB + 1056, [64], F32, nres=8)
        S.dma("sp", lambda e: e.dma_start(out=cols, in_=cols_d), w=[colsR])
        S.dma("pool", lambda e: e.dma_start(out=ident, in_=ident_d), w=[identR])
        S.op("dve", lambda e: e.memset(ones, 1.0), w=[onesR])
        ones32, (ones32R,) = A.take("ones32", CB + 1312, [128], F32)
        S.op("dve", lambda e: e.memset(ones32, 1.0), w=[ones32R])

        def col(i, n=128):
            return cols[0:n, i:i + 1]

        def load_w(dst, dstR, src_rows, ncols_lo, ncols_hi, nk):
            for k in range(nk):
                S.dma("pool", lambda e, k=k: e.dma_start(
                    out=dst[:, k, :], in_=src_rows[k * 128:(k + 1) * 128, ncols_lo:ncols_hi]), w=[dstR[k]])

        def rstd_op(ssc, sdc, rsc, width, rr, mode=None):
            if (mode or RSTD_MODE) == "pow":
                S.op("dve", lambda e: e.tensor_scalar(out=sdc, in0=ssc, scalar1=1.0 / width, scalar2=EPS, op0=ALU.mult,
                                                      op1=ALU.add), r=[rr], w=[rr])
                S.op("pool", lambda e: e.tensor_tensor(out=rsc, in0=sdc, in1=col(133), op=ALU.pow), r=[rr, colsR], w=[rr])
            else:
                S.op("act", lambda e: e.activation(out=sdc, in_=ssc, func=AF.Sqrt, scale=1.0 / width, bias=col(127)),
                     r=[rr, colsR], w=[rr])
                S.op("dve", lambda e: e.reciprocal(out=rsc, in_=sdc), r=[rr], w=[rr])

        def rms_T(xt, xtR, sl, s, gcol0, xs, xsR, width=D, mode=None):
            xin = xt[:, sl, :]
            ssc = stat[:, s:s + 1]
            sdc = stat[:, 8 + s:9 + s]
            rsc = stat[:, 16 + s:17 + s]
            S.op("act", lambda e: e.activation(out=xs[:, s, 0:width], in_=xin, func=AF.Square, accum_out=ssc),
                 r=[xtR[sl]], w=[xsR[s], statR[s]])
            rstd_op(ssc, sdc, rsc, width, statR[s], mode)
            S.op("dve", lambda e: e.tensor_scalar(out=xs[:, s, 0:width], in0=xin, scalar1=rsc, scalar2=None,
                                                  op0=ALU.mult), r=[xtR[sl], statR[s]], w=[xsR[s]])

        def transposes(xs, xsR, nk, gcol0, dstT, dstTR, evac_engs=("act", "dve"), c0=0):
            if c0 or True:
                dstT = dstT[:, :, c0:c0 + 512]
            for k in range(nk):
                ps, pr = PS()
                for s in range(4):
                    S.op("pe", lambda e, s=s, k=k, ps=ps: e.matmul(
                        ps[:, s * 128:(s + 1) * 128], lhsT=xs[:, s, k * 128:(k + 1) * 128], rhs=ident,
                        start=True, stop=True), r=[xsR[s], identR], w=[pr])
                eng = evac_engs[k % len(evac_engs)]
                g = col(gcol0 + k)
                if eng == "act":
                    S.op("act", lambda e, k=k, ps=ps, g=g: e.activation(out=dstT[:, k, :], in_=ps[:, :],
                                                                      func=AF.Copy, scale=g),
                         r=[pr, colsR], w=[dstTR[k]])
                else:
                    S.op("dve", lambda e, k=k, ps=ps, g=g: e.tensor_scalar(out=dstT[:, k, :], in0=ps[:, :], scalar1=g,
                                                                         scalar2=None, op0=ALU.mult),
                         r=[pr, colsR], w=[dstTR[k]])

        def ffn_phase(wg_d, wu_d, wd_d, gcol0, src_d, srcR, dst_d, dstR, pre_hook=None):
            Wg, WgR = A.take("Wg", 0, [8, DFF], BF16, nres=8)
            Wu, WuR = A.take("Wu", 45056, [8, DFF], BF16, nres=8)
            Wd, WdR = A.take("Wd", 90112, [NJ, D], BF16, nres=NJ)
            o0 = 135168
            NSL = 8
            xt, xtR = A.take("xt", o0, [NSL, D], F32, nres=NSL); o0 += NSL * 4096
            xs, xsR = A.take("xs", o0, [4, D], BF16, nres=4); o0 += 8192
            xnT, xnTR = A.take("xnT", o0, [8, 512], BF16, nres=8); o0 += 8192
            o_hff = o0
            wg_v = wg_d.rearrange("(k p) n -> p k n", p=128)
            wu_v = wu_d.rearrange("(k p) n -> p k n", p=128)
            for blk in range(6):
                lo, hi = blk * 512, min(DFF, blk * 512 + 512)
                S.dma("pool", lambda e, lo=lo, hi=hi: e.dma_start(out=Wg[:, :, lo:hi], in_=wg_v[:, :, lo:hi]), w=[WgR[blk]])
                S.dma("pool", lambda e, lo=lo, hi=hi: e.dma_start(out=Wu[:, :, lo:hi], in_=wu_v[:, :, lo:hi]), w=[WuR[blk]])
            load_w(Wd, WdR, wd_d, 0, D, NJ)
            slot = [0]

            def load_tile(i):
                sls = []
                for s in range(4):
                    sl = slot[0] % NSL
                    slot[0] += 1
                    r0 = i * 512 + s * 128
                    S.dma("sp", lambda e, sl=sl, r0=r0: e.dma_start(out=xt[:, sl, :], in_=src_d[r0:r0 + 128, :]),
                          r=[srcR[i * 4 + s]], w=[xtR[sl]])
                    sls.append(sl)
                return sls

            def front(i, sls):
                for s in range(4):
                    rms_T(xt, xtR, sls[s], s, gcol0, xs, xsR)
                transposes(xs, xsR, 8, gcol0, xnT, xnTR)

            def gate_up(i):
                for j in range(NJ):
                    pg, pgR = PS()
                    pu, puR = PS()
                    for k in range(8):
                        S.op("pe", lambda e, k=k, j=j, pg=pg: e.matmul(
                            pg[:, :], lhsT=Wg[:, k, j * 128:(j + 1) * 128], rhs=xnT[:, k, :],
                            start=(k == 0), stop=(k == 7)), r=[WgR[j // 4], xnTR[k]], w=[pgR])
                    for k in range(8):
                        S.op("pe", lambda e, k=k, j=j, pu=pu: e.matmul(
                            pu[:, :], lhsT=Wu[:, k, j * 128:(j + 1) * 128], rhs=xnT[:, k, :],
                            start=(k == 0), stop=(k == 7)), r=[WuR[j // 4], xnTR[k]], w=[puR])
                    b = j % 2
                    S.op("act", lambda e, b=b, pg=pg: e.activation(out=sg[:, b, :], in_=pg[:, :], func=AF.Silu),
                         r=[pgR], w=[sgR[b]])
                    S.op("dve", lambda e, b=b, j=j, pu=pu: e.tensor_tensor(out=hff[:, j, :], in0=pu[:, :], in1=sg[:, b, :],
                                                                         op=ALU.mult),
                         r=[puR, sgR[b]], w=[hffR[j]])

            def down(i, sls):
                for s in range(4):
                    sl = sls[s]
                    for c in range(2):
                        pd, pdR = PS()
                        for j in range(NJ):
                            S.op("pe", lambda e, j=j, s=s, c=c, pd=pd: e.matmul(
                                pd[:, :], lhsT=hff[:, j, s * 128:(s + 1) * 128], rhs=Wd[:, j, c * 512:(c + 1) * 512],
                                start=(j == 0), stop=(j == NJ - 1)), r=[hffR[j], WdR[j]], w=[pdR])
                        S.op("dve", lambda e, sl=sl, c=c, pd=pd: e.scalar_tensor_tensor(
                            out=xt[:, sl, c * 512:(c + 1) * 512], in0=pd[:, :], scalar=0.5,
                            in1=xt[:, sl, c * 512:(c + 1) * 512], op0=ALU.mult, op1=ALU.add),
                             r=[pdR, xtR[sl]], w=[xtR[sl]])
                    r0 = i * 512 + s * 128
                    S.dma("sp", lambda e, sl=sl, r0=r0: e.dma_start(out=dst_d[r0:r0 + 128, :], in_=xt[:, sl, :]),
                          r=[xtR[sl]], w=[dstR[i * 4 + s]])

            sl_cur = load_tile(0)
            front(0, sl_cur)
            if pre_hook is not None:
                pre_hook()
            o0 = o_hff
            hff, hffR = A.take("hff", o0, [NJ, 512], BF16, nres=NJ); o0 += NJ * 1024
            sg, sgR = A.take("sg", o0, [2, 512], BF16, nres=2); o0 += 2048
            assert o0 <= CB, (o0, CB)
            for i in range(NT):
                gate_up(i)
                if i + 1 < NT:
                    sl_nxt = load_tile(i + 1)
                    front(i + 1, sl_nxt)
                down(i, sl_cur)
                if i + 1 < NT:
                    sl_cur = sl_nxt

        ropeR = [R("rope%d" % i) for i in range(2 * NT)]

        def rope_tables():
            base = 184320
            sets = []
            for b in range(2):
                o0 = base + b * 12288
                bufs = []
                for nm, dt in (("posi", I32), ("ang", F32), ("ki", I32), ("kf", F32), ("rc", F32), ("rs", F32)):
                    ap, (rr,) = A.take("%s%d" % (nm, b), o0, [512], dt)
                    o0 += 2048
                    bufs.append((ap, rr))
                sets.append(bufs)

            def reduce_and_sin(ang, angR, ki, kiR, kf, kfR, dst, dstR, offs, scale):
                S.op("dve", lambda e: e.tensor_scalar(out=dst[0:64, :], in0=ang[0:64, :], scalar1=col(125, 64),
                                                      scalar2=offs, op0=ALU.mult, op1=ALU.add),
                     r=[angR, colsR], w=[dstR])
                S.op("dve", lambda e: e.scalar_tensor_tensor(out=dst[0:64, :], in0=ang[0:64, :], scalar=col(134, 64),
                                                             in1=dst[0:64, :], op0=ALU.mult, op1=ALU.add),
                     r=[angR, colsR, dstR], w=[dstR])
                S.op("dve", lambda e: e.tensor_copy(out=ki[0:64, :], in_=dst[0:64, :]), r=[dstR], w=[kiR])
                S.op("dve", lambda e: e.tensor_copy(out=kf[0:64, :], in_=ki[0:64, :]), r=[kiR], w=[kfR])
                S.op("dve", lambda e: e.tensor_tensor(out=dst[0:64, :], in0=dst[0:64, :], in1=kf[0:64, :], op=ALU.subtract),
                     r=[dstR, kfR], w=[dstR])
                S.op("dve", lambda e: e.tensor_single_scalar(out=kf[0:64, :], in_=dst[0:64, :], scalar=0.5, op=ALU.is_gt),
                     r=[dstR], w=[kfR])
                S.op("dve", lambda e: e.tensor_tensor(out=dst[0:64, :], in0=dst[0:64, :], in1=kf[0:64, :], op=ALU.subtract),
                     r=[dstR, kfR], w=[dstR])
                S.op("act", lambda e: e.activation(out=dst[0:64, :], in_=dst[0:64, :], func=AF.Sin, scale=scale),
                     r=[dstR, colsR], w=[dstR])

            def piece(i):
                (posi, posiR), (ang, angR), (ki, kiR), (kf, kfR), (rc, rcR), (rs, rsR) = sets[i % 2]
                c0 = i * 512
                S.dma("sp", lambda e: e.dma_start(out=posi[0:64, :], in_=pos_d[:, c0:c0 + 512]), w=[posiR])
                S.op("dve", lambda e: e.tensor_copy(out=ang[0:64, :], in_=posi[0:64, :]), r=[posiR], w=[angR])
                reduce_and_sin(ang, angR, ki, kiR, kf, kfR, rc, rcR, 0.25, TWO_PI)
                reduce_and_sin(ang, angR, ki, kiR, kf, kfR, rs, rsR, 0.0, col(128, 64))
                S.dma("sp", lambda e: e.dma_start(out=rope_s[0, :, c0:c0 + 512], in_=rc[0:64, :]), r=[rcR], w=[ropeR[i]])
                S.dma("sp", lambda e: e.dma_start(out=rope_s[1, :, c0:c0 + 512], in_=rs[0:64, :]), r=[rsR], w=[ropeR[NT + i]])

            for i in range(NT):
                piece(i)


        xR_in = [R("xin%d" % i) for i in range(NT * 4)]
        ffn_phase(f1g_d, f1u_d, f1d_d, 0, x_d, xR_in, h_s, hR, pre_hook=rope_tables)


        PB = CB - 14 * T - 8 * T
        cqnT, cqnTR = A.take("cqnT", PB, [3, T], BF16, nres=3 * NT)
        ckvnT, ckvnTR = A.take("ckvnT", PB + 6 * T, [2, T], BF16, nres=2 * NT)
        kpeT, kpeTR = A.take("kpeT", PB + 10 * T, [T], BF16, nres=NT)
        cosT, (cosR,) = A.take("cosT", PB + 14 * T, [T], F32)
        sinT, (sinR,) = A.take("sinT", PB + 18 * T, [T], F32)

        def phase_b():
            S.dma("sp", lambda e: e.dma_start(out=cosT[0:64, :], in_=rope_s[0]), r=ropeR, w=[cosR])
            S.dma("sp", lambda e: e.dma_start(out=sinT[0:64, :], in_=rope_s[1]), r=ropeR, w=[sinR])
            Win, WinR = A.take("Win", 0, [8, 2752], BF16, nres=8)
            Wsw, (WswR,) = A.take("Wsw", 44032, [8, 64], BF16)
            o0 = 44032 + 1024
            NSL = 8
            xt, xtR = A.take("xt", o0, [NSL, D], F32, nres=NSL); o0 += NSL * 4096
            xs, xsR = A.take("xs", o0, [4, D], BF16, nres=4); o0 += 8192
            uT2, uT2R = A.take("uT", o0, [2, 8, 512], BF16, nres=16); o0 += 16384
            zt, ztR = A.take("zt", o0, [3, 512], F32, nres=3); o0 += 6144
            cqs, cqsR = A.take("cqs", o0, [4, QL], BF16, nres=4); o0 += 4 * QL * 2
            cks, cksR = A.take("cks", o0, [4, KVL], BF16, nres=4); o0 += 4 * KVL * 2
            tmp, tmpR = A.take("tmpB", o0, [2, 512], F32, nres=2); o0 += 4096
            assert o0 <= PB, (o0, PB)
            load_w(Win, WinR, win_d, 0, 2752, 8)
            S.op("dve", lambda e: e.tensor_copy(out=Wsw[:, :, 0:32], in_=Win[:, :, 2720:2752]), r=WinR, w=[WswR])
            S.op("dve", lambda e: e.tensor_copy(out=Wsw[:, :, 32:64], in_=Win[:, :, 2688:2720]), r=WinR, w=[WswR])
            slot = [0]
            for i in range(NT):
                S.op("pool", lambda e, i=i: e.memset(kpeT[64:128, i * 512:(i + 1) * 512], 0.0), w=[kpeTR[i]])

            def load_tile(i):
                sls = []
                for s in range(4):
                    sl = slot[0] % NSL
                    slot[0] += 1
                    r0 = i * 512 + s * 128
                    S.dma("sp", lambda e, sl=sl, r0=r0: e.dma_start(out=xt[:, sl, :], in_=h_s[r0:r0 + 128, :]),
                          r=[hR[i * 4 + s]], w=[xtR[sl]])
                    sls.append(sl)
                return sls

            def front_a(i, sls):
                for s in range(4):
                    rms_T(xt, xtR, sls[s], s, 8, xs, xsR)

            def front_b(i):
                transposes(xs, xsR, 8, 8, uT2[:, i % 2], uT2R[(i % 2) * 8:(i % 2) * 8 + 8])

            zb = [0]

            def body(i, part):
                c0 = i * 512
                uT = uT2[:, i % 2]
                uTR = uT2R[(i % 2) * 8:(i % 2) * 8 + 8]
                if part == 1:
                    body1(i, c0, uT, uTR)
                else:
                    body2(i, c0, uT, uTR)

            def body1(i, c0, uT, uTR):
                for c in range(16):
                    ps, pr = PS()
                    for k in range(8):
                        S.op("pe", lambda e, k=k, c=c, ps=ps: e.matmul(
                            ps[:, :], lhsT=Win[:, k, c * 128:(c + 1) * 128], rhs=uT[:, k, :],
                            start=(k == 0), stop=(k == 7)), r=[WinR[k], uTR[k]], w=[pr])
                    b = zb[0] % 3
                    zb[0] += 1
                    eng = "act" if c % 2 == 0 else "dve"
                    if eng == "act":
                        S.op("act", lambda e, b=b, ps=ps: e.activation(out=zt[:, b, :], in_=ps[:, :], func=AF.Copy),
                             r=[pr], w=[ztR[b]])
                    else:
                        S.op("dve", lambda e, b=b, ps=ps: e.tensor_copy(out=zt[:, b, :], in_=ps[:, :]), r=[pr], w=[ztR[b]])
                    S.dma("sp", lambda e, b=b, c=c, c0=c0: e.dma_start(out=z_s[c * 128:(c + 1) * 128, c0:c0 + 512],
                                                                     in_=zt[:, b, :]), r=[ztR[b]], w=[zR[c][i]])

            def body2(i, c0, uT, uTR):
                p1, p1R = PS()
                p2, p2R = PS()
                for k in range(8):
                    S.op("pe", lambda e, k=k, p1=p1: e.matmul(p1[0:64, :], lhsT=Win[:, k, 2688:2752], rhs=uT[:, k, :],
                                                            start=(k == 0), stop=(k == 7)), r=[WinR[k], uTR[k]], w=[p1R])
                for k in range(8):
                    S.op("pe", lambda e, k=k, p2=p2: e.matmul(p2[0:64, :], lhsT=Wsw[:, k, :], rhs=uT[:, k, :],
                                                            start=(k == 0), stop=(k == 7)), r=[WswR, uTR[k]], w=[p2R])
                S.op("dve", lambda e, p1=p1: e.tensor_tensor(out=tmp[0:64, 0, :], in0=p1[0:64, :], in1=cosT[0:64, c0:c0 + 512],
                                                           op=ALU.mult), r=[p1R, cosR], w=[tmpR[0]])
                S.op("dve", lambda e, p2=p2: e.tensor_tensor(out=tmp[0:64, 1, :], in0=p2[0:64, :], in1=sinT[0:64, c0:c0 + 512],
                                                           op=ALU.mult), r=[p2R, sinR], w=[tmpR[1]])
                S.op("dve", lambda e: e.tensor_tensor(out=kpeT[0:64, c0:c0 + 512], in0=tmp[0:64, 0, :], in1=tmp[0:64, 1, :],
                                                      op=ALU.add), r=[tmpR[0], tmpR[1]], w=[kpeTR[i]])
                for s in range(4):
                    pq, pqR = PS()
                    pk, pkR = PS()
                    for k in range(8):
                        S.op("pe", lambda e, k=k, s=s, pq=pq: e.matmul(
                            pq[:, 0:QL], lhsT=uT[:, k, s * 128:(s + 1) * 128], rhs=Win[:, k, 2048:2048 + QL],
                            start=(k == 0), stop=(k == 7)), r=[WinR[k], uTR[k]], w=[pqR])
                    for k in range(8):
                        S.op("pe", lambda e, k=k, s=s, pk=pk: e.matmul(
                            pk[:, 0:KVL], lhsT=uT[:, k, s * 128:(s + 1) * 128], rhs=Win[:, k, 2432:2432 + KVL],
                            start=(k == 0), stop=(k == 7)), r=[WinR[k], uTR[k]], w=[pkR])
                    for (pp, ppR, dst, dstR, wd) in ((pq, pqR, cqs, cqsR, QL), (pk, pkR, cks, cksR, KVL)):
                        ssc = stat[:, 24 + s:25 + s]
                        sdc = stat[:, 32 + s:33 + s]
                        rsc = stat[:, 40 + s:41 + s]
                        S.op("act", lambda e, pp=pp, dst=dst, wd=wd, ssc=ssc, s=s: e.activation(
                            out=dst[:, s, :], in_=pp[:, 0:wd], func=AF.Square, accum_out=ssc),
                             r=[ppR], w=[dstR[s], statR[s]])
                        rstd_op(ssc, sdc, rsc, wd, statR[s])
                        S.op("dve", lambda e, pp=pp, dst=dst, wd=wd, rsc=rsc, s=s: e.tensor_scalar(
                            out=dst[:, s, :], in0=pp[:, 0:wd], scalar1=rsc, scalar2=None, op0=ALU.mult),
                             r=[ppR, statR[s]], w=[dstR[s]])
                transposes(cqs, cqsR, 3, 32, cqnT, [cqnTR[k * NT + i] for k in range(3)], c0=c0)
                transposes(cks, cksR, 2, 35, ckvnT, [ckvnTR[k * NT + i] for k in range(2)], c0=c0)

            sl_cur = load_tile(0)
            front_a(0, sl_cur)
            front_b(0)
            for i in range(NT):
                if i + 1 < NT:
                    sl_cur = load_tile(i + 1)
                    front_a(i + 1, sl_cur)
                body(i, 1)
                if i + 1 < NT:
                    front_b(i + 1)
                body(i, 2)

        phase_b()

        def phase_d():
            Wuq, WuqR = A.take("Wuq", 0, [3, 1536], BF16, nres=3)
            Wqs, (WqsR,) = A.take("Wqs", 9216, [3, NH, 64], BF16)
            Wukv, WukvR = A.take("Wukv", 12288, [2, 2048], BF16, nres=2)
            o0 = 20480
            hb = []
            for b in range(2):
                qn, qnR = A.take("qn%d" % b, o0, [T], BF16, nres=NT); o0 += 2 * T
                qp, qpR = A.take("qp%d" % b, o0, [T], BF16, nres=NT); o0 += 2 * T
                kn, knR = A.take("kn%d" % b, o0, [T], BF16, nres=NT); o0 += 2 * T
                vv, vvR = A.take("vv%d" % b, o0, [NS, 128], BF16, nres=NT); o0 += 2 * T
                hb.append((qn, qnR, qp, qpR, kn, knR, vv, vvR))
                for i in range(NT):
                    S.op("pool", lambda e, i=i, qp=qp: e.memset(qp[64:128, i * 512:(i + 1) * 512], 0.0), w=[qpR[i]])
            NPT = 6
            PT, PTR = A.take("PT", o0, [NPT, 512], BF16, nres=NPT); o0 += NPT * 1024
            acc, accR = A.take("acc", o0, [2, 2, 512], F32, nres=4); o0 += 8192
            tmp, tmpR = A.take("tmpD", o0, [2, 512], F32, nres=2); o0 += 4096
            rl, rlR = A.take("rl", o0, [2, 512], F32, nres=2); o0 += 4096
            ot, otR = A.take("ot", o0, [2, 512], BF16, nres=2); o0 += 2048
            assert o0 <= PB, (o0, PB)
            load_w(Wuq, WuqR, wuq_d, 0, 1536, 3)
            load_w(Wukv, WukvR, wukv_d, 0, 2048, 2)
            Wuq4 = Wuq.rearrange("p k (h d) -> p k h d", h=NH)
            S.op("dve", lambda e: e.tensor_copy(out=Wqs[:, :, :, 0:32], in_=Wuq4[:, :, :, 160:192]), r=WuqR, w=[WqsR])
            S.op("dve", lambda e: e.tensor_copy(out=Wqs[:, :, :, 32:64], in_=Wuq4[:, :, :, 128:160]), r=WuqR, w=[WqsR])

            def prep(h):
                for i in range(NT):
                    prep_tile(h, i)

            def prep_tile(h, i):
                qn, qnR, qp, qpR, kn, knR, vv, vvR = hb[h % 2]
                if True:
                    c0 = i * 512
                    lat_q = [cqnTR[k * NT + i] for k in range(3)]
                    lat_k = [ckvnTR[k * NT + i] for k in range(2)]
                    ps, pr = PS(0, 4)
                    for k in range(3):
                        S.op("pe", lambda e, k=k, ps=ps: e.matmul(
                            ps[:, :], lhsT=Wuq[:, k, h * 192:h * 192 + 128], rhs=cqnT[:, k, c0:c0 + 512],
                            start=(k == 0), stop=(k == 2)), r=[WuqR[k], lat_q[k]], w=[pr])
                    S.op("act", lambda e, ps=ps: e.activation(out=qn[:, c0:c0 + 512], in_=ps[:, :], func=AF.Copy),
                         r=[pr], w=[qnR[i]])
                    p1, p1R = PS(0, 4)
                    p2, p2R = PS(0, 4)
                    for k in range(3):
                        S.op("pe", lambda e, k=k, p1=p1: e.matmul(
                            p1[0:64, :], lhsT=Wuq[:, k, h * 192 + 128:h * 192 + 192], rhs=cqnT[:, k, c0:c0 + 512],
                            start=(k == 0), stop=(k == 2)), r=[WuqR[k], lat_q[k]], w=[p1R])
                    for k in range(3):
                        S.op("pe", lambda e, k=k, p2=p2: e.matmul(
                            p2[0:64, :], lhsT=Wqs[:, k, h, :], rhs=cqnT[:, k, c0:c0 + 512],
                            start=(k == 0), stop=(k == 2)), r=[WqsR, lat_q[k]], w=[p2R])
                    S.op("dve", lambda e, p1=p1: e.tensor_tensor(out=tmp[0:64, 0, :], in0=p1[0:64, :],
                                                               in1=cosT[0:64, c0:c0 + 512], op=ALU.mult),
                         r=[p1R, cosR], w=[tmpR[0]])
                    S.op("dve", lambda e, p2=p2: e.tensor_tensor(out=tmp[0:64, 1, :], in0=p2[0:64, :],
                                                               in1=sinT[0:64, c0:c0 + 512], op=ALU.mult),
                         r=[p2R, sinR], w=[tmpR[1]])
                    S.op("dve", lambda e: e.tensor_tensor(out=qp[0:64, c0:c0 + 512], in0=tmp[0:64, 0, :],
                                                          in1=tmp[0:64, 1, :], op=ALU.add),
                         r=[tmpR[0], tmpR[1]], w=[qpR[i]])
                    pk, pkR = PS(0, 4)
                    for k in range(2):
                        S.op("pe", lambda e, k=k, pk=pk: e.matmul(
                            pk[:, :], lhsT=Wukv[:, k, h * 256:h * 256 + 128], rhs=ckvnT[:, k, c0:c0 + 512],
                            start=(k == 0), stop=(k == 1)), r=[WukvR[k], lat_k[k]], w=[pkR])
                    S.op("act", lambda e, pk=pk: e.activation(out=kn[:, c0:c0 + 512], in_=pk[:, :], func=AF.Copy),
                         r=[pkR], w=[knR[i]])
                    pv, pvR = PS(0, 4)
                    for n in range(4):
                        for k in range(2):
                            S.op("pe", lambda e, k=k, n=n, pv=pv: e.matmul(
                                pv[:, n * 128:(n + 1) * 128], lhsT=ckvnT[:, k, c0 + n * 128:c0 + (n + 1) * 128],
                                rhs=Wukv[:, k, h * 256 + 128:h * 256 + 256], start=(k == 0), stop=(k == 1)),
                                 r=[WukvR[k], lat_k[k]], w=[pvR])
                    S.op("dve", lambda e, pv=pv: e.tensor_copy(
                        out=vv[:, i * 4:(i + 1) * 4, :], in_=pv[:, :].rearrange("p (n d) -> p n d", n=4)),
                         r=[pvR], w=[vvR[i]])

            ptc = [0]

            def attend(h):
                for qi in range(NT):
                    attend_q(h, qi)

            def attend_q(h, qi):
                qn, qnR, qp, qpR, kn, knR, vv, vvR = hb[h % 2]
                if True:
                    q0 = qi * 512
                    pO, pOR = PS(4, 6)
                    pL, pLR = PS(6, 8)

                    def scores(kj):
                        ps, pr = PS(0, 4)
                        ki = kj // 4
                        S.op("pe", lambda e, ps=ps: e.matmul(ps[:, :], lhsT=kn[:, kj * 128:(kj + 1) * 128],
                                                           rhs=qn[:, q0:q0 + 512], start=True, stop=False),
                             r=[knR[ki], qnR[qi]], w=[pr])
                        S.op("pe", lambda e, ps=ps: e.matmul(ps[:, :], lhsT=kpeT[:, kj * 128:(kj + 1) * 128],
                                                           rhs=qp[:, q0:q0 + 512], start=False, stop=True),
                             r=[kpeTR[ki], qpR[qi]], w=[pr])
                        b = ptc[0] % NPT
                        ptc[0] += 1
                        S.op("act", lambda e, ps=ps, b=b: e.activation(out=PT[:, b, :], in_=ps[:, :], func=AF.Exp,
                                                                     scale=SM_SCALE), r=[pr], w=[PTR[b]])
                        return b

                    LA = 2
                    pend = [scores(j) for j in range(min(LA, NS))]
                    ab = qi % 2
                    nd = [0]
                    for kj in range(NS):
                        b = pend.pop(0)
                        if kj + LA < NS:
                            pend.append(scores(kj + LA))
                        S.op("pe", lambda e, b=b, kj=kj, pO=pO: e.matmul(
                            pO[:, :], lhsT=vv[:, kj, :], rhs=PT[:, b, :], start=(kj == 0), stop=(kj == NS - 1)),
                             r=[vvR[kj // 4], PTR[b]], w=[pOR])
                        on_pe = (DEN_MODE == "pe") or (DEN_MODE == "hybrid" and kj % PE_EVERY == PE_EVERY - 1)
                        if on_pe:
                            first_pe = (kj == (0 if DEN_MODE == "pe" else PE_EVERY - 1))
                            last_pe = (kj == NS - 1) and DEN_MODE == "pe"
                            S.op("pe", lambda e, b=b, pL=pL, first_pe=first_pe, last_pe=last_pe: e.matmul(
                                pL[:, :], lhsT=ones, rhs=PT[:, b, :], start=first_pe, stop=last_pe),
                                 r=[onesR, PTR[b]], w=[pLR])
                            continue
                        par = nd[0] % 2
                        first = nd[0] < 2
                        nd[0] += 1
                        if first:
                            S.op("dve", lambda e, b=b, par=par: e.tensor_copy(out=acc[:, ab, par, :], in_=PT[:, b, :]),
                                 r=[PTR[b]], w=[accR[ab * 2 + par]])
                        else:
                            S.op("dve", lambda e, b=b, par=par: e.tensor_tensor(out=acc[:, ab, par, :], in0=acc[:, ab, par, :],
                                                                             in1=PT[:, b, :], op=ALU.add),
                                 r=[PTR[b], accR[ab * 2 + par]], w=[accR[ab * 2 + par]])
                    if DEN_MODE != "pe":
                        S.op("dve", lambda e: e.tensor_tensor(out=acc[:, ab, 0, :], in0=acc[:, ab, 0, :], in1=acc[:, ab, 1, :],
                                                              op=ALU.add), r=[accR[ab * 2], accR[ab * 2 + 1]], w=[accR[ab * 2]])
                        S.op("pe", lambda e, pL=pL: e.matmul(pL[:, :], lhsT=ones32, rhs=acc[:, ab, 0, :],
                                                           start=(DEN_MODE == "dve"), stop=True),
                             r=[ones32R, accR[ab * 2]], w=[pLR])
                    ob = qi % 2
                    S.op("dve", lambda e, ob=ob, pL=pL: e.reciprocal(out=rl[:, ob, :], in_=pL[:, :]), r=[pLR], w=[rlR[ob]])
                    S.op("dve", lambda e, ob=ob, pO=pO: e.tensor_tensor(out=ot[:, ob, :], in0=pO[:, :], in1=rl[:, ob, :],
                                                                       op=ALU.mult), r=[pOR, rlR[ob]], w=[otR[ob]])
                    S.dma("sp", lambda e, ob=ob: e.dma_start(out=o_s[h * 128:(h + 1) * 128, q0:q0 + 512], in_=ot[:, ob, :]),
                          r=[otR[ob]], w=[oR[h][qi]])

            prep(0)
            for h in range(NH):
                if h + 1 < NH:
                    prep(h + 1)
                attend(h)

        phase_d()

        def phase_c():
            Wr, (WrR,) = A.take("Wr", 0, [16, 128], BF16)
            Wi, (WiR,) = A.take("Wi", 4096, [16, 128], BF16)
            lc, (lcR,) = A.take("lc", 8192, [8, 16], F32)
            id32, (id32R,) = A.take("id32", 8704, [128], F32)
            dg, (dgR,) = A.take("dg", 9216, [32, 128], F32)
            o0 = 9216 + 16384
            TP = T + 4
            sets = []
            for b in range(2):
                zx, (zxR,) = A.take("zx%d" % b, o0, [TP], F32); o0 += 4 * TP
                zg, zgR = None, None
                xa, xaR = A.take("xa%d" % b, o0, [T], F32, nres=NT); o0 += 4 * T
                xab, xabR = A.take("xab%d" % b, o0, [T], BF16, nres=NT); o0 += 2 * T
                hf, hfR = A.take("hf%d" % b, o0, [T], F32, nres=NT); o0 += 4 * T
                sets.append((zx, zxR, zg, zgR, xa, xaR, xab, xabR, hf, hfR))
            GRP = 3
            NRG = 2 * GRP
            tA, tAR = A.take("tA", o0, [NRG, 512], F32, nres=NRG); o0 += NRG * 2048
            tB, tBR = A.take("tB", o0, [NRG, 512], F32, nres=NRG); o0 += NRG * 2048
            tM, tMR = A.take("tM", o0, [NRG, 512], F32, nres=NRG); o0 += NRG * 2048
            NR = 3
            tZ, tZR = A.take("tZ", o0, [NR, 512], F32, nres=NR); o0 += NR * 2048
            hbc, (hbcR,) = A.take("hbc", o0, [32], F32); o0 += 128
            NRH = GRP + 2
            tH, tHR = A.take("tH", o0, [NRH, 512], F32, nres=NRH); o0 += NRH * 2048
            tG, tGR = A.take("tG", o0, [NR, 512], F32, nres=NR); o0 += NR * 2048
            tS, tSR = A.take("tS", o0, [NR, 512], F32, nres=NR); o0 += NR * 2048
            tY, tYR = A.take("tY", o0, [NR, 512], BF16, nres=NR); o0 += NR * 1024
            assert o0 <= CB, (o0, CB)
            S.dma("sp", lambda e: e.dma_start(out=id32, in_=ident_d), w=[id32R])
            for j in range(32):
                S.op("dve", lambda e, j=j: e.tensor_scalar(out=dg[:, j, :], in0=id32, scalar1=col(37 + j), scalar2=None,
                                                           op0=ALU.mult), r=[id32R, colsR], w=[dgR])
            S.dma("pool", lambda e: e.dma_start(out=Wr, in_=wr_d.rearrange("g k m -> k g m")), w=[WrR])
            S.dma("pool", lambda e: e.dma_start(out=Wi, in_=wi_d.rearrange("g k m -> k g m")), w=[WiR])
            lam = cols[:, 109:125]
            L = lambda j: lc[:, j, :]
            S.op("dve", lambda e: e.tensor_scalar(out=L(0), in0=lam, scalar1=-1.0, scalar2=None, op0=ALU.mult),
                 r=[colsR], w=[lcR])
            S.op("dve", lambda e: e.tensor_tensor(out=L(0), in0=L(0), in1=lam, op=ALU.max), r=[colsR, lcR], w=[lcR])
            S.op("act", lambda e: e.activation(out=L(1), in_=L(0), func=AF.Exp, scale=-1.0), r=[lcR], w=[lcR])
            S.op("dve", lambda e: e.tensor_scalar(out=L(2), in0=L(1), scalar1=2.0, scalar2=None, op0=ALU.add), r=[lcR], w=[lcR])
            S.op("dve", lambda e: e.reciprocal(out=L(2), in_=L(2)), r=[lcR], w=[lcR])
            S.op("dve", lambda e: e.tensor_tensor(out=L(3), in0=L(1), in1=L(2), op=ALU.mult), r=[lcR], w=[lcR])
            S.op("dve", lambda e: e.tensor_tensor(out=L(4), in0=L(3), in1=L(3), op=ALU.mult), r=[lcR], w=[lcR])
            S.op("dve", lambda e: e.tensor_scalar(out=L(5), in0=L(4), scalar1=1.0 / 13, scalar2=1.0 / 11, op0=ALU.mult,
                                                  op1=ALU.add), r=[lcR], w=[lcR])
            for cst in (1.0 / 9, 1.0 / 7, 1.0 / 5, 1.0 / 3, 1.0):
                S.op("dve", lambda e: e.tensor_tensor(out=L(5), in0=L(5), in1=L(4), op=ALU.mult), r=[lcR], w=[lcR])
                S.op("dve", lambda e, cst=cst: e.tensor_scalar(out=L(5), in0=L(5), scalar1=cst, scalar2=None, op0=ALU.add),
                     r=[lcR], w=[lcR])
            S.op("dve", lambda e: e.tensor_tensor(out=L(5), in0=L(5), in1=L(3), op=ALU.mult), r=[lcR], w=[lcR])
            S.op("dve", lambda e: e.tensor_scalar(out=L(6), in0=lam, scalar1=-1.0, scalar2=0.0, op0=ALU.mult, op1=ALU.max),
                 r=[colsR], w=[lcR])
            S.op("dve", lambda e: e.scalar_tensor_tensor(out=L(6), in0=L(5), scalar=2.0, in1=L(6), op0=ALU.mult, op1=ALU.add),
                 r=[lcR], w=[lcR])
            S.op("dve", lambda e: e.tensor_scalar(out=L(7), in0=L(6), scalar1=-8.0, scalar2=None, op0=ALU.mult),
                 r=[lcR], w=[lcR])
            S.op("dve", lambda e: e.tensor_scalar(out=L(6), in0=L(6), scalar1=-4.0, scalar2=None, op0=ALU.mult),
                 r=[lcR], w=[lcR])
            S.op("dve", lambda e: e.tensor_scalar(out=hbc, in0=cols[:, 77:109], scalar1=0.5, scalar2=None, op0=ALU.mult),
                 r=[colsR], w=[hbcR])
            for b in range(2):
                zx, zxR = sets[b][0], sets[b][1]
                S.op("pool", lambda e, zx=zx: e.memset(zx[:, 0:2], 0.0), w=[zxR])
                S.op("pool", lambda e, zx=zx: e.memset(zx[:, T + 2:T + 4], 0.0), w=[zxR])

            def load(n):
                zx, zxR = sets[n % 2][0:2]
                S.dma("sp", lambda e: e.dma_start(out=zx[:, 2:T + 2], in_=z_s[n * 128:(n + 1) * 128, :]),
                      r=zR[n], w=[zxR])

            GC = 2.0 * math.sqrt(2.0 / math.pi)
            ctr = {"g": 0, "h": 0, "t": 0}

            def conv(n, i):
                zx, zxR, zg, zgR, xa, xaR, xab, xabR, hf, hfR = sets[n % 2]
                sl = slice(i * 512, (i + 1) * 512)
                c0 = i * 512
                ps, pr = PS()
                for k in range(4):
                    S.op("pe", lambda e, k=k: e.matmul(ps[:, :], lhsT=dg[:, n * 4 + k, :], rhs=zx[:, c0 + k:c0 + k + 512],
                                                      start=(k == 0), stop=(k == 3)), r=[dgR, zxR], w=[pr])
                S.op("act", lambda e: e.activation(out=xa[:, sl], in_=ps[:, :], func=AF.Identity, bias=col(69 + n)),
                     r=[pr, colsR], w=[xaR[i]])
                S.op("dve", lambda e: e.tensor_copy(out=xab[:, sl], in_=xa[:, sl]), r=[xaR[i]], w=[xabR[i]])

            def gates_group(n, d, tiles):
                zx, zxR, zg, zgR, xa, xaR, xab, xabR, hf, hfR = sets[n % 2]
                g = d * 8 + n
                slots = []
                pss = []
                for i in tiles:
                    sl = slice(i * 512, (i + 1) * 512)
                    s_ = ctr["g"] % NRG
                    ctr["g"] += 1
                    slots.append(s_)
                    pr_, prR = PS()
                    pi_, piR = PS()
                    pss.append((pr_, prR, pi_, piR))
                    S.op("pe", lambda e, pr_=pr_, sl=sl: e.matmul(pr_[:, :], lhsT=Wr[:, g, :], rhs=xab[:, sl], start=True, stop=True),
                         r=[WrR, xabR[i]], w=[prR])
                    S.op("pe", lambda e, pi_=pi_, sl=sl: e.matmul(pi_[:, :], lhsT=Wi[:, g, :], rhs=xab[:, sl], start=True, stop=True),
                         r=[WiR, xabR[i]], w=[piR])
                for (i, s_, (pr_, prR, pi_, piR)) in zip(tiles, slots, pss):
                    S.op("act", lambda e, pr_=pr_, s_=s_: e.activation(out=tA[:, s_, :], in_=pr_[:, :], func=AF.Tanh, scale=0.5,
                                                                       bias=hbc[:, g:g + 1]), r=[prR, hbcR], w=[tAR[s_]])
                    S.op("act", lambda e, pi_=pi_, s_=s_: e.activation(out=tB[:, s_, :], in_=pi_[:, :], func=AF.Tanh, scale=0.5,
                                                                       bias=hbc[:, 16 + g:17 + g]), r=[piR, hbcR], w=[tBR[s_]])
                for (i, s_) in zip(tiles, slots):
                    S.op("act", lambda e, s_=s_: e.activation(out=tM[:, s_, :], in_=tA[:, s_, :], func=AF.Exp,
                                                             scale=lc[:, 7, g:g + 1], bias=lc[:, 7, g:g + 1]),
                         r=[tAR[s_], lcR], w=[tMR[s_]])
                    S.op("act", lambda e, s_=s_: e.activation(out=tA[:, s_, :], in_=tA[:, s_, :], func=AF.Exp,
                                                             scale=lc[:, 6, g:g + 1], bias=lc[:, 6, g:g + 1]),
                         r=[tAR[s_], lcR], w=[tAR[s_]])
                for (i, s_) in zip(tiles, slots):
                    S.op("act", lambda e, s_=s_: e.activation(out=tM[:, s_, :], in_=tM[:, s_, :], func=AF.Sqrt, scale=-0.25,
                                                             bias=col(132)), r=[tMR[s_], colsR], w=[tMR[s_]])
                for (i, s_) in zip(tiles, slots):
                    sl = slice(i * 512, (i + 1) * 512)
                    S.op("dve", lambda e, s_=s_, sl=sl: e.tensor_tensor(out=tM[:, s_, :], in0=tM[:, s_, :], in1=xa[:, sl], op=ALU.mult),
                         r=[tMR[s_], xaR[i]], w=[tMR[s_]])
                    S.op("dve", lambda e, s_=s_: e.scalar_tensor_tensor(out=tB[:, s_, :], in0=tB[:, s_, :], scalar=1.0, in1=tM[:, s_, :],
                                                                       op0=ALU.add, op1=ALU.mult),
                         r=[tBR[s_], tMR[s_]], w=[tBR[s_]])
                return slots

            def scan_f(n, i, s_):
                hf, hfR = sets[n % 2][8], sets[n % 2][9]
                c0 = i * 512
                init = 0.0 if i == 0 else hf[:, c0 - 1:c0]
                rr = [tAR[s_], tBR[s_]] + ([hfR[i - 1]] if i > 0 else [])
                S.op("dve", lambda e: e.tensor_tensor_scan(out=hf[:, c0:c0 + 512], data0=tA[:, s_, :], data1=tB[:, s_, :],
                                                           initial=init, op0=ALU.mult, op1=ALU.add), r=rr, w=[hfR[i]])

            def scan_b(n, i, s_, prev_h):
                hs = ctr["h"] % NRH
                ctr["h"] += 1
                init = 0.0 if prev_h is None else tH[:, prev_h, 0:1]
                rr = [tAR[s_], tBR[s_]] + ([tHR[prev_h]] if prev_h is not None else [])
                S.op("dve", lambda e: e.tensor_tensor_scan(out=tH[:, hs, ::-1], data0=tA[:, s_, ::-1], data1=tB[:, s_, ::-1],
                                                           initial=init, op0=ALU.mult, op1=ALU.add), r=rr, w=[tHR[hs]])
                return hs

            def tail(n, i, hs):
                zx, zxR, zg, zgR, xa, xaR, xab, xabR, hf, hfR = sets[n % 2]
                sl = slice(i * 512, (i + 1) * 512)
                t_ = ctr["t"] % NR
                ctr["t"] += 1
                S.dma("sp", lambda e: e.dma_start(out=tZ[:, t_, :], in_=z_s[(8 + n) * 128:(9 + n) * 128, sl]),
                      r=[zR[8 + n][i]], w=[tZR[t_]])
                S.op("dve", lambda e: e.tensor_tensor(out=tG[:, t_, :], in0=hf[:, sl], in1=tH[:, hs, :], op=ALU.add),
                     r=[hfR[i], tHR[hs]], w=[tGR[t_]])
                S.op("act", lambda e: e.activation(out=tS[:, t_, :], in_=tZ[:, t_, :], func=AF.Square,
                                                   scale=math.sqrt(0.044715)), r=[tZR[t_]], w=[tSR[t_]])
                S.op("dve", lambda e: e.scalar_tensor_tensor(out=tS[:, t_, :], in0=tS[:, t_, :], scalar=1.0, in1=tZ[:, t_, :],
                                                             op0=ALU.add, op1=ALU.mult), r=[tSR[t_], tZR[t_]], w=[tSR[t_]])
                S.op("act", lambda e: e.activation(out=tS[:, t_, :], in_=tS[:, t_, :], func=AF.Tanh, scale=0.5 * GC),
                     r=[tSR[t_]], w=[tSR[t_]])
                S.op("dve", lambda e: e.scalar_tensor_tensor(out=tG[:, t_, :], in0=tG[:, t_, :], scalar=0.5, in1=tZ[:, t_, :],
                                                             op0=ALU.mult, op1=ALU.mult), r=[tGR[t_], tZR[t_]], w=[tGR[t_]])
                S.op("dve", lambda e: e.scalar_tensor_tensor(out=tY[:, t_, :], in0=tS[:, t_, :], scalar=1.0, in1=tG[:, t_, :],
                                                             op0=ALU.add, op1=ALU.mult), r=[tSR[t_], tGR[t_]], w=[tYR[t_]])
                S.dma("sp", lambda e: e.dma_start(out=ya_s[n * 128:(n + 1) * 128, sl], in_=tY[:, t_, :]),
                      r=[tYR[t_]], w=[yaR[n][i]])

            def chunk(n):
                for i in range(NT):
                    conv(n, i)
                groups = [list(range(a, min(a + GRP, NT))) for a in range(0, NT, GRP)]
                for tiles in groups:
                    slots = gates_group(n, 0, tiles)
                    for i, s_ in zip(tiles, slots):
                        scan_f(n, i, s_)
                prev_h = None
                for tiles in reversed(groups):
                    tiles = tiles[::-1]
                    slots = gates_group(n, 1, tiles)
                    hss = []
                    for i, s_ in zip(tiles, slots):
                        prev_h = scan_b(n, i, s_, prev_h)
                        hss.append((i, prev_h))
                    for (ti, th) in hss:
                        tail(n, ti, th)

            load(0)
            for n in range(8):
                if n + 1 < 8:
                    load(n + 1)
                chunk(n)

        phase_c()

        def phase_e1():
            Wga, WgaR = A.take("Wga", 0, [8, 2048], BF16, nres=8)
            Wlo, WloR = A.take("Wlo", 32768, [8, D], BF16, nres=8)
            Wmo, WmoR = A.take("Wmo", 49152, [8, D], BF16, nres=8)
            Wo, WoR = A.take("Wo", 65536, [8, D], BF16, nres=8)
            o0 = 81920
            NSL = 8
            xt, xtR = A.take("xt", o0, [NSL, D], F32, nres=NSL); o0 += NSL * 4096
            xs, xsR = A.take("xs", o0, [4, D], BF16, nres=4); o0 += 8192
            uT2, uT2R = A.take("uT", o0, [2, 8, 512], BF16, nres=16); o0 += 16384
            yat, yatR = A.take("yat", o0, [2, 8, 512], BF16, nres=2); o0 += 16384
            ott, ottR = A.take("ott", o0, [2, 8, 512], BF16, nres=2); o0 += 16384
            mT, mTR = A.take("mT", o0, [8, 512], BF16, nres=8); o0 += 8192
            sgt, sgtR = A.take("sgt", o0, [4, 512], F32, nres=4); o0 += 8192
            assert o0 <= CB
            win_v = win_d.rearrange("(k p) n -> p k n", p=128)
            wlo_v = wlo_d.rearrange("(k p) n -> p k n", p=128)
            wmo_v = wmo_d.rearrange("(k p) n -> p k n", p=128)
            for blk in range(4):
                lo, hi = blk * 256, blk * 256 + 256
                S.dma("pool", lambda e, lo=lo, hi=hi: e.dma_start(out=Wga[:, :, lo:hi], in_=win_v[:, :, 2752 + lo:2752 + hi]),
                      w=[WgaR[blk]])
                S.dma("pool", lambda e, lo=lo, hi=hi: e.dma_start(out=Wga[:, :, 1024 + lo:1024 + hi],
                                                                 in_=win_v[:, :, 3776 + lo:3776 + hi]), w=[WgaR[4 + blk]])
                S.dma("pool", lambda e, lo=lo, hi=hi: e.dma_start(out=Wlo[:, :, lo:hi], in_=wlo_v[:, :, lo:hi]), w=[WloR[blk]])
                S.dma("pool", lambda e, lo=lo, hi=hi: e.dma_start(out=Wmo[:, :, lo:hi], in_=wmo_v[:, :, lo:hi]), w=[WmoR[blk]])
            load_w(Wo, WoR, wo_d, 0, D, 8)
            ya_v = ya_s.rearrange("(k p) t -> p k t", p=128)
            o_v = o_s.rearrange("(k p) t -> p k t", p=128)
            slot = [0]

            def load_tile(i):
                sls = []
                for s in range(4):
                    sl = slot[0] % NSL
                    slot[0] += 1
                    r0 = i * 512 + s * 128
                    S.dma("sp", lambda e, sl=sl, r0=r0: e.dma_start(out=xt[:, sl, :], in_=h_s[r0:r0 + 128, :]),
                          r=[hR[i * 4 + s]], w=[xtR[sl]])
                    sls.append(sl)
                b = i % 2
                S.dma("sp", lambda e: e.dma_start(out=yat[:, b], in_=ya_v[:, :, i * 512:(i + 1) * 512]),
                      r=[yaR[c][i] for c in range(8)], w=[yatR[b]])
                S.dma("sp", lambda e: e.dma_start(out=ott[:, b], in_=o_v[:, :, i * 512:(i + 1) * 512]),
                      r=[oR[h][i] for h in range(NH)], w=[ottR[b]])
                return sls

            def front_a(i, sls):
                for s in range(4):
                    rms_T(xt, xtR, sls[s], s, 8, xs, xsR)

            def front_b(i):
                transposes(xs, xsR, 8, 8, uT2[:, i % 2], uT2R[(i % 2) * 8:(i % 2) * 8 + 8])

            sc = [0]

            def body(i, sls, part):
                b = i % 2
                uT = uT2[:, i % 2]
                uTR = uT2R[(i % 2) * 8:(i % 2) * 8 + 8]
                if part == 1:
                    body1(i, sls, b, uT, uTR)
                else:
                    body2(i, sls)

            def body1(i, sls, b, uT, uTR):
                for c in range(8):
                    pA, pAR = PS(); pB, pBR = PS(); pYA, pYAR = PS(); pYB, pYBR = PS()
                    for (pp, ppR, W, WR, cc, src, srcR) in (
                            (pA, pAR, Wga, WgaR, c * 128, None, None), (pB, pBR, Wga, WgaR, 1024 + c * 128, None, None),
                            (pYA, pYAR, Wlo, WloR, c * 128, yat, yatR), (pYB, pYBR, Wmo, WmoR, c * 128, ott, ottR)):
                        for k in range(8):
                            if src is None:
                                rhs = uT[:, k, :]; rr = uTR[k]
                            else:
                                rhs = src[:, b, k, :]; rr = srcR[b]
                            S.op("pe", lambda e, k=k, pp=pp, W=W, cc=cc, rhs=rhs: e.matmul(
                                pp[:, :], lhsT=W[:, k, cc:cc + 128], rhs=rhs, start=(k == 0), stop=(k == 7)),
                                 r=[WR[cc // 256], rr], w=[ppR])
                    t0 = sc[0] % 2 * 2
                    sc[0] += 1
                    S.op("act", lambda e, pA=pA, t0=t0: e.activation(out=sgt[:, t0, :], in_=pA[:, :], func=AF.Sigmoid),
                         r=[pAR], w=[sgtR[t0]])
                    S.op("act", lambda e, pB=pB, t0=t0: e.activation(out=sgt[:, t0 + 1, :], in_=pB[:, :], func=AF.Sigmoid),
                         r=[pBR], w=[sgtR[t0 + 1]])
                    S.op("dve", lambda e, pYA=pYA, t0=t0: e.tensor_tensor(out=sgt[:, t0, :], in0=pYA[:, :], in1=sgt[:, t0, :],
                                                                         op=ALU.mult), r=[pYAR, sgtR[t0]], w=[sgtR[t0]])
                    S.op("dve", lambda e, pYB=pYB, t0=t0: e.tensor_tensor(out=sgt[:, t0 + 1, :], in0=pYB[:, :],
                                                                         in1=sgt[:, t0 + 1, :], op=ALU.mult),
                         r=[pYBR, sgtR[t0 + 1]], w=[sgtR[t0 + 1]])
                    S.op("pool", lambda e, c=c, t0=t0: e.tensor_tensor(out=mT[:, c, :], in0=sgt[:, t0, :], in1=sgt[:, t0 + 1, :],
                                                                      op=ALU.add), r=[sgtR[t0], sgtR[t0 + 1]], w=[mTR[c]])

            def body2(i, sls):
                for s in range(4):
                    sl = sls[s]
                    for c2 in range(2):
                        pd, pdR = PS()
                        for k in range(8):
                            S.op("pe", lambda e, k=k, s=s, c2=c2, pd=pd: e.matmul(
                                pd[:, :], lhsT=mT[:, k, s * 128:(s + 1) * 128], rhs=Wo[:, k, c2 * 512:(c2 + 1) * 512],
                                start=(k == 0), stop=(k == 7)), r=[mTR[k], WoR[k]], w=[pdR])
                        S.op("dve", lambda e, sl=sl, c2=c2, pd=pd: e.tensor_tensor(
                            out=xt[:, sl, c2 * 512:(c2 + 1) * 512], in0=pd[:, :], in1=xt[:, sl, c2 * 512:(c2 + 1) * 512],
                            op=ALU.add), r=[pdR, xtR[sl]], w=[xtR[sl]])
                    r0 = i * 512 + s * 128
                    S.dma("sp", lambda e, sl=sl, r0=r0: e.dma_start(out=h_s[r0:r0 + 128, :], in_=xt[:, sl, :]),
                          r=[xtR[sl]], w=[hR[i * 4 + s]])

            sl_cur = load_tile(0)
            front_a(0, sl_cur)
            front_b(0)
            for i in range(NT):
                cur = sl_cur
                if i + 1 < NT:
                    sl_cur = load_tile(i + 1)
                    front_a(i + 1, sl_cur)
                body(i, cur, 1)
                if i + 1 < NT:
                    front_b(i + 1)
                body(i, cur, 2)

        phase_e1()

        ffn_phase(f2g_d, f2u_d, f2d_d, 16, h_s, hR, h_s, hR)

        def phase_e3():
            Wpg, WpgR = A.take("Wpg", 0, [8, D], BF16, nres=8)
            Wpp, WppR = A.take("Wpp", 16384, [2, D], BF16, nres=2)
            bc, (bcR,) = A.take("bc", 20480, [2 * D], F32)
            o0 = 28672
            NSL = 8
            xt, xtR = A.take("xt", o0, [NSL, D], F32, nres=NSL); o0 += NSL * 4096
            xs, xsR = A.take("xs", o0, [4, D], BF16, nres=4); o0 += 8192
            hT, hTR = A.take("hT", o0, [8, 512], BF16, nres=8); o0 += 8192
            gg, ggR = A.take("gg", o0, [4, D], F32, nres=4); o0 += 16384
            pin, pinR = A.take("pin", o0, [2, 4, PLE], F32, nres=2); o0 += 8192
            pbs, pbsR = A.take("pbs", o0, [4, PLE], BF16, nres=4); o0 += 2048
            pT, pTR = A.take("pT", o0, [2, 512], BF16, nres=2); o0 += 2048
            t1, t1R = A.take("t1", o0, [2, D], F32, nres=2); o0 += 8192
            outt, outtR = A.take("outt", o0, [2, D], F32, nres=2); o0 += 8192
            load_w(Wpg, WpgR, wpg_d, 0, D, 8)
            load_w(Wpp, WppR, wpp_d, 0, D, 2)
            S.dma("sp", lambda e: e.dma_start(out=bc, in_=bc_d), w=[bcR])
            p_v = p_d.rearrange("(n s p) c -> n p s c", s=4, p=128)
            slot = [0]

            def load_tile(i):
                sls = []
                for s in range(4):
                    sl = slot[0] % NSL
                    slot[0] += 1
                    r0 = i * 512 + s * 128
                    S.dma("sp", lambda e, sl=sl, r0=r0: e.dma_start(out=xt[:, sl, :], in_=h_s[r0:r0 + 128, :]),
                          r=[hR[i * 4 + s]], w=[xtR[sl]])
                    sls.append(sl)
                b = i % 2
                S.dma("sp", lambda e: e.dma_start(out=pin[:, b], in_=p_v[i]), w=[pinR[b]])
                return sls

            def front(i, sls):
                b = i % 2
                for s in range(4):
                    rms_T(xt, xtR, sls[s], s, 24, xs, xsR, mode="pow")
                transposes(xs, xsR, 8, 24, hT, hTR)
                for s in range(4):
                    S.op("act", lambda e, s=s: e.activation(out=pbs[:, s, :], in_=pin[:, b, s, :], func=AF.Copy),
                         r=[pinR[b]], w=[pbsR[s]])
                for k in range(2):
                    ps, pr = PS()
                    for s in range(4):
                        S.op("pe", lambda e, s=s, k=k, ps=ps: e.matmul(
                            ps[:, s * 128:(s + 1) * 128], lhsT=pbs[:, s, k * 128:(k + 1) * 128], rhs=ident,
                            start=True, stop=True), r=[pbsR[s], identR], w=[pr])
                    S.op("dve", lambda e, k=k, ps=ps: e.tensor_copy(out=pT[:, k, :], in_=ps[:, :]), r=[pr], w=[pTR[k]])

            oc = [0]

            pend = []

            def body(i, sls):
                for s in range(4):
                    ctx = body_a(i, sls, s)
                    if pend:
                        body_b(*pend.pop(0))
                    pend.append(ctx)

            def body_a(i, sls, s):
                if True:
                    sl = sls[s]
                    pps = []
                    for c2 in range(2):
                        pg, pgR = PS()
                        for k in range(8):
                            S.op("pe", lambda e, k=k, c2=c2, pg=pg: e.matmul(
                                pg[:, :], lhsT=hT[:, k, s * 128:(s + 1) * 128], rhs=Wpg[:, k, c2 * 512:(c2 + 1) * 512],
                                start=(k == 0), stop=(k == 7)), r=[hTR[k], WpgR[k]], w=[pgR])
                        S.op("act", lambda e, c2=c2, pg=pg: e.activation(out=gg[:, s, c2 * 512:(c2 + 1) * 512], in_=pg[:, :],
                                                                       func=AF.Sigmoid), r=[pgR], w=[ggR[s]])
                        pp, ppR = PS()
                        for k in range(2):
                            S.op("pe", lambda e, k=k, c2=c2, pp=pp: e.matmul(
                                pp[:, :], lhsT=pT[:, k, s * 128:(s + 1) * 128], rhs=Wpp[:, k, c2 * 512:(c2 + 1) * 512],
                                start=(k == 0), stop=(k == 1)), r=[pTR[k], WppR[k]], w=[ppR])
                        pps.append((pp, ppR))
                    tb = oc[0] % 2
                    oc[0] += 1
                    c_a = stat[:, 48 + s:49 + s]; c_b = stat[:, 52 + s:53 + s]; c_r = stat[:, 56 + s:57 + s]
                    for c2 in range(2):
                        pp, ppR = pps[c2]
                        S.op("act", lambda e, pp=pp, c2=c2, cc=(c_a if c2 == 0 else c_b): e.activation(
                            out=t1[:, tb, c2 * 512:(c2 + 1) * 512], in_=pp[:, :], func=AF.Square, accum_out=cc),
                             r=[ppR], w=[t1R[tb], statR[s]])
                    S.op("dve", lambda e: e.tensor_tensor(out=c_a, in0=c_a, in1=c_b, op=ALU.add), r=[statR[s]], w=[statR[s]])
                    rstd_op(c_a, c_b, c_r, D, statR[s], "pow")
                    return (i, s, sl, pps, tb, c_r)

            def body_b(i, s, sl, pps, tb, c_r):
                if True:
                    for c2 in range(2):
                        pp, ppR = pps[c2]
                        S.op("dve", lambda e, pp=pp, c2=c2: e.scalar_tensor_tensor(
                            out=t1[:, tb, c2 * 512:(c2 + 1) * 512], in0=pp[:, :], scalar=c_r,
                            in1=bc[:, c2 * 512:(c2 + 1) * 512], op0=ALU.mult, op1=ALU.mult),
                             r=[ppR, statR[s], bcR], w=[t1R[tb]])
                    S.op("dve", lambda e: e.tensor_tensor(out=t1[:, tb, :], in0=t1[:, tb, :], in1=gg[:, s, :], op=ALU.mult),
                         r=[t1R[tb], ggR[s]], w=[t1R[tb]])
                    S.op("dve", lambda e: e.tensor_tensor(out=xt[:, sl, :], in0=xt[:, sl, :], in1=t1[:, tb, :], op=ALU.add),
                         r=[t1R[tb], xtR[sl]], w=[xtR[sl]])
                    f_s = stat[:, 4 + s:5 + s]; f_d = stat[:, 12 + s:13 + s]; f_r = stat[:, 20 + s:21 + s]
                    S.op("act", lambda e: e.activation(out=outt[:, tb, :], in_=xt[:, sl, :], func=AF.Square, accum_out=f_s),
                         r=[xtR[sl]], w=[outtR[tb], statR[4 + s]])
                    rstd_op(f_s, f_d, f_r, D, statR[4 + s], "pow")
                    S.op("dve", lambda e: e.scalar_tensor_tensor(out=outt[:, tb, :], in0=xt[:, sl, :], scalar=f_r,
                                                                 in1=bc[:, D:2 * D], op0=ALU.mult, op1=ALU.mult),
                         r=[xtR[sl], statR[4 + s], bcR], w=[outtR[tb]])
                    r0 = i * 512 + s * 128
                    S.dma("sp", lambda e, r0=r0: e.dma_start(out=out_d[r0:r0 + 128, :], in_=outt[:, tb, :]),
                          r=[outtR[tb]], w=[])

            sl_cur = load_tile(0)
            front(0, sl_cur)
            for i in range(NT):
                cur = sl_cur
                while pend:
                    body_b(*pend.pop(0))
                if i + 1 < NT:
                    sl_cur = load_tile(i + 1)
                body(i, cur)
                while pend:
                    body_b(*pend.pop(0))
                if i + 1 < NT:
                    front(i + 1, sl_cur)
            while pend:
                body_b(*pend.pop(0))

        phase_e3()

        S.finish()
    return nc


def _cols(inp):
    c = np.zeros((128, NCOL), np.float32)

    def chunks(v):
        v = np.asarray(v, np.float32).reshape(-1, 128)
        return v.T

    c[:, 0:8] = chunks(inp["ffn1_norm"][0])
    c[:, 8:16] = chunks(inp["mix_norm"][0])
    c[:, 16:24] = chunks(inp["ffn2_norm"][0])
    c[:, 24:32] = chunks(inp["ple_norm"][0])
    c[:, 32:35] = chunks(inp["q_norm"][0])
    c[:, 35:37] = chunks(inp["kv_norm"][0])
    cw = np.asarray(inp["conv_w"][0], np.float32)
    c[:, 37:69] = cw.reshape(4, 8, 128).transpose(2, 1, 0).reshape(128, 32)
    c[:, 69:77] = chunks(inp["conv_b"][0])
    c[:, 77:93] = chunks(np.asarray(inp["lru_b_r"][0]).reshape(-1))
    c[:, 93:109] = chunks(np.asarray(inp["lru_b_i"][0]).reshape(-1))
    c[:, 109:125] = chunks(np.asarray(inp["lru_lambda"][0]).reshape(-1))
    invf64 = 10000.0 ** (-(np.arange(32, dtype=np.float64)) / 32.0)
    c64 = np.concatenate([invf64, invf64]) / TWO_PI
    c_hi = c64.astype(np.float32)
    c[0:64, 125] = c_hi
    c[0:64, 134] = (c64 - c_hi.astype(np.float64)).astype(np.float32)
    c[:, 127] = EPS
    c[0:32, 128] = -TWO_PI
    c[32:64, 128] = TWO_PI
    c[0:32, 129] = math.pi
    c[32:64, 129] = -math.pi
    c[:, 130] = -math.pi
    c[:, 131] = 1.0
    c[:, 132] = 0.25
    c[:, 133] = -0.5
    return c


def make_in_maps(inp, T, ncores):
    shared = {
        "cols": _cols(inp),
        "ident": np.eye(128, dtype=np.float32),
        "bc": np.ascontiguousarray(np.broadcast_to(
            np.concatenate([np.asarray(inp["ple_proj_norm"][0], np.float32),
                            np.asarray(inp["final_norm"], np.float32)])[None, :], (128, 2 * D))),
    }
    for k in ("ffn1_w_gate", "ffn1_w_up", "ffn1_w_down", "ffn2_w_gate", "ffn2_w_up", "ffn2_w_down", "w_in",
              "w_lru_out", "w_uq", "w_ukv", "w_mla_out", "w_o", "ple_w_gate", "ple_w_proj"):
        shared[k] = np.ascontiguousarray(np.asarray(inp[k], np.float32)[0])
    shared["lru_w_r"] = np.ascontiguousarray(np.asarray(inp["lru_w_r"], np.float32)[0].reshape(16, 128, 128))
    shared["lru_w_i"] = np.ascontiguousarray(np.asarray(inp["lru_w_i"], np.float32)[0].reshape(16, 128, 128))
    maps = []
    for b in range(ncores):
        m = dict(shared)
        m["x"] = np.ascontiguousarray(np.asarray(inp["x"], np.float32)[b, :T])
        m["p"] = np.ascontiguousarray(np.asarray(inp["p"], np.float32)[0, b, :T])
        m["pos"] = np.ascontiguousarray(np.broadcast_to(np.asarray(inp["positions"], np.int32)[b, :T].reshape(1, T), (64, T)))
        maps.append(m)
    return maps


_NC_CACHE = {}


def kernel(**inputs):
    T = 4096
    n = 8
    if T not in _NC_CACHE:
        _NC_CACHE[T] = build(T)
    nc = _NC_CACHE[T]
    maps = make_in_maps(inputs, T, n)
    res = run_bass_kernel_spmd(nc, maps, core_ids=list(range(n)))
    return np.stack([np.asarray(r["out"], np.float32) for r in res.results], axis=0)
```

```python
import contextlib
import math
import numpy as np
import concourse.bass as bass
import concourse.mybir as mybir
from concourse.bass_utils import run_bass_kernel_spmd

F32 = mybir.dt.float32
BF16 = mybir.dt.bfloat16
I32 = mybir.dt.int32
U8 = mybir.dt.uint8
AF = mybir.ActivationFunctionType
ALU = mybir.AluOpType

D = 1024
DFF = 2816
NJ = DFF // 128
PLE = 256
NH = 8
QL = 384
KVL = 256
EPS = 1e-6
NCOL = 136
SM_SCALE = 192.0 ** -0.5
TWO_PI = 2.0 * math.pi
RSTD_MODE = "sqrt"
PE_EVERY = 16
DEN_MODE = "hybrid"

ENGS = ("pe", "act", "dve", "pool", "sp")
DMA_K = 8
SAME_ENGINE_RAW_ONLY = False
DMA_KQ = {}


class R:
    __slots__ = ("name", "w", "rs")

    def __init__(self, name=""):
        self.name = name
        self.w = None
        self.rs = {}


class Op:
    __slots__ = ("eng", "fn", "deps", "dma", "sig", "val", "sem", "n")


class Sched:
    def __init__(self, nc, es):
        self.nc = nc
        self.es = es
        self.ops = {e: [] for e in ENGS}
        self.dma_hist = {e: [] for e in ENGS}
        self.nops = 0

    def _rec(self, eng, fn, r, w, dma):
        o = Op()
        o.eng = eng; o.fn = fn; o.dma = dma; o.sig = False; o.val = 0; o.sem = None
        o.n = self.nops; self.nops += 1
        deps = []
        for x in r:
            if x.w is not None:
                deps.append(x.w)
        for x in w:
            if x.w is not None:
                deps.append(x.w)
            deps.extend(x.rs.values())
        fdeps = []
        seen = set()
        for p in deps:
            if p is o or id(p) in seen:
                continue
            seen.add(id(p))
            if (not p.dma) and (not dma) and p.eng == eng:
                if eng == "pe":
                    continue
                if SAME_ENGINE_RAW_ONLY and not any(x.w is p for x in r):
                    continue
            fdeps.append(p)
        o.deps = fdeps
        for x in r:
            x.rs[("dma", o.n) if dma else eng] = o
        for x in w:
            x.w = o
            x.rs = {}
        if dma:
            h = self.dma_hist[eng]
            kk = DMA_KQ.get(eng, DMA_K)
            if len(h) >= kk:
                o.deps.append(h[-kk])
            h.append(o)
        self.ops[eng].append(o)
        return o

    def op(self, eng, fn, r=(), w=()):
        return self._rec(eng, fn, r, w, False)

    def dma(self, eng, fn, r=(), w=()):
        return self._rec(eng, fn, r, w, True)

    def finish(self):
        nc = self.nc
        for e in ENGS:
            for o in self.ops[e]:
                for p in o.deps:
                    p.sig = True
        sems = {e: self.es.enter_context(nc.semaphore("s_" + e)) for e in ENGS}
        dsems = {}
        for e in ENGS:
            if self.dma_hist[e]:
                dsems[e] = [self.es.enter_context(nc.semaphore("d_%s%d" % (e, i))) for i in range(DMA_K)]
        for e in ENGS:
            c = 0
            nd = 0
            for o in self.ops[e]:
                if o.dma:
                    o.sem = dsems[e][nd % DMA_K]
                    o.val = 16 * (nd // DMA_K + 1)
                    nd += 1
                elif o.sig:
                    c += 1
                    o.val = c
                    o.sem = sems[e]
        block = self.es.enter_context(nc.Block())

        def emit(name, engine):
            waited = {}
            for o in self.ops[name]:
                for p in o.deps:
                    k = id(p.sem)
                    if waited.get(k, 0) >= p.val:
                        continue
                    waited[k] = p.val
                    engine.wait_ge(p.sem, p.val)
                inst = o.fn(engine)
                if o.dma:
                    inst.then_inc(o.sem, 16)
                elif o.sig:
                    inst.then_inc(o.sem, 1)

        @block.tensor
        def _(eng):
            emit("pe", eng)

        @block.scalar
        def _(eng):
            emit("act", eng)

        @block.vector
        def _(eng):
            emit("dve", eng)

        @block.gpsimd
        def _(eng):
            emit("pool", eng)

        @block.sync
        def _(eng):
            emit("sp", eng)
            for e in ENGS:
                for o in self.dma_hist[e][-DMA_K:]:
                    eng.wait_ge(o.sem, o.val)


class Arena:
    def __init__(self, nc, es, cap):
        self.t = es.enter_context(nc.sbuf_tensor("arena", [128, cap], U8))
        self.cap = cap
        self.live = []

    def take(self, name, off, fshape, dt, nres=1):
        esz = 2 if dt == BF16 else 4
        size = int(np.prod(fshape)) * esz
        assert off % 4 == 0 and off + size <= self.cap, (name, off, size, self.cap)
        inh = []
        seen = set()
        for (o, s, rl) in self.live:
            if o < off + size and off < o + s:
                for r in rl:
                    for q in ([r.w] if r.w is not None else []) + list(r.rs.values()):
                        if id(q) not in seen:
                            seen.add(id(q))
                            inh.append(q)
        rl = [R("%s%d" % (name, i)) for i in range(nres)]
        for r in rl:
            for i, o in enumerate(inh):
                r.rs[("inh", i)] = o
        self.live.append((off, size, rl))
        ap = self.t[:, off:off + size].bitcast(dt)
        if len(fshape) == 2:
            ap = ap.rearrange("p (a b) -> p a b", a=fshape[0])
        elif len(fshape) == 3:
            ap = ap.rearrange("p (a b c) -> p a b c", a=fshape[0], b=fshape[1])
        return ap, rl


def build(T, dbg=False):
    NT = T // 512
    NS = T // 128
    nc = bass.Bass("TRN2", target_bir_lowering=False)

    def din(name, shape, dt=F32):
        return nc.dram_tensor(name, shape, dt, kind="ExternalInput").ap()

    skind = "ExternalOutput" if dbg else "Internal"

    def dscr(name, shape, dt=F32):
        return nc.dram_tensor(name, shape, dt, kind=skind).ap()

    x_d = din("x", [T, D]); p_d = din("p", [T, PLE]); pos_d = din("pos", [64, T], I32)
    cols_d = din("cols", [128, NCOL]); ident_d = din("ident", [128, 128]); bc_d = din("bc", [128, 2 * D])
    f1g_d = din("ffn1_w_gate", [D, DFF]); f1u_d = din("ffn1_w_up", [D, DFF]); f1d_d = din("ffn1_w_down", [DFF, D])
    f2g_d = din("ffn2_w_gate", [D, DFF]); f2u_d = din("ffn2_w_up", [D, DFF]); f2d_d = din("ffn2_w_down", [DFF, D])
    win_d = din("w_in", [D, 4800])
    wr_d = din("lru_w_r", [16, 128, 128]); wi_d = din("lru_w_i", [16, 128, 128])
    wlo_d = din("w_lru_out", [D, D]); wuq_d = din("w_uq", [QL, 1536]); wukv_d = din("w_ukv", [KVL, 2048])
    wmo_d = din("w_mla_out", [D, D]); wo_d = din("w_o", [D, D])
    wpg_d = din("ple_w_gate", [D, D]); wpp_d = din("ple_w_proj", [PLE, D])
    out_d = nc.dram_tensor("out", [T, D], F32, kind="ExternalOutput").ap()
    h_s = dscr("h_s", [T, D])
    z_s = dscr("z_s", [2 * D, T])
    o_s = dscr("o_s", [D, T], BF16)
    ya_s = dscr("ya_s", [D, T], BF16)
    rope_s = dscr("rope_s", [2, 64, T])

    es = contextlib.ExitStack()
    with es:
        S = Sched(nc, es)
        CAP = 212000
        A = Arena(nc, es, CAP)
        pst = [es.enter_context(nc.psum_tensor("ps%d" % i, [128, 512], F32)) for i in range(8)]
        psR = [R("ps%d" % i) for i in range(8)]
        ps_ctr = {}

        def PS(lo=0, hi=8):
            c = ps_ctr.get((lo, hi), 0)
            ps_ctr[(lo, hi)] = c + 1
            i = lo + c % (hi - lo)
            return pst[i], psR[i]

        hR = [R("h%d" % i) for i in range(NT * 4)]
        zR = [[R("z%d_%d" % (c, i)) for i in range(NT)] for c in range(16)]
        oR = [[R("o%d_%d" % (h, i)) for i in range(NT)] for h in range(NH)]
        yaR = [[R("ya%d_%d" % (c, i)) for i in range(NT)] for c in range(8)]

        CB = CAP - 2304
        cols, (colsR,) = A.take("cols", CB, [NCOL], F32)
        ident, (identR,) = A.take("ident", CB + 544, [128], BF16)
        ones, (onesR,) = A.take("ones", CB + 800, [128], BF16)
        stat, statR = A.take("stat", CB + 1056, [64], F32, nres=8)
        S.dma("sp", lambda e: e.dma_start(out=cols, in_=cols_d), w=[colsR])
        S.dma("pool", lambda e: e.dma_start(out=ident, in_=ident_d), w=[identR])
        S.op("dve", lambda e: e.memset(ones, 1.0), w=[onesR])
        ones32, (ones32R,) = A.take("ones32", CB + 1312, [128], F32)
        S.op("dve", lambda e: e.memset(ones32, 1.0), w=[ones32R])

        def col(i, n=128):
            return cols[0:n, i:i + 1]

        def load_w(dst, dstR, src_rows, ncols_lo, ncols_hi, nk):
            for k in range(nk):
                S.dma("pool", lambda e, k=k: e.dma_start(
                    out=dst[:, k, :], in_=src_rows[k * 128:(k + 1) * 128, ncols_lo:ncols_hi]), w=[dstR[k]])

        def rstd_op(ssc, sdc, rsc, width, rr, mode=None):
            if (mode or RSTD_MODE) == "pow":
                S.op("dve", lambda e: e.tensor_scalar(out=sdc, in0=ssc, scalar1=1.0 / width, scalar2=EPS, op0=ALU.mult,
                                                      op1=ALU.add), r=[rr], w=[rr])
                S.op("pool", lambda e: e.tensor_tensor(out=rsc, in0=sdc, in1=col(133), op=ALU.pow), r=[rr, colsR], w=[rr])
            else:
                S.op("act", lambda e: e.activation(out=sdc, in_=ssc, func=AF.Sqrt, scale=1.0 / width, bias=col(127)),
                     r=[rr, colsR], w=[rr])
                S.op("dve", lambda e: e.reciprocal(out=rsc, in_=sdc), r=[rr], w=[rr])

        def rms_T(xt, xtR, sl, s, gcol0, xs, xsR, width=D, mode=None):
            xin = xt[:, sl, :]
            ssc = stat[:, s:s + 1]
            sdc = stat[:, 8 + s:9 + s]
            rsc = stat[:, 16 + s:17 + s]
            S.op("act", lambda e: e.activation(out=xs[:, s, 0:width], in_=xin, func=AF.Square, accum_out=ssc),
                 r=[xtR[sl]], w=[xsR[s], statR[s]])
            rstd_op(ssc, sdc, rsc, width, statR[s], mode)
            S.op("dve", lambda e: e.tensor_scalar(out=xs[:, s, 0:width], in0=xin, scalar1=rsc, scalar2=None,
                                                  op0=ALU.mult), r=[xtR[sl], statR[s]], w=[xsR[s]])

        def transposes(xs, xsR, nk, gcol0, dstT, dstTR, evac_engs=("act", "dve"), c0=0):
            if c0 or True:
                dstT = dstT[:, :, c0:c0 + 512]
            for k in range(nk):
                ps, pr = PS()
                for s in range(4):
                    S.op("pe", lambda e, s=s, k=k, ps=ps: e.matmul(
                        ps[:, s * 128:(s + 1) * 128], lhsT=xs[:, s, k * 128:(k + 1) * 128], rhs=ident,
                        start=True, stop=True), r=[xsR[s], identR], w=[pr])
                eng = evac_engs[k % len(evac_engs)]
                g = col(gcol0 + k)
                if eng == "act":
                    S.op("act", lambda e, k=k, ps=ps, g=g: e.activation(out=dstT[:, k, :], in_=ps[:, :],
                                                                      func=AF.Copy, scale=g),
                         r=[pr, colsR], w=[dstTR[k]])
                else:
                    S.op("dve", lambda e, k=k, ps=ps, g=g: e.tensor_scalar(out=dstT[:, k, :], in0=ps[:, :], scalar1=g,
                                                                         scalar2=None, op0=ALU.mult),
                         r=[pr, colsR], w=[dstTR[k]])

        def ffn_phase(wg_d, wu_d, wd_d, gcol0, src_d, srcR, dst_d, dstR, pre_hook=None):
            Wg, WgR = A.take("Wg", 0, [8, DFF], BF16, nres=8)
            Wu, WuR = A.take("Wu", 45056, [8, DFF], BF16, nres=8)
            Wd, WdR = A.take("Wd", 90112, [NJ, D], BF16, nres=NJ)
            o0 = 135168
            NSL = 8
            xt, xtR = A.take("xt", o0, [NSL, D], F32, nres=NSL); o0 += NSL * 4096
            xs, xsR = A.take("xs", o0, [4, D], BF16, nres=4); o0 += 8192
            xnT, xnTR = A.take("xnT", o0, [8, 512], BF16, nres=8); o0 += 8192
            o_hff = o0
            wg_v = wg_d.rearrange("(k p) n -> p k n", p=128)
            wu_v = wu_d.rearrange("(k p) n -> p k n", p=128)
            for blk in range(6):
                lo, hi = blk * 512, min(DFF, blk * 512 + 512)
                S.dma("pool", lambda e, lo=lo, hi=hi: e.dma_start(out=Wg[:, :, lo:hi], in_=wg_v[:, :, lo:hi]), w=[WgR[blk]])
                S.dma("pool", lambda e, lo=lo, hi=hi: e.dma_start(out=Wu[:, :, lo:hi], in_=wu_v[:, :, lo:hi]), w=[WuR[blk]])
            load_w(Wd, WdR, wd_d, 0, D, NJ)
            slot = [0]

            def load_tile(i):
                sls = []
                for s in range(4):
                    sl = slot[0] % NSL
                    slot[0] += 1
                    r0 = i * 512 + s * 128
                    S.dma("sp", lambda e, sl=sl, r0=r0: e.dma_start(out=xt[:, sl, :], in_=src_d[r0:r0 + 128, :]),
                          r=[srcR[i * 4 + s]], w=[xtR[sl]])
                    sls.append(sl)
                return sls

            def front(i, sls):
                for s in range(4):
                    rms_T(xt, xtR, sls[s], s, gcol0, xs, xsR)
                transposes(xs, xsR, 8, gcol0, xnT, xnTR)

            def gate_up(i):
                for j in range(NJ):
                    pg, pgR = PS()
                    pu, puR = PS()
                    for k in range(8):
                        S.op("pe", lambda e, k=k, j=j, pg=pg: e.matmul(
                            pg[:, :], lhsT=Wg[:, k, j * 128:(j + 1) * 128], rhs=xnT[:, k, :],
                            start=(k == 0), stop=(k == 7)), r=[WgR[j // 4], xnTR[k]], w=[pgR])
                    for k in range(8):
                        S.op("pe", lambda e, k=k, j=j, pu=pu: e.matmul(
                            pu[:, :], lhsT=Wu[:, k, j * 128:(j + 1) * 128], rhs=xnT[:, k, :],
                            start=(k == 0), stop=(k == 7)), r=[WuR[j // 4], xnTR[k]], w=[puR])
                    b = j % 2
                    S.op("act", lambda e, b=b, pg=pg: e.activation(out=sg[:, b, :], in_=pg[:, :], func=AF.Silu),
                         r=[pgR], w=[sgR[b]])
                    S.op("dve", lambda e, b=b, j=j, pu=pu: e.tensor_tensor(out=hff[:, j, :], in0=pu[:, :], in1=sg[:, b, :],
                                                                         op=ALU.mult),
                         r=[puR, sgR[b]], w=[hffR[j]])

            def down(i, sls):
                for s in range(4):
                    sl = sls[s]
                    for c in range(2):
                        pd, pdR = PS()
                        for j in range(NJ):
                            S.op("pe", lambda e, j=j, s=s, c=c, pd=pd: e.matmul(
                                pd[:, :], lhsT=hff[:, j, s * 128:(s + 1) * 128], rhs=Wd[:, j, c * 512:(c + 1) * 512],
                                start=(j == 0), stop=(j == NJ - 1)), r=[hffR[j], WdR[j]], w=[pdR])
                        S.op("dve", lambda e, sl=sl, c=c, pd=pd: e.scalar_tensor_tensor(
                            out=xt[:, sl, c * 512:(c + 1) * 512], in0=pd[:, :], scalar=0.5,
                            in1=xt[:, sl, c * 512:(c + 1) * 512], op0=ALU.mult, op1=ALU.add),
                             r=[pdR, xtR[sl]], w=[xtR[sl]])
                    r0 = i * 512 + s * 128
                    S.dma("sp", lambda e, sl=sl, r0=r0: e.dma_start(out=dst_d[r0:r0 + 128, :], in_=xt[:, sl, :]),
                          r=[xtR[sl]], w=[dstR[i * 4 + s]])

            sl_cur = load_tile(0)
            front(0, sl_cur)
            if pre_hook is not None:
                pre_hook()
            o0 = o_hff
            hff, hffR = A.take("hff", o0, [NJ, 512], BF16, nres=NJ); o0 += NJ * 1024
            sg, sgR = A.take("sg", o0, [2, 512], BF16, nres=2); o0 += 2048
            assert o0 <= CB, (o0, CB)
            for i in range(NT):
                gate_up(i)
                if i + 1 < NT:
                    sl_nxt = load_tile(i + 1)
                    front(i + 1, sl_nxt)
                down(i, sl_cur)
                if i + 1 < NT:
                    sl_cur = sl_nxt

        ropeR = [R("rope%d" % i) for i in range(2 * NT)]

        def rope_tables():
            base = 184320
            sets = []
            for b in range(2):
                o0 = base + b * 12288
                bufs = []
                for nm, dt in (("posi", I32), ("ang", F32), ("ki", I32), ("kf", F32), ("rc", F32), ("rs", F32)):
                    ap, (rr,) = A.take("%s%d" % (nm, b), o0, [512], dt)
                    o0 += 2048
                    bufs.append((ap, rr))
                sets.append(bufs)

            def reduce_and_sin(ang, angR, ki, kiR, kf, kfR, dst, dstR, offs, scale):
                S.op("dve", lambda e: e.tensor_scalar(out=dst[0:64, :], in0=ang[0:64, :], scalar1=col(125, 64),
                                                      scalar2=offs, op0=ALU.mult, op1=ALU.add),
                     r=[angR, colsR], w=[dstR])
                S.op("dve", lambda e: e.scalar_tensor_tensor(out=dst[0:64, :], in0=ang[0:64, :], scalar=col(134, 64),
                                                             in1=dst[0:64, :], op0=ALU.mult, op1=ALU.add),
                     r=[angR, colsR, dstR], w=[dstR])
                S.op("dve", lambda e: e.tensor_copy(out=ki[0:64, :], in_=dst[0:64, :]), r=[dstR], w=[kiR])
                S.op("dve", lambda e: e.tensor_copy(out=kf[0:64, :], in_=ki[0:64, :]), r=[kiR], w=[kfR])
                S.op("dve", lambda e: e.tensor_tensor(out=dst[0:64, :], in0=dst[0:64, :], in1=kf[0:64, :], op=ALU.subtract),
                     r=[dstR, kfR], w=[dstR])
                S.op("dve", lambda e: e.tensor_single_scalar(out=kf[0:64, :], in_=dst[0:64, :], scalar=0.5, op=ALU.is_gt),
                     r=[dstR], w=[kfR])
                S.op("dve", lambda e: e.tensor_tensor(out=dst[0:64, :], in0=dst[0:64, :], in1=kf[0:64, :], op=ALU.subtract),
                     r=[dstR, kfR], w=[dstR])
                S.op("act", lambda e: e.activation(out=dst[0:64, :], in_=dst[0:64, :], func=AF.Sin, scale=scale),
                     r=[dstR, colsR], w=[dstR])

            def piece(i):
                (posi, posiR), (ang, angR), (ki, kiR), (kf, kfR), (rc, rcR), (rs, rsR) = sets[i % 2]
                c0 = i * 512
                S.dma("sp", lambda e: e.dma_start(out=posi[0:64, :], in_=pos_d[:, c0:c0 + 512]), w=[posiR])
                S.op("dve", lambda e: e.tensor_copy(out=ang[0:64, :], in_=posi[0:64, :]), r=[posiR], w=[angR])
                reduce_and_sin(ang, angR, ki, kiR, kf, kfR, rc, rcR, 0.25, TWO_PI)
                reduce_and_sin(ang, angR, ki, kiR, kf, kfR, rs, rsR, 0.0, col(128, 64))
                S.dma("sp", lambda e: e.dma_start(out=rope_s[0, :, c0:c0 + 512], in_=rc[0:64, :]), r=[rcR], w=[ropeR[i]])
                S.dma("sp", lambda e: e.dma_start(out=rope_s[1, :, c0:c0 + 512], in_=rs[0:64, :]), r=[rsR], w=[ropeR[NT + i]])

            for i in range(NT):
                piece(i)


        xR_in = [R("xin%d" % i) for i in range(NT * 4)]
        ffn_phase(f1g_d, f1u_d, f1d_d, 0, x_d, xR_in, h_s, hR, pre_hook=rope_tables)


        PB = CB - 14 * T - 8 * T
        cqnT, cqnTR = A.take("cqnT", PB, [3, T], BF16, nres=3 * NT)
        ckvnT, ckvnTR = A.take("ckvnT", PB + 6 * T, [2, T], BF16, nres=2 * NT)
        kpeT, kpeTR = A.take("kpeT", PB + 10 * T, [T], BF16, nres=NT)
        cosT, (cosR,) = A.take("cosT", PB + 14 * T, [T], F32)
        sinT, (sinR,) = A.take("sinT", PB + 18 * T, [T], F32)

        def phase_b():
            S.dma("sp", lambda e: e.dma_start(out=cosT[0:64, :], in_=rope_s[0]), r=ropeR, w=[cosR])
            S.dma("sp", lambda e: e.dma_start(out=sinT[0:64, :], in_=rope_s[1]), r=ropeR, w=[sinR])
            Win, WinR = A.take("Win", 0, [8, 2752], BF16, nres=8)
            Wsw, (WswR,) = A.take("Wsw", 44032, [8, 64], BF16)
            o0 = 44032 + 1024
            NSL = 8
            xt, xtR = A.take("xt", o0, [NSL, D], F32, nres=NSL); o0 += NSL * 4096
            xs, xsR = A.take("xs", o0, [4, D], BF16, nres=4); o0 += 8192
            uT2, uT2R = A.take("uT", o0, [2, 8, 512], BF16, nres=16); o0 += 16384
            zt, ztR = A.take("zt", o0, [3, 512], F32, nres=3); o0 += 6144
            cqs, cqsR = A.take("cqs", o0, [4, QL], BF16, nres=4); o0 += 4 * QL * 2
            cks, cksR = A.take("cks", o0, [4, KVL], BF16, nres=4); o0 += 4 * KVL * 2
            tmp, tmpR = A.take("tmpB", o0, [2, 512], F32, nres=2); o0 += 4096
            assert o0 <= PB, (o0, PB)
            load_w(Win, WinR, win_d, 0, 2752, 8)
            S.op("dve", lambda e: e.tensor_copy(out=Wsw[:, :, 0:32], in_=Win[:, :, 2720:2752]), r=WinR, w=[WswR])
            S.op("dve", lambda e: e.tensor_copy(out=Wsw[:, :, 32:64], in_=Win[:, :, 2688:2720]), r=WinR, w=[WswR])
            slot = [0]
            for i in range(NT):
                S.op("pool", lambda e, i=i: e.memset(kpeT[64:128, i * 512:(i + 1) * 512], 0.0), w=[kpeTR[i]])

            def load_tile(i):
                sls = []
                for s in range(4):
                    sl = slot[0] % NSL
                    slot[0] += 1
                    r0 = i * 512 + s * 128
                    S.dma("sp", lambda e, sl=sl, r0=r0: e.dma_start(out=xt[:, sl, :], in_=h_s[r0:r0 + 128, :]),
                          r=[hR[i * 4 + s]], w=[xtR[sl]])
                    sls.append(sl)
                return sls

            def front_a(i, sls):
                for s in range(4):
                    rms_T(xt, xtR, sls[s], s, 8, xs, xsR)

            def front_b(i):
                transposes(xs, xsR, 8, 8, uT2[:, i % 2], uT2R[(i % 2) * 8:(i % 2) * 8 + 8])

            zb = [0]

            def body(i, part):
                c0 = i * 512
                uT = uT2[:, i % 2]
                uTR = uT2R[(i % 2) * 8:(i % 2) * 8 + 8]
                if part == 1:
                    body1(i, c0, uT, uTR)
                else:
                    body2(i, c0, uT, uTR)

            def body1(i, c0, uT, uTR):
                for c in range(16):
                    ps, pr = PS()
                    for k in range(8):
                        S.op("pe", lambda e, k=k, c=c, ps=ps: e.matmul(
                            ps[:, :], lhsT=Win[:, k, c * 128:(c + 1) * 128], rhs=uT[:, k, :],
                            start=(k == 0), stop=(k == 7)), r=[WinR[k], uTR[k]], w=[pr])
                    b = zb[0] % 3
                    zb[0] += 1
                    eng = "act" if c % 2 == 0 else "dve"
                    if eng == "act":
                        S.op("act", lambda e, b=b, ps=ps: e.activation(out=zt[:, b, :], in_=ps[:, :], func=AF.Copy),
                             r=[pr], w=[ztR[b]])
                    else:
                        S.op("dve", lambda e, b=b, ps=ps: e.tensor_copy(out=zt[:, b, :], in_=ps[:, :]), r=[pr], w=[ztR[b]])
                    S.dma("sp", lambda e, b=b, c=c, c0=c0: e.dma_start(out=z_s[c * 128:(c + 1) * 128, c0:c0 + 512],
                                                                     in_=zt[:, b, :]), r=[ztR[b]], w=[zR[c][i]])

            def body2(i, c0, uT, uTR):
                p1, p1R = PS()
                p2, p2R = PS()
                for k in range(8):
                    S.op("pe", lambda e, k=k, p1=p1: e.matmul(p1[0:64, :], lhsT=Win[:, k, 2688:2752], rhs=uT[:, k, :],
                                                            start=(k == 0), stop=(k == 7)), r=[WinR[k], uTR[k]], w=[p1R])
                for k in range(8):
                    S.op("pe", lambda e, k=k, p2=p2: e.matmul(p2[0:64, :], lhsT=Wsw[:, k, :], rhs=uT[:, k, :],
                                                            start=(k == 0), stop=(k == 7)), r=[WswR, uTR[k]], w=[p2R])
                S.op("dve", lambda e, p1=p1: e.tensor_tensor(out=tmp[0:64, 0, :], in0=p1[0:64, :], in1=cosT[0:64, c0:c0 + 512],
                                                           op=ALU.mult), r=[p1R, cosR], w=[tmpR[0]])
                S.op("dve", lambda e, p2=p2: e.tensor_tensor(out=tmp[0:64, 1, :], in0=p2[0:64, :], in1=sinT[0:64, c0:c0 + 512],
                                                           op=ALU.mult), r=[p2R, sinR], w=[tmpR[1]])
                S.op("dve", lambda e: e.tensor_tensor(out=kpeT[0:64, c0:c0 + 512], in0=tmp[0:64, 0, :], in1=tmp[0:64, 1, :],
                                                      op=ALU.add), r=[tmpR[0], tmpR[1]], w=[kpeTR[i]])
                for s in range(4):
                    pq, pqR = PS()
                    pk, pkR = PS()
                    for k in range(8):
                        S.op("pe", lambda e, k=k, s=s, pq=pq: e.matmul(
                            pq[:, 0:QL], lhsT=uT[:, k, s * 128:(s + 1) * 128], rhs=Win[:, k, 2048:2048 + QL],
                            start=(k == 0), stop=(k == 7)), r=[WinR[k], uTR[k]], w=[pqR])
                    for k in range(8):
                        S.op("pe", lambda e, k=k, s=s, pk=pk: e.matmul(
                            pk[:, 0:KVL], lhsT=uT[:, k, s * 128:(s + 1) * 128], rhs=Win[:, k, 2432:2432 + KVL],
                            start=(k == 0), stop=(k == 7)), r=[WinR[k], uTR[k]], w=[pkR])
                    for (pp, ppR, dst, dstR, wd) in ((pq, pqR, cqs, cqsR, QL), (pk, pkR, cks, cksR, KVL)):
                        ssc = stat[:, 24 + s:25 + s]
                        sdc = stat[:, 32 + s:33 + s]
                        rsc = stat[:, 40 + s:41 + s]
                        S.op("act", lambda e, pp=pp, dst=dst, wd=wd, ssc=ssc, s=s: e.activation(
                            out=dst[:, s, :], in_=pp[:, 0:wd], func=AF.Square, accum_out=ssc),
                             r=[ppR], w=[dstR[s], statR[s]])
                        rstd_op(ssc, sdc, rsc, wd, statR[s])
                        S.op("dve", lambda e, pp=pp, dst=dst, wd=wd, rsc=rsc, s=s: e.tensor_scalar(
                            out=dst[:, s, :], in0=pp[:, 0:wd], scalar1=rsc, scalar2=None, op0=ALU.mult),
                             r=[ppR, statR[s]], w=[dstR[s]])
                transposes(cqs, cqsR, 3, 32, cqnT, [cqnTR[k * NT + i] for k in range(3)], c0=c0)
                transposes(cks, cksR, 2, 35, ckvnT, [ckvnTR[k * NT + i] for k in range(2)], c0=c0)

            sl_cur = load_tile(0)
            front_a(0, sl_cur)
            front_b(0)
            for i in range(NT):
                if i + 1 < NT:
                    sl_cur = load_tile(i + 1)
                    front_a(i + 1, sl_cur)
                body(i, 1)
                if i + 1 < NT:
                    front_b(i + 1)
                body(i, 2)

        phase_b()

        def phase_d():
            Wuq, WuqR = A.take("Wuq", 0, [3, 1536], BF16, nres=3)
            Wqs, (WqsR,) = A.take("Wqs", 9216, [3, NH, 64], BF16)
            Wukv, WukvR = A.take("Wukv", 12288, [2, 2048], BF16, nres=2)
            o0 = 20480
            hb = []
            for b in range(2):
                qn, qnR = A.take("qn%d" % b, o0, [T], BF16, nres=NT); o0 += 2 * T
                qp, qpR = A.take("qp%d" % b, o0, [T], BF16, nres=NT); o0 += 2 * T
                kn, knR = A.take("kn%d" % b, o0, [T], BF16, nres=NT); o0 += 2 * T
                vv, vvR = A.take("vv%d" % b, o0, [NS, 128], BF16, nres=NT); o0 += 2 * T
                hb.append((qn, qnR, qp, qpR, kn, knR, vv, vvR))
                for i in range(NT):
                    S.op("pool", lambda e, i=i, qp=qp: e.memset(qp[64:128, i * 512:(i + 1) * 512], 0.0), w=[qpR[i]])
            NPT = 6
            PT, PTR = A.take("PT", o0, [NPT, 512], BF16, nres=NPT); o0 += NPT * 1024
            acc, accR = A.take("acc", o0, [2, 2, 512], F32, nres=4); o0 += 8192
            tmp, tmpR = A.take("tmpD", o0, [2, 512], F32, nres=2); o0 += 4096
            rl, rlR = A.take("rl", o0, [2, 512], F32, nres=2); o0 += 4096
            ot, otR = A.take("ot", o0, [2, 512], BF16, nres=2); o0 += 2048
            assert o0 <= PB, (o0, PB)
            load_w(Wuq, WuqR, wuq_d, 0, 1536, 3)
            load_w(Wukv, WukvR, wukv_d, 0, 2048, 2)
            Wuq4 = Wuq.rearrange("p k (h d) -> p k h d", h=NH)
            S.op("dve", lambda e: e.tensor_copy(out=Wqs[:, :, :, 0:32], in_=Wuq4[:, :, :, 160:192]), r=WuqR, w=[WqsR])
            S.op("dve", lambda e: e.tensor_copy(out=Wqs[:, :, :, 32:64], in_=Wuq4[:, :, :, 128:160]), r=WuqR, w=[WqsR])

            def prep(h):
                for i in range(NT):
                    prep_tile(h, i)

            def prep_tile(h, i):
                qn, qnR, qp, qpR, kn, knR, vv, vvR = hb[h % 2]
                if True:
                    c0 = i * 512
                    lat_q = [cqnTR[k * NT + i] for k in range(3)]
                    lat_k = [ckvnTR[k * NT + i] for k in range(2)]
                    ps, pr = PS(0, 4)
                    for k in range(3):
                        S.op("pe", lambda e, k=k, ps=ps: e.matmul(
                            ps[:, :], lhsT=Wuq[:, k, h * 192:h * 192 + 128], rhs=cqnT[:, k, c0:c0 + 512],
                            start=(k == 0), stop=(k == 2)), r=[WuqR[k], lat_q[k]], w=[pr])
                    S.op("act", lambda e, ps=ps: e.activation(out=qn[:, c0:c0 + 512], in_=ps[:, :], func=AF.Copy),
                         r=[pr], w=[qnR[i]])
                    p1, p1R = PS(0, 4)
                    p2, p2R = PS(0, 4)
                    for k in range(3):
                        S.op("pe", lambda e, k=k, p1=p1: e.matmul(
                            p1[0:64, :], lhsT=Wuq[:, k, h * 192 + 128:h * 192 + 192], rhs=cqnT[:, k, c0:c0 + 512],
                            start=(k == 0), stop=(k == 2)), r=[WuqR[k], lat_q[k]], w=[p1R])
                    for k in range(3):
                        S.op("pe", lambda e, k=k, p2=p2: e.matmul(
                            p2[0:64, :], lhsT=Wqs[:, k, h, :], rhs=cqnT[:, k, c0:c0 + 512],
                            start=(k == 0), stop=(k == 2)), r=[WqsR, lat_q[k]], w=[p2R])
                    S.op("dve", lambda e, p1=p1: e.tensor_tensor(out=tmp[0:64, 0, :], in0=p1[0:64, :],
                                                               in1=cosT[0:64, c0:c0 + 512], op=ALU.mult),
                         r=[p1R, cosR], w=[tmpR[0]])
                    S.op("dve", lambda e, p2=p2: e.tensor_tensor(out=tmp[0:64, 1, :], in0=p2[0:64, :],
                                                               in1=sinT[0:64, c0:c0 + 512], op=ALU.mult),
                         r=[p2R, sinR], w=[tmpR[1]])
                    S.op("dve", lambda e: e.tensor_tensor(out=qp[0:64, c0:c0 + 512], in0=tmp[0:64, 0, :],
                                                          in1=tmp[0:64, 1, :], op=ALU.add),
                         r=[tmpR[0], tmpR[1]], w=[qpR[i]])
                    pk, pkR = PS(0, 4)
                    for k in range(2):
                        S.op("pe", lambda e, k=k, pk=pk: e.matmul(
                            pk[:, :], lhsT=Wukv[:, k, h * 256:h * 256 + 128], rhs=ckvnT[:, k, c0:c0 + 512],
                            start=(k == 0), stop=(k == 1)), r=[WukvR[k], lat_k[k]], w=[pkR])
                    S.op("act", lambda e, pk=pk: e.activation(out=kn[:, c0:c0 + 512], in_=pk[:, :], func=AF.Copy),
                         r=[pkR], w=[knR[i]])
                    pv, pvR = PS(0, 4)
                    for n in range(4):
                        for k in range(2):
                            S.op("pe", lambda e, k=k, n=n, pv=pv: e.matmul(
                                pv[:, n * 128:(n + 1) * 128], lhsT=ckvnT[:, k, c0 + n * 128:c0 + (n + 1) * 128],
                                rhs=Wukv[:, k, h * 256 + 128:h * 256 + 256], start=(k == 0), stop=(k == 1)),
                                 r=[WukvR[k], lat_k[k]], w=[pvR])
                    S.op("dve", lambda e, pv=pv: e.tensor_copy(
                        out=vv[:, i * 4:(i + 1) * 4, :], in_=pv[:, :].rearrange("p (n d) -> p n d", n=4)),
                         r=[pvR], w=[vvR[i]])

            ptc = [0]

            def attend(h):
                for qi in range(NT):
                    attend_q(h, qi)

            def attend_q(h, qi):
                qn, qnR, qp, qpR, kn, knR, vv, vvR = hb[h % 2]
                if True:
                    q0 = qi * 512
                    pO, pOR = PS(4, 6)
                    pL, pLR = PS(6, 8)

                    def scores(kj):
                        ps, pr = PS(0, 4)
                        ki = kj // 4
                        S.op("pe", lambda e, ps=ps: e.matmul(ps[:, :], lhsT=kn[:, kj * 128:(kj + 1) * 128],
                                                           rhs=qn[:, q0:q0 + 512], start=True, stop=False),
                             r=[knR[ki], qnR[qi]], w=[pr])
                        S.op("pe", lambda e, ps=ps: e.matmul(ps[:, :], lhsT=kpeT[:, kj * 128:(kj + 1) * 128],
                                                           rhs=qp[:, q0:q0 + 512], start=False, stop=True),
                             r=[kpeTR[ki], qpR[qi]], w=[pr])
                        b = ptc[0] % NPT
                        ptc[0] += 1
                        S.op("act", lambda e, ps=ps, b=b: e.activation(out=PT[:, b, :], in_=ps[:, :], func=AF.Exp,
                                                                     scale=SM_SCALE), r=[pr], w=[PTR[b]])
                        return b

                    LA = 2
                    pend = [scores(j) for j in range(min(LA, NS))]
                    ab = qi % 2
                    nd = [0]
                    for kj in range(NS):
                        b = pend.pop(0)
                        if kj + LA < NS:
                            pend.append(scores(kj + LA))
                        S.op("pe", lambda e, b=b, kj=kj, pO=pO: e.matmul(
                            pO[:, :], lhsT=vv[:, kj, :], rhs=PT[:, b, :], start=(kj == 0), stop=(kj == NS - 1)),
                             r=[vvR[kj // 4], PTR[b]], w=[pOR])
                        on_pe = (DEN_MODE == "pe") or (DEN_MODE == "hybrid" and kj % PE_EVERY == PE_EVERY - 1)
                        if on_pe:
                            first_pe = (kj == (0 if DEN_MODE == "pe" else PE_EVERY - 1))
                            last_pe = (kj == NS - 1) and DEN_MODE == "pe"
                            S.op("pe", lambda e, b=b, pL=pL, first_pe=first_pe, last_pe=last_pe: e.matmul(
                                pL[:, :], lhsT=ones, rhs=PT[:, b, :], start=first_pe, stop=last_pe),
                                 r=[onesR, PTR[b]], w=[pLR])
                            continue
                        par = nd[0] % 2
                        first = nd[0] < 2
                        nd[0] += 1
                        if first:
                            S.op("dve", lambda e, b=b, par=par: e.tensor_copy(out=acc[:, ab, par, :], in_=PT[:, b, :]),
                                 r=[PTR[b]], w=[accR[ab * 2 + par]])
                        else:
                            S.op("dve", lambda e, b=b, par=par: e.tensor_tensor(out=acc[:, ab, par, :], in0=acc[:, ab, par, :],
                                                                             in1=PT[:, b, :], op=ALU.add),
                                 r=[PTR[b], accR[ab * 2 + par]], w=[accR[ab * 2 + par]])
                    if DEN_MODE != "pe":
                        S.op("dve", lambda e: e.tensor_tensor(out=acc[:, ab, 0, :], in0=acc[:, ab, 0, :], in1=acc[:, ab, 1, :],
                                                              op=ALU.add), r=[accR[ab * 2], accR[ab * 2 + 1]], w=[accR[ab * 2]])
                        S.op("pe", lambda e, pL=pL: e.matmul(pL[:, :], lhsT=ones32, rhs=acc[:, ab, 0, :],
                                                           start=(DEN_MODE == "dve" or NS < PE_EVERY), stop=True),
                             r=[ones32R, accR[ab * 2]], w=[pLR])
                    ob = qi % 2
                    S.op("dve", lambda e, ob=ob, pL=pL: e.reciprocal(out=rl[:, ob, :], in_=pL[:, :]), r=[pLR], w=[rlR[ob]])
                    S.op("dve", lambda e, ob=ob, pO=pO: e.tensor_tensor(out=ot[:, ob, :], in0=pO[:, :], in1=rl[:, ob, :],
                                                                       op=ALU.mult), r=[pOR, rlR[ob]], w=[otR[ob]])
                    S.dma("sp", lambda e, ob=ob: e.dma_start(out=o_s[h * 128:(h + 1) * 128, q0:q0 + 512], in_=ot[:, ob, :]),
                          r=[otR[ob]], w=[oR[h][qi]])

            prep(0)
            for h in range(NH):
                if h + 1 < NH:
                    prep(h + 1)
                attend(h)

        phase_d()

        def phase_c():
            Wr, (WrR,) = A.take("Wr", 0, [16, 128], BF16)
            Wi, (WiR,) = A.take("Wi", 4096, [16, 128], BF16)
            lc, (lcR,) = A.take("lc", 8192, [8, 16], F32)
            id32, (id32R,) = A.take("id32", 8704, [128], F32)
            dg, (dgR,) = A.take("dg", 9216, [32, 128], F32)
            o0 = 9216 + 16384
            TP = T + 4
            sets = []
            for b in range(2):
                zx, (zxR,) = A.take("zx%d" % b, o0, [TP], F32); o0 += 4 * TP
                zg, zgR = None, None
                xa, xaR = A.take("xa%d" % b, o0, [T], F32, nres=NT); o0 += 4 * T
                xab, xabR = A.take("xab%d" % b, o0, [T], BF16, nres=NT); o0 += 2 * T
                hf, hfR = A.take("hf%d" % b, o0, [T], F32, nres=NT); o0 += 4 * T
                sets.append((zx, zxR, zg, zgR, xa, xaR, xab, xabR, hf, hfR))
            GRP = 3
            NRG = 2 * GRP
            tA, tAR = A.take("tA", o0, [NRG, 512], F32, nres=NRG); o0 += NRG * 2048
            tB, tBR = A.take("tB", o0, [NRG, 512], F32, nres=NRG); o0 += NRG * 2048
            tM, tMR = A.take("tM", o0, [NRG, 512], F32, nres=NRG); o0 += NRG * 2048
            NR = 3
            tZ, tZR = A.take("tZ", o0, [NR, 512], F32, nres=NR); o0 += NR * 2048
            hbc, (hbcR,) = A.take("hbc", o0, [32], F32); o0 += 128
            NRH = GRP + 2
            tH, tHR = A.take("tH", o0, [NRH, 512], F32, nres=NRH); o0 += NRH * 2048
            tG, tGR = A.take("tG", o0, [NR, 512], F32, nres=NR); o0 += NR * 2048
            tS, tSR = A.take("tS", o0, [NR, 512], F32, nres=NR); o0 += NR * 2048
            tY, tYR = A.take("tY", o0, [NR, 512], BF16, nres=NR); o0 += NR * 1024
            assert o0 <= CB, (o0, CB)
            S.dma("sp", lambda e: e.dma_start(out=id32, in_=ident_d), w=[id32R])
            for j in range(32):
                S.op("dve", lambda e, j=j: e.tensor_scalar(out=dg[:, j, :], in0=id32, scalar1=col(37 + j), scalar2=None,
                                                           op0=ALU.mult), r=[id32R, colsR], w=[dgR])
            S.dma("pool", lambda e: e.dma_start(out=Wr, in_=wr_d.rearrange("g k m -> k g m")), w=[WrR])
            S.dma("pool", lambda e: e.dma_start(out=Wi, in_=wi_d.rearrange("g k m -> k g m")), w=[WiR])
            lam = cols[:, 109:125]
            L = lambda j: lc[:, j, :]
            S.op("dve", lambda e: e.tensor_scalar(out=L(0), in0=lam, scalar1=-1.0, scalar2=None, op0=ALU.mult),
                 r=[colsR], w=[lcR])
            S.op("dve", lambda e: e.tensor_tensor(out=L(0), in0=L(0), in1=lam, op=ALU.max), r=[colsR, lcR], w=[lcR])
            S.op("act", lambda e: e.activation(out=L(1), in_=L(0), func=AF.Exp, scale=-1.0), r=[lcR], w=[lcR])
            S.op("dve", lambda e: e.tensor_scalar(out=L(2), in0=L(1), scalar1=2.0, scalar2=None, op0=ALU.add), r=[lcR], w=[lcR])
            S.op("dve", lambda e: e.reciprocal(out=L(2), in_=L(2)), r=[lcR], w=[lcR])
            S.op("dve", lambda e: e.tensor_tensor(out=L(3), in0=L(1), in1=L(2), op=ALU.mult), r=[lcR], w=[lcR])
            S.op("dve", lambda e: e.tensor_tensor(out=L(4), in0=L(3), in1=L(3), op=ALU.mult), r=[lcR], w=[lcR])
            S.op("dve", lambda e: e.tensor_scalar(out=L(5), in0=L(4), scalar1=1.0 / 13, scalar2=1.0 / 11, op0=ALU.mult,
                                                  op1=ALU.add), r=[lcR], w=[lcR])
            for cst in (1.0 / 9, 1.0 / 7, 1.0 / 5, 1.0 / 3, 1.0):
                S.op("dve", lambda e: e.tensor_tensor(out=L(5), in0=L(5), in1=L(4), op=ALU.mult), r=[lcR], w=[lcR])
                S.op("dve", lambda e, cst=cst: e.tensor_scalar(out=L(5), in0=L(5), scalar1=cst, scalar2=None, op0=ALU.add),
                     r=[lcR], w=[lcR])
            S.op("dve", lambda e: e.tensor_tensor(out=L(5), in0=L(5), in1=L(3), op=ALU.mult), r=[lcR], w=[lcR])
            S.op("dve", lambda e: e.tensor_scalar(out=L(6), in0=lam, scalar1=-1.0, scalar2=0.0, op0=ALU.mult, op1=ALU.max),
                 r=[colsR], w=[lcR])
            S.op("dve", lambda e: e.scalar_tensor_tensor(out=L(6), in0=L(5), scalar=2.0, in1=L(6), op0=ALU.mult, op1=ALU.add),
                 r=[lcR], w=[lcR])
            S.op("dve", lambda e: e.tensor_scalar(out=L(7), in0=L(6), scalar1=-8.0, scalar2=None, op0=ALU.mult),
                 r=[lcR], w=[lcR])
            S.op("dve", lambda e: e.tensor_scalar(out=L(6), in0=L(6), scalar1=-4.0, scalar2=None, op0=ALU.mult),
                 r=[lcR], w=[lcR])
            S.op("dve", lambda e: e.tensor_scalar(out=hbc, in0=cols[:, 77:109], scalar1=0.5, scalar2=None, op0=ALU.mult),
                 r=[colsR], w=[hbcR])
            for b in range(2):
                zx, zxR = sets[b][0], sets[b][1]
                S.op("pool", lambda e, zx=zx: e.memset(zx[:, 0:2], 0.0), w=[zxR])
                S.op("pool", lambda e, zx=zx: e.memset(zx[:, T + 2:T + 4], 0.0), w=[zxR])

            def load(n):
                zx, zxR = sets[n % 2][0:2]
                S.dma("sp", lambda e: e.dma_start(out=zx[:, 2:T + 2], in_=z_s[n * 128:(n + 1) * 128, :]),
                      r=zR[n], w=[zxR])

            GC = 2.0 * math.sqrt(2.0 / math.pi)
            ctr = {"g": 0, "h": 0, "t": 0}

            def conv(n, i):
                zx, zxR, zg, zgR, xa, xaR, xab, xabR, hf, hfR = sets[n % 2]
                sl = slice(i * 512, (i + 1) * 512)
                c0 = i * 512
                ps, pr = PS()
                for k in range(4):
                    S.op("pe", lambda e, k=k: e.matmul(ps[:, :], lhsT=dg[:, n * 4 + k, :], rhs=zx[:, c0 + k:c0 + k + 512],
                                                      start=(k == 0), stop=(k == 3)), r=[dgR, zxR], w=[pr])
                S.op("act", lambda e: e.activation(out=xa[:, sl], in_=ps[:, :], func=AF.Identity, bias=col(69 + n)),
                     r=[pr, colsR], w=[xaR[i]])
                S.op("dve", lambda e: e.tensor_copy(out=xab[:, sl], in_=xa[:, sl]), r=[xaR[i]], w=[xabR[i]])

            def gates_group(n, d, tiles):
                zx, zxR, zg, zgR, xa, xaR, xab, xabR, hf, hfR = sets[n % 2]
                g = d * 8 + n
                slots = []
                pss = []
                for i in tiles:
                    sl = slice(i * 512, (i + 1) * 512)
                    s_ = ctr["g"] % NRG
                    ctr["g"] += 1
                    slots.append(s_)
                    pr_, prR = PS()
                    pi_, piR = PS()
                    pss.append((pr_, prR, pi_, piR))
                    S.op("pe", lambda e, pr_=pr_, sl=sl: e.matmul(pr_[:, :], lhsT=Wr[:, g, :], rhs=xab[:, sl], start=True, stop=True),
                         r=[WrR, xabR[i]], w=[prR])
                    S.op("pe", lambda e, pi_=pi_, sl=sl: e.matmul(pi_[:, :], lhsT=Wi[:, g, :], rhs=xab[:, sl], start=True, stop=True),
                         r=[WiR, xabR[i]], w=[piR])
                for (i, s_, (pr_, prR, pi_, piR)) in zip(tiles, slots, pss):
                    S.op("act", lambda e, pr_=pr_, s_=s_: e.activation(out=tA[:, s_, :], in_=pr_[:, :], func=AF.Tanh, scale=0.5,
                                                                       bias=hbc[:, g:g + 1]), r=[prR, hbcR], w=[tAR[s_]])
                    S.op("act", lambda e, pi_=pi_, s_=s_: e.activation(out=tB[:, s_, :], in_=pi_[:, :], func=AF.Tanh, scale=0.5,
                                                                       bias=hbc[:, 16 + g:17 + g]), r=[piR, hbcR], w=[tBR[s_]])
                for (i, s_) in zip(tiles, slots):
                    S.op("act", lambda e, s_=s_: e.activation(out=tM[:, s_, :], in_=tA[:, s_, :], func=AF.Exp,
                                                             scale=lc[:, 7, g:g + 1], bias=lc[:, 7, g:g + 1]),
                         r=[tAR[s_], lcR], w=[tMR[s_]])
                    S.op("act", lambda e, s_=s_: e.activation(out=tA[:, s_, :], in_=tA[:, s_, :], func=AF.Exp,
                                                             scale=lc[:, 6, g:g + 1], bias=lc[:, 6, g:g + 1]),
                         r=[tAR[s_], lcR], w=[tAR[s_]])
                for (i, s_) in zip(tiles, slots):
                    S.op("act", lambda e, s_=s_: e.activation(out=tM[:, s_, :], in_=tM[:, s_, :], func=AF.Sqrt, scale=-0.25,
                                                             bias=col(132)), r=[tMR[s_], colsR], w=[tMR[s_]])
                for (i, s_) in zip(tiles, slots):
                    sl = slice(i * 512, (i + 1) * 512)
                    S.op("dve", lambda e, s_=s_, sl=sl: e.tensor_tensor(out=tM[:, s_, :], in0=tM[:, s_, :], in1=xa[:, sl], op=ALU.mult),
                         r=[tMR[s_], xaR[i]], w=[tMR[s_]])
                    S.op("dve", lambda e, s_=s_: e.scalar_tensor_tensor(out=tB[:, s_, :], in0=tB[:, s_, :], scalar=1.0, in1=tM[:, s_, :],
                                                                       op0=ALU.add, op1=ALU.mult),
                         r=[tBR[s_], tMR[s_]], w=[tBR[s_]])
                return slots

            def scan_f(n, i, s_):
                hf, hfR = sets[n % 2][8], sets[n % 2][9]
                c0 = i * 512
                init = 0.0 if i == 0 else hf[:, c0 - 1:c0]
                rr = [tAR[s_], tBR[s_]] + ([hfR[i - 1]] if i > 0 else [])
                S.op("dve", lambda e: e.tensor_tensor_scan(out=hf[:, c0:c0 + 512], data0=tA[:, s_, :], data1=tB[:, s_, :],
                                                           initial=init, op0=ALU.mult, op1=ALU.add), r=rr, w=[hfR[i]])

            def scan_b(n, i, s_, prev_h):
                hs = ctr["h"] % NRH
                ctr["h"] += 1
                init = 0.0 if prev_h is None else tH[:, prev_h, 0:1]
                rr = [tAR[s_], tBR[s_]] + ([tHR[prev_h]] if prev_h is not None else [])
                S.op("dve", lambda e: e.tensor_tensor_scan(out=tH[:, hs, ::-1], data0=tA[:, s_, ::-1], data1=tB[:, s_, ::-1],
                                                           initial=init, op0=ALU.mult, op1=ALU.add), r=rr, w=[tHR[hs]])
                return hs

            def tail(n, i, hs):
                zx, zxR, zg, zgR, xa, xaR, xab, xabR, hf, hfR = sets[n % 2]
                sl = slice(i * 512, (i + 1) * 512)
                t_ = ctr["t"] % NR
                ctr["t"] += 1
                S.dma("sp", lambda e: e.dma_start(out=tZ[:, t_, :], in_=z_s[(8 + n) * 128:(9 + n) * 128, sl]),
                      r=[zR[8 + n][i]], w=[tZR[t_]])
                S.op("dve", lambda e: e.tensor_tensor(out=tG[:, t_, :], in0=hf[:, sl], in1=tH[:, hs, :], op=ALU.add),
                     r=[hfR[i], tHR[hs]], w=[tGR[t_]])
                S.op("act", lambda e: e.activation(out=tS[:, t_, :], in_=tZ[:, t_, :], func=AF.Square,
                                                   scale=math.sqrt(0.044715)), r=[tZR[t_]], w=[tSR[t_]])
                S.op("dve", lambda e: e.scalar_tensor_tensor(out=tS[:, t_, :], in0=tS[:, t_, :], scalar=1.0, in1=tZ[:, t_, :],
                                                             op0=ALU.add, op1=ALU.mult), r=[tSR[t_], tZR[t_]], w=[tSR[t_]])
                S.op("act", lambda e: e.activation(out=tS[:, t_, :], in_=tS[:, t_, :], func=AF.Tanh, scale=0.5 * GC),
                     r=[tSR[t_]], w=[tSR[t_]])
                S.op("dve", lambda e: e.scalar_tensor_tensor(out=tG[:, t_, :], in0=tG[:, t_, :], scalar=0.5, in1=tZ[:, t_, :],
                                                             op0=ALU.mult, op1=ALU.mult), r=[tGR[t_], tZR[t_]], w=[tGR[t_]])
                S.op("dve", lambda e: e.scalar_tensor_tensor(out=tY[:, t_, :], in0=tS[:, t_, :], scalar=1.0, in1=tG[:, t_, :],
                                                             op0=ALU.add, op1=ALU.mult), r=[tSR[t_], tGR[t_]], w=[tYR[t_]])
                S.dma("sp", lambda e: e.dma_start(out=ya_s[n * 128:(n + 1) * 128, sl], in_=tY[:, t_, :]),
                      r=[tYR[t_]], w=[yaR[n][i]])

            def chunk(n):
                for i in range(NT):
                    conv(n, i)
                groups = [list(range(a, min(a + GRP, NT))) for a in range(0, NT, GRP)]
                for tiles in groups:
                    slots = gates_group(n, 0, tiles)
                    for i, s_ in zip(tiles, slots):
                        scan_f(n, i, s_)
                prev_h = None
                for tiles in reversed(groups):
                    tiles = tiles[::-1]
                    slots = gates_group(n, 1, tiles)
                    hss = []
                    for i, s_ in zip(tiles, slots):
                        prev_h = scan_b(n, i, s_, prev_h)
                        hss.append((i, prev_h))
                    for (ti, th) in hss:
                        tail(n, ti, th)

            load(0)
            for n in range(8):
                if n + 1 < 8:
                    load(n + 1)
                chunk(n)

        phase_c()

        def phase_e1():
            Wga, WgaR = A.take("Wga", 0, [8, 2048], BF16, nres=8)
            Wlo, WloR = A.take("Wlo", 32768, [8, D], BF16, nres=8)
            Wmo, WmoR = A.take("Wmo", 49152, [8, D], BF16, nres=8)
            Wo, WoR = A.take("Wo", 65536, [8, D], BF16, nres=8)
            o0 = 81920
            NSL = 8
            xt, xtR = A.take("xt", o0, [NSL, D], F32, nres=NSL); o0 += NSL * 4096
            xs, xsR = A.take("xs", o0, [4, D], BF16, nres=4); o0 += 8192
            uT2, uT2R = A.take("uT", o0, [2, 8, 512], BF16, nres=16); o0 += 16384
            yat, yatR = A.take("yat", o0, [2, 8, 512], BF16, nres=2); o0 += 16384
            ott, ottR = A.take("ott", o0, [2, 8, 512], BF16, nres=2); o0 += 16384
            mT, mTR = A.take("mT", o0, [8, 512], BF16, nres=8); o0 += 8192
            sgt, sgtR = A.take("sgt", o0, [4, 512], F32, nres=4); o0 += 8192
            assert o0 <= CB
            win_v = win_d.rearrange("(k p) n -> p k n", p=128)
            wlo_v = wlo_d.rearrange("(k p) n -> p k n", p=128)
            wmo_v = wmo_d.rearrange("(k p) n -> p k n", p=128)
            for blk in range(4):
                lo, hi = blk * 256, blk * 256 + 256
                S.dma("pool", lambda e, lo=lo, hi=hi: e.dma_start(out=Wga[:, :, lo:hi], in_=win_v[:, :, 2752 + lo:2752 + hi]),
                      w=[WgaR[blk]])
                S.dma("pool", lambda e, lo=lo, hi=hi: e.dma_start(out=Wga[:, :, 1024 + lo:1024 + hi],
                                                                 in_=win_v[:, :, 3776 + lo:3776 + hi]), w=[WgaR[4 + blk]])
                S.dma("pool", lambda e, lo=lo, hi=hi: e.dma_start(out=Wlo[:, :, lo:hi], in_=wlo_v[:, :, lo:hi]), w=[WloR[blk]])
                S.dma("pool", lambda e, lo=lo, hi=hi: e.dma_start(out=Wmo[:, :, lo:hi], in_=wmo_v[:, :, lo:hi]), w=[WmoR[blk]])
            load_w(Wo, WoR, wo_d, 0, D, 8)
            ya_v = ya_s.rearrange("(k p) t -> p k t", p=128)
            o_v = o_s.rearrange("(k p) t -> p k t", p=128)
            slot = [0]

            def load_tile(i):
                sls = []
                for s in range(4):
                    sl = slot[0] % NSL
                    slot[0] += 1
                    r0 = i * 512 + s * 128
                    S.dma("sp", lambda e, sl=sl, r0=r0: e.dma_start(out=xt[:, sl, :], in_=h_s[r0:r0 + 128, :]),
                          r=[hR[i * 4 + s]], w=[xtR[sl]])
                    sls.append(sl)
                b = i % 2
                S.dma("sp", lambda e: e.dma_start(out=yat[:, b], in_=ya_v[:, :, i * 512:(i + 1) * 512]),
                      r=[yaR[c][i] for c in range(8)], w=[yatR[b]])
                S.dma("sp", lambda e: e.dma_start(out=ott[:, b], in_=o_v[:, :, i * 512:(i + 1) * 512]),
                      r=[oR[h][i] for h in range(NH)], w=[ottR[b]])
                return sls

            def front_a(i, sls):
                for s in range(4):
                    rms_T(xt, xtR, sls[s], s, 8, xs, xsR)

            def front_b(i):
                transposes(xs, xsR, 8, 8, uT2[:, i % 2], uT2R[(i % 2) * 8:(i % 2) * 8 + 8])

            sc = [0]

            def body(i, sls, part):
                b = i % 2
                uT = uT2[:, i % 2]
                uTR = uT2R[(i % 2) * 8:(i % 2) * 8 + 8]
                if part == 1:
                    body1(i, sls, b, uT, uTR)
                else:
                    body2(i, sls)

            def body1(i, sls, b, uT, uTR):
                for c in range(8):
                    pA, pAR = PS(); pB, pBR = PS(); pYA, pYAR = PS(); pYB, pYBR = PS()
                    for (pp, ppR, W, WR, cc, src, srcR) in (
                            (pA, pAR, Wga, WgaR, c * 128, None, None), (pB, pBR, Wga, WgaR, 1024 + c * 128, None, None),
                            (pYA, pYAR, Wlo, WloR, c * 128, yat, yatR), (pYB, pYBR, Wmo, WmoR, c * 128, ott, ottR)):
                        for k in range(8):
                            if src is None:
                                rhs = uT[:, k, :]; rr = uTR[k]
                            else:
                                rhs = src[:, b, k, :]; rr = srcR[b]
                            S.op("pe", lambda e, k=k, pp=pp, W=W, cc=cc, rhs=rhs: e.matmul(
                                pp[:, :], lhsT=W[:, k, cc:cc + 128], rhs=rhs, start=(k == 0), stop=(k == 7)),
                                 r=[WR[cc // 256], rr], w=[ppR])
                    t0 = sc[0] % 2 * 2
                    sc[0] += 1
                    S.op("act", lambda e, pA=pA, t0=t0: e.activation(out=sgt[:, t0, :], in_=pA[:, :], func=AF.Sigmoid),
                         r=[pAR], w=[sgtR[t0]])
                    S.op("act", lambda e, pB=pB, t0=t0: e.activation(out=sgt[:, t0 + 1, :], in_=pB[:, :], func=AF.Sigmoid),
                         r=[pBR], w=[sgtR[t0 + 1]])
                    S.op("dve", lambda e, pYA=pYA, t0=t0: e.tensor_tensor(out=sgt[:, t0, :], in0=pYA[:, :], in1=sgt[:, t0, :],
                                                                         op=ALU.mult), r=[pYAR, sgtR[t0]], w=[sgtR[t0]])
                    S.op("dve", lambda e, pYB=pYB, t0=t0: e.tensor_tensor(out=sgt[:, t0 + 1, :], in0=pYB[:, :],
                                                                         in1=sgt[:, t0 + 1, :], op=ALU.mult),
                         r=[pYBR, sgtR[t0 + 1]], w=[sgtR[t0 + 1]])
                    S.op("pool", lambda e, c=c, t0=t0: e.tensor_tensor(out=mT[:, c, :], in0=sgt[:, t0, :], in1=sgt[:, t0 + 1, :],
                                                                      op=ALU.add), r=[sgtR[t0], sgtR[t0 + 1]], w=[mTR[c]])

            def body2(i, sls):
                for s in range(4):
                    sl = sls[s]
                    for c2 in range(2):
                        pd, pdR = PS()
                        for k in range(8):
                            S.op("pe", lambda e, k=k, s=s, c2=c2, pd=pd: e.matmul(
                                pd[:, :], lhsT=mT[:, k, s * 128:(s + 1) * 128], rhs=Wo[:, k, c2 * 512:(c2 + 1) * 512],
                                start=(k == 0), stop=(k == 7)), r=[mTR[k], WoR[k]], w=[pdR])
                        S.op("dve", lambda e, sl=sl, c2=c2, pd=pd: e.tensor_tensor(
                            out=xt[:, sl, c2 * 512:(c2 + 1) * 512], in0=pd[:, :], in1=xt[:, sl, c2 * 512:(c2 + 1) * 512],
                            op=ALU.add), r=[pdR, xtR[sl]], w=[xtR[sl]])
                    r0 = i * 512 + s * 128
                    S.dma("sp", lambda e, sl=sl, r0=r0: e.dma_start(out=h_s[r0:r0 + 128, :], in_=xt[:, sl, :]),
                          r=[xtR[sl]], w=[hR[i * 4 + s]])

            sl_cur = load_tile(0)
            front_a(0, sl_cur)
            front_b(0)
            for i in range(NT):
                cur = sl_cur
                if i + 1 < NT:
                    sl_cur = load_tile(i + 1)
                    front_a(i + 1, sl_cur)
                body(i, cur, 1)
                if i + 1 < NT:
                    front_b(i + 1)
                body(i, cur, 2)

        phase_e1()

        ffn_phase(f2g_d, f2u_d, f2d_d, 16, h_s, hR, h_s, hR)

        def phase_e3():
            Wpg, WpgR = A.take("Wpg", 0, [8, D], BF16, nres=8)
            Wpp, WppR = A.take("Wpp", 16384, [2, D], BF16, nres=2)
            bc, (bcR,) = A.take("bc", 20480, [2 * D], F32)
            o0 = 28672
            NSL = 8
            xt, xtR = A.take("xt", o0, [NSL, D], F32, nres=NSL); o0 += NSL * 4096
            xs, xsR = A.take("xs", o0, [4, D], BF16, nres=4); o0 += 8192
            hT, hTR = A.take("hT", o0, [8, 512], BF16, nres=8); o0 += 8192
            gg, ggR = A.take("gg", o0, [4, D], F32, nres=4); o0 += 16384
            pin, pinR = A.take("pin", o0, [2, 4, PLE], F32, nres=2); o0 += 8192
            pbs, pbsR = A.take("pbs", o0, [4, PLE], BF16, nres=4); o0 += 2048
            pT, pTR = A.take("pT", o0, [2, 512], BF16, nres=2); o0 += 2048
            t1, t1R = A.take("t1", o0, [2, D], F32, nres=2); o0 += 8192
            outt, outtR = A.take("outt", o0, [2, D], F32, nres=2); o0 += 8192
            load_w(Wpg, WpgR, wpg_d, 0, D, 8)
            load_w(Wpp, WppR, wpp_d, 0, D, 2)
            S.dma("sp", lambda e: e.dma_start(out=bc, in_=bc_d), w=[bcR])
            p_v = p_d.rearrange("(n s p) c -> n p s c", s=4, p=128)
            slot = [0]

            def load_tile(i):
                sls = []
                for s in range(4):
                    sl = slot[0] % NSL
                    slot[0] += 1
                    r0 = i * 512 + s * 128
                    S.dma("sp", lambda e, sl=sl, r0=r0: e.dma_start(out=xt[:, sl, :], in_=h_s[r0:r0 + 128, :]),
                          r=[hR[i * 4 + s]], w=[xtR[sl]])
                    sls.append(sl)
                b = i % 2
                S.dma("sp", lambda e: e.dma_start(out=pin[:, b], in_=p_v[i]), w=[pinR[b]])
                return sls

            def front(i, sls):
                b = i % 2
                for s in range(4):
                    rms_T(xt, xtR, sls[s], s, 24, xs, xsR, mode="pow")
                transposes(xs, xsR, 8, 24, hT, hTR)
                for s in range(4):
                    S.op("act", lambda e, s=s: e.activation(out=pbs[:, s, :], in_=pin[:, b, s, :], func=AF.Copy),
                         r=[pinR[b]], w=[pbsR[s]])
                for k in range(2):
                    ps, pr = PS()
                    for s in range(4):
                        S.op("pe", lambda e, s=s, k=k, ps=ps: e.matmul(
                            ps[:, s * 128:(s + 1) * 128], lhsT=pbs[:, s, k * 128:(k + 1) * 128], rhs=ident,
                            start=True, stop=True), r=[pbsR[s], identR], w=[pr])
                    S.op("dve", lambda e, k=k, ps=ps: e.tensor_copy(out=pT[:, k, :], in_=ps[:, :]), r=[pr], w=[pTR[k]])

            oc = [0]

            pend = []

            def body(i, sls):
                for s in range(4):
                    ctx = body_a(i, sls, s)
                    if pend:
                        body_b(*pend.pop(0))
                    pend.append(ctx)

            def body_a(i, sls, s):
                if True:
                    sl = sls[s]
                    pps = []
                    for c2 in range(2):
                        pg, pgR = PS()
                        for k in range(8):
                            S.op("pe", lambda e, k=k, c2=c2, pg=pg: e.matmul(
                                pg[:, :], lhsT=hT[:, k, s * 128:(s + 1) * 128], rhs=Wpg[:, k, c2 * 512:(c2 + 1) * 512],
                                start=(k == 0), stop=(k == 7)), r=[hTR[k], WpgR[k]], w=[pgR])
                        S.op("act", lambda e, c2=c2, pg=pg: e.activation(out=gg[:, s, c2 * 512:(c2 + 1) * 512], in_=pg[:, :],
                                                                       func=AF.Sigmoid), r=[pgR], w=[ggR[s]])
                        pp, ppR = PS()
                        for k in range(2):
                            S.op("pe", lambda e, k=k, c2=c2, pp=pp: e.matmul(
                                pp[:, :], lhsT=pT[:, k, s * 128:(s + 1) * 128], rhs=Wpp[:, k, c2 * 512:(c2 + 1) * 512],
                                start=(k == 0), stop=(k == 1)), r=[pTR[k], WppR[k]], w=[ppR])
                        pps.append((pp, ppR))
                    tb = oc[0] % 2
                    oc[0] += 1
                    c_a = stat[:, 48 + s:49 + s]; c_b = stat[:, 52 + s:53 + s]; c_r = stat[:, 56 + s:57 + s]
                    for c2 in range(2):
                        pp, ppR = pps[c2]
                        S.op("act", lambda e, pp=pp, c2=c2, cc=(c_a if c2 == 0 else c_b): e.activation(
                            out=t1[:, tb, c2 * 512:(c2 + 1) * 512], in_=pp[:, :], func=AF.Square, accum_out=cc),
                             r=[ppR], w=[t1R[tb], statR[s]])
                    S.op("dve", lambda e: e.tensor_tensor(out=c_a, in0=c_a, in1=c_b, op=ALU.add), r=[statR[s]], w=[statR[s]])
                    rstd_op(c_a, c_b, c_r, D, statR[s], "pow")
                    return (i, s, sl, pps, tb, c_r)

            def body_b(i, s, sl, pps, tb, c_r):
                if True:
                    for c2 in range(2):
                        pp, ppR = pps[c2]
                        S.op("dve", lambda e, pp=pp, c2=c2: e.scalar_tensor_tensor(
                            out=t1[:, tb, c2 * 512:(c2 + 1) * 512], in0=pp[:, :], scalar=c_r,
                            in1=bc[:, c2 * 512:(c2 + 1) * 512], op0=ALU.mult, op1=ALU.mult),
                             r=[ppR, statR[s], bcR], w=[t1R[tb]])
                    S.op("dve", lambda e: e.tensor_tensor(out=t1[:, tb, :], in0=t1[:, tb, :], in1=gg[:, s, :], op=ALU.mult),
                         r=[t1R[tb], ggR[s]], w=[t1R[tb]])
                    S.op("dve", lambda e: e.tensor_tensor(out=xt[:, sl, :], in0=xt[:, sl, :], in1=t1[:, tb, :], op=ALU.add),
                         r=[t1R[tb], xtR[sl]], w=[xtR[sl]])
                    f_s = stat[:, 4 + s:5 + s]; f_d = stat[:, 12 + s:13 + s]; f_r = stat[:, 20 + s:21 + s]
                    S.op("act", lambda e: e.activation(out=outt[:, tb, :], in_=xt[:, sl, :], func=AF.Square, accum_out=f_s),
                         r=[xtR[sl]], w=[outtR[tb], statR[4 + s]])
                    rstd_op(f_s, f_d, f_r, D, statR[4 + s], "pow")
                    S.op("dve", lambda e: e.scalar_tensor_tensor(out=outt[:, tb, :], in0=xt[:, sl, :], scalar=f_r,
                                                                 in1=bc[:, D:2 * D], op0=ALU.mult, op1=ALU.mult),
                         r=[xtR[sl], statR[4 + s], bcR], w=[outtR[tb]])
                    r0 = i * 512 + s * 128
                    S.dma("sp", lambda e, r0=r0: e.dma_start(out=out_d[r0:r0 + 128, :], in_=outt[:, tb, :]),
                          r=[outtR[tb]], w=[])

            sl_cur = load_tile(0)
            front(0, sl_cur)
            for i in range(NT):
                cur = sl_cur
                while pend:
                    body_b(*pend.pop(0))
                if i + 1 < NT:
                    sl_cur = load_tile(i + 1)
                body(i, cur)
                while pend:
                    body_b(*pend.pop(0))
                if i + 1 < NT:
                    front(i + 1, sl_cur)
            while pend:
                body_b(*pend.pop(0))

        phase_e3()

        S.finish()
    return nc


def _cols(inp):
    c = np.zeros((128, NCOL), np.float32)

    def chunks(v):
        v = np.asarray(v, np.float32).reshape(-1, 128)
        return v.T

    c[:, 0:8] = chunks(inp["ffn1_norm"][0])
    c[:, 8:16] = chunks(inp["mix_norm"][0])
    c[:, 16:24] = chunks(inp["ffn2_norm"][0])
    c[:, 24:32] = chunks(inp["ple_norm"][0])
    c[:, 32:35] = chunks(inp["q_norm"][0])
    c[:, 35:37] = chunks(inp["kv_norm"][0])
    cw = np.asarray(inp["conv_w"][0], np.float32)
    c[:, 37:69] = cw.reshape(4, 8, 128).transpose(2, 1, 0).reshape(128, 32)
    c[:, 69:77] = chunks(inp["conv_b"][0])
    c[:, 77:93] = chunks(np.asarray(inp["lru_b_r"][0]).reshape(-1))
    c[:, 93:109] = chunks(np.asarray(inp["lru_b_i"][0]).reshape(-1))
    c[:, 109:125] = chunks(np.asarray(inp["lru_lambda"][0]).reshape(-1))
    invf64 = 10000.0 ** (-(np.arange(32, dtype=np.float64)) / 32.0)
    c64 = np.concatenate([invf64, invf64]) / TWO_PI
    c_hi = c64.astype(np.float32)
    c[0:64, 125] = c_hi
    c[0:64, 134] = (c64 - c_hi.astype(np.float64)).astype(np.float32)
    c[:, 127] = EPS
    c[0:32, 128] = -TWO_PI
    c[32:64, 128] = TWO_PI
    c[0:32, 129] = math.pi
    c[32:64, 129] = -math.pi
    c[:, 130] = -math.pi
    c[:, 131] = 1.0
    c[:, 132] = 0.25
    c[:, 133] = -0.5
    return c


def make_in_maps(inp, T, ncores):
    shared = {
        "cols": _cols(inp),
        "ident": np.eye(128, dtype=np.float32),
        "bc": np.ascontiguousarray(np.broadcast_to(
            np.concatenate([np.asarray(inp["ple_proj_norm"][0], np.float32),
                            np.asarray(inp["final_norm"], np.float32)])[None, :], (128, 2 * D))),
    }
    for k in ("ffn1_w_gate", "ffn1_w_up", "ffn1_w_down", "ffn2_w_gate", "ffn2_w_up", "ffn2_w_down", "w_in",
              "w_lru_out", "w_uq", "w_ukv", "w_mla_out", "w_o", "ple_w_gate", "ple_w_proj"):
        shared[k] = np.ascontiguousarray(np.asarray(inp[k], np.float32)[0])
    shared["lru_w_r"] = np.ascontiguousarray(np.asarray(inp["lru_w_r"], np.float32)[0].reshape(16, 128, 128))
    shared["lru_w_i"] = np.ascontiguousarray(np.asarray(inp["lru_w_i"], np.float32)[0].reshape(16, 128, 128))
    maps = []
    for b in range(ncores):
        m = dict(shared)
        m["x"] = np.ascontiguousarray(np.asarray(inp["x"], np.float32)[b, :T])
        m["p"] = np.ascontiguousarray(np.asarray(inp["p"], np.float32)[0, b, :T])
        m["pos"] = np.ascontiguousarray(np.broadcast_to(np.asarray(inp["positions"], np.int32)[b, :T].reshape(1, T), (64, T)))
        maps.append(m)
    return maps


_NC_CACHE = {}


def kernel(**inputs):
    T = 4096
    n = 8
    if T not in _NC_CACHE:
        _NC_CACHE[T] = build(T)
    nc = _NC_CACHE[T]
    maps = make_in_maps(inputs, T, n)
    res = run_bass_kernel_spmd(nc, maps, core_ids=list(range(n)))
    return np.stack([np.asarray(r["out"], np.float32) for r in res.results], axis=0)
```
